# Optimizing a Trainium2 kernel written in Bass

```python
import jax, jax.numpy as jnp
from jax import lax
import numpy as np

D_MODEL = 2048
BATCH = 2
SEQ = 4096
DEPTH = 4
DEC_BATCH = 8
DEC_SEQ = 4096
PAST_LEN = 128

GRID_W = 64
N_MEM = 256
MIX_W = 2 * D_MODEL
MEM_HEADS = 4
MEM_HD = 256
MEM_W = MEM_HEADS * MEM_HD
TOK_W = MIX_W - MEM_W
HG_DK = 128
HG_DV = 128
HG_HEADS = TOK_W // HG_DV
NA_HD = 128
NA_HEADS = TOK_W // NA_HD
NA_KH = 8
NA_KW = 16
CHUNK = 32
N_A_LAYERS = (DEPTH + 1) // 2
N_B_LAYERS = DEPTH // 2
P_A = 4 * TOK_W + MEM_W + MIX_W
P_B = 3 * TOK_W + MEM_W + MIX_W
EPS = 1e-6

kernel_name = "hybrid_hgrn2_natten_memory_encoder"


def rms_norm(x, g):
    xf = x.astype(jnp.float32)
    y = xf * lax.rsqrt(jnp.mean(xf * xf, axis=-1, keepdims=True) + EPS)
    return (y * g.astype(jnp.float32)).astype(x.dtype)


def _to_chunks(a):
    n, b, t, h, d = a.shape
    return a.reshape(n, b, t // CHUNK, CHUNK, h, d).transpose(2, 0, 1, 4, 3, 5)


def gla_chunk_scan(q, k, v, log_f):
    n, b, t, h, dk = q.shape
    dv = v.shape[-1]
    tri = jnp.tril(jnp.ones((CHUNK, CHUNK), dtype=bool))

    def step(S, inp):
        qc, kc, vc, gc = inp
        cb = jnp.cumsum(gc, axis=-2)
        cb_last = cb[..., -1, :]
        diff = cb[..., :, None, :] - cb[..., None, :, :]
        decay = jnp.exp(jnp.where(tri[:, :, None], diff, -jnp.inf))
        A = jnp.einsum('...td,...sd,...tsd->...ts', qc, kc, decay)
        o = jnp.einsum('...ts,...sv->...tv', A, vc) + jnp.einsum('...td,...dv->...tv', qc * jnp.exp(cb), S)
        S_new = S * jnp.exp(cb_last)[..., :, None] + jnp.einsum(
            '...sd,...sv->...dv', kc * jnp.exp(cb_last[..., None, :] - cb), vc)
        return S_new, o

    S0 = jnp.zeros((n, b, h, dk, dv), jnp.float32)
    _, o = lax.scan(step, S0, (_to_chunks(q), _to_chunks(k), _to_chunks(v), _to_chunks(log_f)))
    return o.transpose(1, 2, 0, 4, 3, 5).reshape(n, b, t, h, dv)


def hgrn2_bidir(q, f_fwd, f_bwd, iv, lb, onorm_g):
    B, T, _ = q.shape
    f32 = jnp.float32
    shp = (B, T, HG_HEADS, HG_DK)
    f_pre = jnp.stack([f_fwd.reshape(shp), f_bwd.reshape(shp)]).astype(f32)
    lbr = lb.reshape(2, 1, 1, HG_HEADS, HG_DK)
    log_f = jnp.logaddexp(jnp.log(lbr), jnp.log1p(-lbr) + jax.nn.log_sigmoid(f_pre))
    k = (1.0 - lbr) * jax.nn.sigmoid(-f_pre)
    qs = jax.nn.silu(q.reshape(shp).astype(f32)) * (HG_DK ** -0.5)
    vv = iv.reshape(B, T, HG_HEADS, HG_DV).astype(f32)
    qd = jnp.stack([qs, qs[:, ::-1]])
    vd = jnp.stack([vv, vv[:, ::-1]])
    kd = jnp.stack([k[0], k[1][:, ::-1]])
    gd = jnp.stack([log_f[0], log_f[1][:, ::-1]])
    o = gla_chunk_scan(qd, kd, vd, gd)
    o = o[0] + o[1][:, ::-1]
    o = o * lax.rsqrt(jnp.mean(o * o, axis=-1, keepdims=True) + EPS)
    return o.reshape(B, T, TOK_W) * onorm_g.astype(f32)


def neighborhood_attention(q, k, v, rpb):
    B, T, _ = q.shape
    rows = T // GRID_W
    kh = min(NA_KH, rows)
    kw = NA_KW
    qg = q.reshape(B, rows, GRID_W, NA_HEADS, NA_HD) * (NA_HD ** -0.5)
    kg = k.reshape(B, rows, GRID_W, NA_HEADS, NA_HD)
    vg = v.reshape(B, rows, GRID_W, NA_HEADS, NA_HD)
    col = jnp.arange(GRID_W)
    cs = jnp.clip(col - kw // 2, 0, GRID_W - kw)
    col_mask = (col[None, :] >= cs[:, None]) & (col[None, :] < cs[:, None] + kw)
    dc_idx = jnp.clip(col[None, :] - col[:, None] + (kw - 1), 0, 2 * kw - 2)
    rpb_c = rpb.astype(jnp.float32)[:, :, dc_idx]

    def row_block(r):
        rs = jnp.clip(r - kh // 2, 0, rows - kh)
        kb = lax.dynamic_slice_in_dim(kg, rs, kh, axis=1)
        vb = lax.dynamic_slice_in_dim(vg, rs, kh, axis=1)
        qr = lax.dynamic_index_in_dim(qg, r, axis=1, keepdims=False)
        dr_idx = rs + jnp.arange(kh) - r + (NA_KH - 1)
        bias = jnp.take(rpb_c, dr_idx, axis=1).transpose(0, 2, 1, 3)
        s = jnp.einsum('bqhd,bjkhd->bhqjk', qr, kb).astype(jnp.float32) + bias[None]
        s = jnp.where(col_mask[:, None, :], s, -jnp.inf)
        p = jax.nn.softmax(s.reshape(B, NA_HEADS, GRID_W, kh * GRID_W), axis=-1)
        p = p.reshape(B, NA_HEADS, GRID_W, kh, GRID_W).astype(v.dtype)
        return jnp.einsum('bhqjk,bjkhd->bqhd', p, vb)

    out = lax.map(row_block, jnp.arange(rows))
    return out.transpose(1, 0, 2, 3, 4).reshape(B, T, TOK_W)


def memory_attention(qm, km, vm):
    B, T, _ = qm.shape
    qh = qm.reshape(B, T, MEM_HEADS, MEM_HD)
    kh = km.reshape(B, -1, MEM_HEADS, MEM_HD)
    vh = vm.reshape(B, -1, MEM_HEADS, MEM_HD)
    s = jnp.einsum('bthd,bmhd->bhtm', qh, kh).astype(jnp.float32) * (MEM_HD ** -0.5)
    p = jax.nn.softmax(s, axis=-1).astype(vm.dtype)
    return jnp.einsum('bhtm,bmhd->bthd', p, vh).reshape(B, T, MEM_W)


def trunk(x, mem, w_in_a, w_in_b, w_mem_kv, w_out, norm_g, mem_norm_g, lower_bounds, hg_onorm_g, na_rpb, final_g):
    split_a = [int(c) for c in np.cumsum([TOK_W] * 4 + [MEM_W])]
    split_b = [int(c) for c in np.cumsum([TOK_W] * 3 + [MEM_W])]
    for i in range(DEPTH):
        h = rms_norm(x, norm_g[i])
        km, vm = jnp.split(rms_norm(mem, mem_norm_g[i]) @ w_mem_kv[i], 2, axis=-1)
        if i % 2 == 0:
            a = i // 2
            q, f_f, f_b, iv, qm, gate = jnp.split(h @ w_in_a[a], split_a, axis=-1)
            o_mix = hgrn2_bidir(q, f_f, f_b, iv, lower_bounds[a], hg_onorm_g[a]).astype(x.dtype)
        else:
            bi = i // 2
            q, k, v, qm, gate = jnp.split(h @ w_in_b[bi], split_b, axis=-1)
            o_mix = neighborhood_attention(q, k, v, na_rpb[bi])
        o_mem = memory_attention(qm, km, vm)
        o = jnp.concatenate([o_mix, o_mem], axis=-1) * jax.nn.silu(gate)
        x = x + o @ w_out[i]
    return rms_norm(x, final_g)


def setup_inputs(seed: int = 0) -> dict:
    key = jax.random.key(seed)
    ks = jax.random.split(key, 16)
    f32 = jnp.float32
    nrm = jax.random.normal
    return {
        "x_prompt": nrm(ks[0], (BATCH, SEQ, D_MODEL), f32),
        "x_sample": nrm(ks[1], (DEC_BATCH, DEC_SEQ, D_MODEL), f32),
        "mem_prompt": nrm(ks[2], (BATCH, N_MEM, D_MODEL), f32),
        "mem_sample": nrm(ks[3], (DEC_BATCH, N_MEM, D_MODEL), f32),
        "w_in_a": nrm(ks[4], (N_A_LAYERS, D_MODEL, P_A), f32) * (D_MODEL ** -0.5),
        "w_in_b": nrm(ks[5], (N_B_LAYERS, D_MODEL, P_B), f32) * (D_MODEL ** -0.5),
        "w_mem_kv": nrm(ks[6], (DEPTH, D_MODEL, 2 * MEM_W), f32) * (D_MODEL ** -0.5),
        "w_out": nrm(ks[7], (DEPTH, MIX_W, D_MODEL), f32) * (MIX_W ** -0.5),
        "norm_g": 1.0 + 0.02 * nrm(ks[8], (DEPTH, D_MODEL), f32),
        "mem_norm_g": 1.0 + 0.02 * nrm(ks[9], (DEPTH, D_MODEL), f32),
        "hg_lb_logits": 0.5 * nrm(ks[10], (N_A_LAYERS, 2, TOK_W), f32),
        "hg_onorm_g": 1.0 + 0.02 * nrm(ks[11], (N_A_LAYERS, TOK_W), f32),
        "na_rpb": 0.1 * nrm(ks[12], (N_B_LAYERS, NA_HEADS, 2 * NA_KH - 1, 2 * NA_KW - 1), f32),
        "final_g": 1.0 + 0.02 * nrm(ks[13], (D_MODEL,), f32),
    }


def reference(x_prompt, x_sample, mem_prompt, mem_sample, w_in_a, w_in_b, w_mem_kv, w_out,
              norm_g, mem_norm_g, hg_lb_logits, hg_onorm_g, na_rpb, final_g):
    p = jax.nn.softmax(hg_lb_logits.astype(jnp.float32), axis=0)
    lower_bounds = jnp.cumsum(p, axis=0) - p[0:1]
    y_prompt = trunk(x_prompt, mem_prompt, w_in_a, w_in_b, w_mem_kv, w_out, norm_g, mem_norm_g,
                     lower_bounds, hg_onorm_g, na_rpb, final_g)
    y_sample = trunk(x_sample, mem_sample, w_in_a, w_in_b, w_mem_kv, w_out, norm_g, mem_norm_g,
                     lower_bounds, hg_onorm_g, na_rpb, final_g)
    return (y_prompt, y_sample)
```

```python
import numpy as np
import ml_dtypes
from contextlib import ExitStack
import concourse.bass as bass
import concourse.mybir as mybir
from concourse.bass_utils import run_bass_kernel_spmd

F32 = mybir.dt.float32
BF16 = mybir.dt.bfloat16
AF = mybir.ActivationFunctionType
ALU = mybir.AluOpType
AX = mybir.AxisListType

D = 2048
KC = D // 128
TOK_W = 3072
NH = 24
MEM_W = 1024
MIX_W = 4096
P_A = 4 * TOK_W + MEM_W + MIX_W
P_B = 3 * TOK_W + MEM_W + MIX_W
NMEM = 256
EPS = 1e-6
GW = 64
CH = 32


class SemObj:
    __slots__ = ("sem", "step", "count")

    def __init__(self, sem, step):
        self.sem = sem
        self.step = step
        self.count = 0


class Buf:
    __slots__ = ("lw", "rd", "dsem")

    def __init__(self):
        self.lw = {}
        self.rd = {}
        self.dsem = None


class Tl:
    __slots__ = ("t", "b")

    def __init__(self, t):
        self.t = t
        self.b = Buf()

    def __getitem__(self, idx):
        return self.t[idx]


class Ring:
    def __init__(self, tiles):
        self.tiles = tiles
        self.i = 0

    def next(self):
        t = self.tiles[self.i % len(self.tiles)]
        self.i += 1
        return t


class KB:
    ENG = ("sp", "pe", "act", "dve", "pool")

    def __init__(self, nc, es, n_dsem=96):
        self.nc = nc
        self.esem = {n: SemObj(es.enter_context(nc.semaphore("s_" + n)), 1)
                     for n in ("pe", "act", "dve", "pool")}
        self.dsems = [SemObj(es.enter_context(nc.semaphore("d%d" % i)), 16) for i in range(n_dsem)]
        self.dfree = list(self.dsems)
        self.ops = {n: [] for n in self.ENG}
        self.waited = {n: {} for n in self.ENG}
        self.uid = 0
        self.nops = 0

    def _deps(self, en, R, W):
        need = {}
        for b in R:
            for so, v in b.lw.items():
                if need.get(so, 0) < v:
                    need[so] = v
        for b in W:
            for so, v in b.lw.items():
                if need.get(so, 0) < v:
                    need[so] = v
            for so, v in b.rd.items():
                if need.get(so, 0) < v:
                    need[so] = v
        waits = []
        wd = self.waited[en]
        pe_so = self.esem["pe"]
        for so, v in need.items():
            if en == "pe" and so is pe_so:
                continue
            if wd.get(so, 0) < v:
                waits.append((so.sem, v))
                wd[so] = v
        return waits

    def op(self, en, fn, R=(), W=()):
        R = [x.b if isinstance(x, Tl) else x for x in R]
        W = [x.b if isinstance(x, Tl) else x for x in W]
        waits = self._deps(en, R, W)
        so = self.esem[en]
        so.count += 1
        n = so.count
        self.ops[en].append((waits, fn, so.sem, 1))
        for b in R:
            b.rd[so] = n
        for b in W:
            b.lw[so] = n
        self.nops += 1

    def dma(self, out, in_, sb, load, **kw):
        b = sb.b if isinstance(sb, Tl) else sb
        if b.dsem is None:
            b.dsem = self.dfree.pop()
        waits = self._deps("sp", [] if load else [b], [b] if load else [])
        so = b.dsem
        so.count += 16
        self.ops["sp"].append((waits, lambda e: e.dma_start(out=out, in_=in_, **kw), so.sem, 16))
        if load:
            b.lw[so] = so.count
        else:
            b.rd[so] = so.count
        self.nops += 1

    def barrier(self):
        sos = list(self.esem.values()) + [d for d in self.dsems if d.count > 0]
        for en in self.ENG:
            wd = self.waited[en]
            waits = []
            for so in sos:
                if en == "pe" and so is self.esem["pe"]:
                    continue
                if wd.get(so, 0) < so.count:
                    waits.append((so.sem, so.count))
                    wd[so] = so.count
            if waits:
                self.ops[en].append((waits, None, None, 0))

    def flush(self):
        nc = self.nc
        with nc.Block() as block:
            decos = {"sp": block.sync, "pe": block.tensor, "act": block.scalar,
                     "dve": block.vector, "pool": block.gpsimd}
            for en in self.ENG:
                ops = self.ops[en]
                if not ops:
                    continue

                def body(e, ops=ops):
                    for waits, fn, sem, inc in ops:
                        for s, v in waits:
                            e.wait_ge(s, v)
                        if fn is not None:
                            fn(e).then_inc(sem, inc)
                decos[en](body)
                self.ops[en] = []


class Phase:
    def __init__(self, k, name):
        self.k = k
        k.uid += 1
        self.name = "%s%d" % (name, k.uid)
        self.es = ExitStack()
        self.tiles = []
        self.n = 0

    def _mk(self, fn, shape, dt):
        self.n += 1
        t = self.es.enter_context(fn("%s_%d" % (self.name, self.n), list(shape), dt))
        tl = Tl(t)
        self.tiles.append(tl)
        return tl

    def sb(self, shape, dt):
        return self._mk(self.k.nc.sbuf_tensor, shape, dt)

    def ps(self, shape, dt):
        return self._mk(self.k.nc.psum_tensor, shape, dt)

    def ring(self, n, shape, dt, psum=False):
        return Ring([(self.ps if psum else self.sb)(shape, dt) for _ in range(n)])

    def end(self):
        k = self.k
        k.barrier()
        k.flush()
        for tl in self.tiles:
            if tl.b.dsem is not None:
                k.dfree.append(tl.b.dsem)
                tl.b.dsem = None
        self.es.close()


def act(k, out, in_, func, R, W, bias=None, scale=1.0, accum=None):
    kw = {}
    if bias is not None:
        kw["bias"] = bias
    if accum is not None:
        kw["accum_out"] = accum
    k.op("act", lambda e: e.activation(out=out, in_=in_, func=func, scale=scale, **kw), R, W)


def tt(k, en, out, in0, in1, op, R, W):
    k.op(en, lambda e: e.tensor_tensor(out=out, in0=in0, in1=in1, op=op), R, W)


def ts(k, en, out, in0, s1, s2, op0, op1, R, W):
    if s2 is None:
        k.op(en, lambda e: e.tensor_scalar(out=out, in0=in0, scalar1=s1, scalar2=None, op0=op0), R, W)
    else:
        k.op(en, lambda e: e.tensor_scalar(out=out, in0=in0, scalar1=s1, scalar2=s2, op0=op0, op1=op1), R, W)


def stt(k, out, in0, scalar, in1, op0, op1, R, W):
    k.op("dve", lambda e: e.scalar_tensor_tensor(out=out, in0=in0, scalar=scalar, in1=in1, op0=op0, op1=op1), R, W)


def copy(k, en, out, in_, R, W):
    if en == "act":
        k.op("act", lambda e: e.copy(out=out, in_=in_), R, W)
    else:
        k.op(en, lambda e: e.tensor_copy(out=out, in_=in_), R, W)


def mm(k, out, lhsT, rhs, start, stop, R, W, **kw):
    k.op("pe", lambda e: e.matmul(out, lhsT, rhs, start=start, stop=stop, **kw), R, W)


def tr(k, out, in_, ident, R, W):
    k.op("pe", lambda e: e.transpose(out, in_, ident), R, W)


class Cfg:
    pass


def phase_conv(k, src2d, dst3d, K, P):
    ph = Phase(k, "cv")
    CW = 2048
    fr = ph.ring(3, [128, CW], F32)
    br = ph.ring(3, [128, CW], BF16)
    items = [(kc, c0) for kc in range(K // 128) for c0 in range(0, P, CW)]
    engs = ("dve", "act", "pool", "dve", "act")
    ftiles = {}

    def load(i):
        kc, c0 = items[i]
        n = min(CW, P - c0)
        f = fr.next()
        ftiles[i] = f
        k.dma(f[:, :n], src2d[kc * 128:(kc + 1) * 128, c0:c0 + n], f, True)

    load(0)
    for i, (kc, c0) in enumerate(items):
        n = min(CW, P - c0)
        if i + 1 < len(items):
            load(i + 1)
        f = ftiles.pop(i)
        b = br.next()
        copy(k, engs[i % len(engs)], b[:, :n], f[:, :n], [f], [b])
        k.dma(dst3d[:, kc, c0:c0 + n], b[:, :n], b, False)
    ph.end()


def phase_prep(k, c, xsrc, gvec, T, hT=None, yout=None, hTr=None):
    ph = Phase(k, "pp")
    gb = ph.sb([128, D], F32)
    k.dma(gb[:, :], gvec.partition_broadcast(128), gb, True)
    ident = ph.sb([128, 128], BF16)
    k.dma(ident[:, :], c.ident[:, :], ident, True)
    xr = ph.ring(3, [128, D], F32)
    ssr = ph.ring(3, [128, 4], F32)
    epsb = ph.sb([128, 1], F32)
    k.op("pool", lambda e: e.memset(epsb[:, :], EPS), [], [epsb])
    if hT is not None:
        hbr = ph.ring(2, [128, D], BF16)
        hts = ph.ring(2, [128, KC, 512], BF16)
        tpr = ph.ring(4, [128, 4, 128], BF16, psum=True)
        if hTr is not None:
            jrev = ph.sb([128, 128], BF16)
            k.dma(jrev[:, :], c.jrev[:, :], jrev, True)
            htrs = ph.ring(2, [128, KC, 512], BF16)
            rpr = ph.ring(3, [128, 4, 128], F32, psum=True)
    else:
        junk = ph.sb([128, D], BF16)
        yr = ph.ring(2, [128, D], F32)
    nt = T // 128
    xt = {}

    def load(i):
        x = xr.next()
        xt[i] = x
        k.dma(x[:, :], xsrc[i * 128:(i + 1) * 128, :], x, True)

    load(0)
    ev = 0
    for i in range(nt):
        if i + 1 < nt:
            load(i + 1)
        x = xt.pop(i)
        ss = ssr.next()
        if hT is not None:
            hb = hbr.next()
            sq = hb
        else:
            sq = junk
        act(k, sq[:, :], x[:, :], AF.Square, [x], [sq, ss], accum=ss[:, 0:1])
        act(k, ss[:, 1:2], ss[:, 0:1], AF.Ln, [ss, epsb], [ss], bias=epsb[:, 0:1], scale=1.0 / D)
        act(k, ss[:, 2:3], ss[:, 1:2], AF.Exp, [ss], [ss], scale=-0.5)
        if hT is not None:
            stt(k, hb[:, :], x[:, :], ss[:, 2:3], gb[:, :], ALU.mult, ALU.mult, [x, ss, gb], [hb])
            j = i % 4
            if j == 0:
                ht = hts.next()
            for q in range(4):
                tp = tpr.next()
                for cc in range(4):
                    tr(k, tp[:, cc, :], hb[:, (4 * q + cc) * 128:(4 * q + cc + 1) * 128], ident[:, :],
                       [hb, ident], [tp])
                copy(k, "act" if ev % 2 == 0 else "dve", ht[:, 4 * q:4 * q + 4, j * 128:(j + 1) * 128],
                     tp[:, :, :], [tp], [ht])
                ev += 1
            if hTr is not None:
                if j == 0:
                    htr = htrs.next()
                for q in range(4):
                    rp = rpr.next()
                    for cc in range(4):
                        mm(k, rp[:, cc, :], hb[:, (4 * q + cc) * 128:(4 * q + cc + 1) * 128], jrev[:, :], True, True,
                           [hb, jrev], [rp])
                    copy(k, "act" if ev % 2 == 0 else "dve", htr[:, 4 * q:4 * q + 4, (3 - j) * 128:(4 - j) * 128],
                         rp[:, :, :], [rp], [htr])
                    ev += 1
            if j == 3:
                g4 = i // 4
                k.dma(hT[:, :, g4 * 512:(g4 + 1) * 512], ht[:, :, :], ht, False)
                if hTr is not None:
                    gr = nt // 4 - 1 - g4
                    k.dma(hTr[:, :, gr * 512:(gr + 1) * 512], htr[:, :, :], htr, False)
        else:
            y = yr.next()
            stt(k, y[:, :], x[:, :], ss[:, 2:3], gb[:, :], ALU.mult, ALU.mult, [x, ss, gb], [y])
            k.dma(yout[i * 128:(i + 1) * 128, :], y[:, :], y, False)
    ph.end()


def phase_proj(k, c, hT, wbf, jobs, T):
    ph = Phase(k, "pj")
    TT = 1024
    hts = ph.sb([128, KC, TT], BF16)
    wr = ph.ring(2, [128, KC, 512], BF16)
    pr = ph.ring(6, [128, 512], F32, psum=True)
    ofr = ph.ring(3, [128, 512], F32)
    obr = ph.ring(4, [128, 512], BF16)
    items = []
    for (mode, c0, ncols, dst, kind) in jobs:
        for blk in range(ncols // 512):
            items.append((mode, c0 + blk * 512, blk, dst, kind))
    ev = [0]

    def evac(ps, kind):
        if kind == "f32":
            o = ofr.next()
        else:
            o = obr.next()
        if kind == "silu":
            act(k, o[:, :], ps[:, :], AF.Silu, [ps], [o])
        else:
            copy(k, "act" if ev[0] % 2 == 0 else "dve", o[:, :], ps[:, :], [ps], [o])
            ev[0] += 1
        return o

    for st in range(T // TT):
        k.dma(hts[:, :, :], hT[:, :, st * TT:(st + 1) * TT], hts, True)
        wt = {}

        def loadw(i):
            w = wr.next()
            wt[i] = w
            k.dma(w[:, :, :], wbf[:, :, items[i][1]:items[i][1] + 512], w, True)

        loadw(0)
        for i, (mode, cc, blk, dst, kind) in enumerate(items):
            if i + 1 < len(items):
                loadw(i + 1)
            w = wt.pop(i)
            if mode == "FM":
                for j in range(4):
                    for tc in range(TT // 512):
                        ps = pr.next()
                        for kc in range(KC):
                            mm(k, ps[:, :], w[:, kc, j * 128:(j + 1) * 128], hts[:, kc, tc * 512:(tc + 1) * 512],
                               kc == 0, kc == KC - 1, [w, hts], [ps])
                        o = evac(ps, kind)
                        t0 = st * TT + tc * 512
                        k.dma(dst[blk * 4 + j, :, t0:t0 + 512], o[:, :], o, False)
            else:
                for sub in range(TT // 128):
                    ps = pr.next()
                    for kc in range(KC):
                        mm(k, ps[:, :], hts[:, kc, sub * 128:(sub + 1) * 128], w[:, kc, :],
                           kc == 0, kc == KC - 1, [w, hts], [ps])
                    o = evac(ps, kind)
                    t0 = st * TT + sub * 128
                    k.dma(dst[t0:t0 + 128, blk * 512:(blk + 1) * 512], o[:, :], o, False)
    ph.end()


def phase_memkv(k, c, memsrc, gvec, wkv, kmT, vmx):
    ph = Phase(k, "mk")
    gb = ph.sb([128, D], F32)
    k.dma(gb[:, :], gvec.partition_broadcast(128), gb, True)
    ident = ph.sb([128, 128], BF16)
    k.dma(ident[:, :], c.ident[:, :], ident, True)
    epsb = ph.sb([128, 1], F32)
    k.op("pool", lambda e: e.memset(epsb[:, :], EPS), [], [epsb])
    w = ph.sb([128, KC, 2048], BF16)
    k.dma(w[:, :, :], wkv[:, :, :], w, True)
    hmT = ph.sb([128, KC, NMEM], BF16)
    xr = ph.ring(2, [128, D], F32)
    hbr = ph.ring(2, [128, D], BF16)
    ssr = ph.ring(2, [128, 4], F32)
    tpr = ph.ring(2, [128, 4, 128], BF16, psum=True)
    pr = ph.ring(2, [128, 512], F32, psum=True)
    for i in range(2):
        x = xr.next()
        k.dma(x[:, :], memsrc[i * 128:(i + 1) * 128, :], x, True)
        ss = ssr.next()
        hb = hbr.next()
        act(k, hb[:, :], x[:, :], AF.Square, [x], [hb, ss], accum=ss[:, 0:1])
        act(k, ss[:, 1:2], ss[:, 0:1], AF.Ln, [ss, epsb], [ss], bias=epsb[:, 0:1], scale=1.0 / D)
        act(k, ss[:, 2:3], ss[:, 1:2], AF.Exp, [ss], [ss], scale=-0.5)
        stt(k, hb[:, :], x[:, :], ss[:, 2:3], gb[:, :], ALU.mult, ALU.mult, [x, ss, gb], [hb])
        for q in range(4):
            tp = tpr.next()
            for cc in range(4):
                tr(k, tp[:, cc, :], hb[:, (4 * q + cc) * 128:(4 * q + cc + 1) * 128], ident[:, :], [hb, ident], [tp])
            copy(k, "dve", hmT[:, 4 * q:4 * q + 4, i * 128:(i + 1) * 128], tp[:, :, :], [tp], [hmT])
    kms = ph.sb([128, 8, NMEM], BF16)
    for j in range(8):
        ps = pr.next()
        for kc in range(KC):
            mm(k, ps[:, 0:NMEM], w[:, kc, j * 128:(j + 1) * 128], hmT[:, kc, :], kc == 0, kc == KC - 1, [w, hmT], [ps])
        copy(k, "act", kms[:, j, :], ps[:, 0:NMEM], [ps], [kms])
    k.dma(kmT[:, :, :], kms[:, :, :], kms, False)
    vms = ph.sb([128, 2, 4, 257], BF16)
    k.op("pool", lambda e: e.memset(vms[:, :, :, :], 1.0), [], [vms])
    for mt in range(2):
        for cb in range(2):
            ps = pr.next()
            for kc in range(KC):
                mm(k, ps[:, :], hmT[:, kc, mt * 128:(mt + 1) * 128], w[:, kc, 1024 + cb * 512:1024 + (cb + 1) * 512],
                   kc == 0, kc == KC - 1, [w, hmT], [ps])
            copy(k, "dve", vms[:, mt, 2 * cb:2 * cb + 2, 0:256], ps[:, :].rearrange("p (h d) -> p h d", h=2), [ps], [vms])
    k.dma(vmx[:, :, :, :], vms[:, :, :, :], vms, False)
    ph.end()


def bc_mid(ap2d, n):
    a = ap2d.ap
    return bass.AP(ap2d.tensor, ap2d.offset, [list(a[0]), [0, n], list(a[1])])


def phase_mematt(k, c, qmT, kmT, vmx, sg, om, T):
    ph = Phase(k, "ma")
    kms = ph.sb([128, 8, NMEM], BF16)
    k.dma(kms[:, :, :], kmT[:, :, :], kms, True)
    vms = ph.sb([128, 2, 4, 257], BF16)
    k.dma(vms[:, :, :, :], vmx[:, :, :, :], vms, True)
    qr = ph.ring(2, [128, 8, 512], BF16)
    sgr = ph.ring(2, [128, 4, 1024], BF16)
    ptr = ph.ring(2, [128, 2, 512], BF16)
    pss = ph.ring(3, [128, 512], F32, psum=True)
    pso = ph.ring(3, [128, 512], F32, psum=True)
    rdr = ph.ring(4, [128, 2], F32)
    otr = ph.ring(8, [128, 1024], BF16)
    n5 = T // 512
    tl = {}

    def load(i):
        q = qr.next()
        g = sgr.next()
        tl[i] = (q, g)
        k.dma(q[:, :, :], qmT[:, :, i * 512:(i + 1) * 512].rearrange("j p t -> p j t"), q, True)
        k.dma(g[:, :, :], sg[i * 512:(i + 1) * 512, 3072:4096].rearrange("(s p) d -> p s d", p=128), g, True)

    load(0)
    for i in range(n5):
        if i + 1 < n5:
            load(i + 1)
        q, g = tl.pop(i)
        ots = [otr.next() for _ in range(4)]
        for hm in range(4):
            pt = ptr.next()
            for mt in range(2):
                ps = pss.next()
                for dc in range(2):
                    mm(k, ps[:, :], kms[:, 2 * hm + dc, mt * 128:(mt + 1) * 128], q[:, 2 * hm + dc, :],
                       dc == 0, dc == 1, [kms, q], [ps])
                act(k, pt[:, mt, :], ps[:, :], AF.Exp, [ps], [pt], scale=1.0 / 16.0)
            for sub in range(4):
                po = pso.next()
                for mt in range(2):
                    mm(k, po[:, 0:257], pt[:, mt, sub * 128:(sub + 1) * 128], vms[:, mt, hm, :],
                       mt == 0, mt == 1, [pt, vms], [po])
                rd = rdr.next()
                k.op("dve", lambda e, rd=rd, po=po: e.reciprocal(out=rd[:, 0:1], in_=po[:, 256:257]), [po], [rd])
                stt(k, ots[sub][:, hm * 256:(hm + 1) * 256], po[:, 0:256], rd[:, 0:1],
                    g[:, sub, hm * 256:(hm + 1) * 256], ALU.mult, ALU.mult, [po, rd, g], [ots[sub]])
        for sub in range(4):
            t0 = i * 512 + sub * 128
            k.dma(om[t0:t0 + 128, :], ots[sub][:, :], ots[sub], False)
    ph.end()


def phase_chain(k, c, zsrc, qoff, foff, lbD, cbD, lcol0, qtD, ktD, khD, decD, T):
    ph = Phase(k, "ch")
    lbs = ph.sb([128, 48], F32)
    k.dma(lbs[:, :], lbD[:, :], lbs, True)
    cbs = ph.sb([128, 48], F32)
    k.dma(cbs[:, :], cbD[:, :], cbs, True)
    one = ph.sb([128, 64], F32)
    k.dma(one[:, :], c.ones[:, :], one, True)
    msk = ph.sb([128, 512], F32)
    k.dma(msk[:, :], c.cmask[:, :], msk, True)
    ident = ph.sb([128, 128], BF16)
    k.dma(ident[:, :], c.ident[:, :], ident, True)
    zr = ph.ring(4, [128, 512], F32)
    names = ("eq", "Lq", "e", "L1", "L2", "lf", "G", "t", "u", "w", "v1", "E")
    R = {n: ph.ring(2, [128, 512], F32) for n in names}
    qtr = ph.ring(2, [128, 512], BF16)
    ktr = ph.ring(2, [128, 512], BF16)
    khr = ph.ring(2, [128, 512], BF16)
    khs = ph.ring(2, [128, 4, 128], BF16)
    tpr = ph.ring(2, [128, 4, 128], BF16, psum=True)
    nch = T // 32
    decs = ph.sb([128, NH, nch], F32)
    items = [(h, tc) for h in range(NH) for tc in range(T // 512)]
    zt = {}
    sc = float(128 ** -0.5)

    def load(i):
        h, tc = items[i]
        zq = zr.next()
        zf = zr.next()
        zt[i] = (zq, zf)
        k.dma(zq[:, :], zsrc[qoff + h, :, tc * 512:(tc + 1) * 512], zq, True)
        k.dma(zf[:, :], zsrc[foff + h, :, tc * 512:(tc + 1) * 512], zf, True)

    load(0)
    for i, (h, tc) in enumerate(items):
        if i + 1 < len(items):
            load(i + 1)
        zq, zf = zt.pop(i)
        col = lcol0 + h
        t = {n: R[n].next() for n in names}
        o1 = one[:, 0:1]
        act(k, t["eq"][:, :], zq[:, :], AF.Exp, [zq], [t["eq"]], scale=-1.0)
        act(k, t["Lq"][:, :], t["eq"][:, :], AF.Ln, [t["eq"], one], [t["Lq"]], bias=o1)
        act(k, t["e"][:, :], zf[:, :], AF.Exp, [zf], [t["e"]], scale=-1.0)
        act(k, t["L1"][:, :], t["e"][:, :], AF.Ln, [t["e"], one], [t["L1"]], bias=o1)
        act(k, t["L2"][:, :], t["e"][:, :], AF.Ln, [t["e"], one, lbs], [t["L2"]], bias=o1, scale=lbs[:, col:col + 1])
        tt(k, "pool", t["lf"][:, :], t["L2"][:, :], t["L1"][:, :], ALU.subtract, [t["L2"], t["L1"]], [t["lf"]])
        G = t["G"]
        k.op("dve", lambda e, G=G, lf=t["lf"]: e.tensor_tensor_scan(
            out=G[:, :], data0=msk[:, :], data1=lf[:, :], initial=0.0, op0=ALU.mult, op1=ALU.add), [msk, t["lf"]], [G])
        tt(k, "pool", t["t"][:, :], zf[:, :], t["L1"][:, :], ALU.add, [zf, t["L1"]], [t["t"]])
        tt(k, "pool", t["u"][:, :], t["t"][:, :], G[:, :], ALU.add, [t["t"], G], [t["u"]])
        kt = ktr.next()
        act(k, kt[:, :], t["u"][:, :], AF.Exp, [t["u"], cbs], [kt], bias=cbs[:, col:col + 1], scale=-1.0)
        G3 = G[:, :].rearrange("p (c j) -> p c j", j=32)
        glast = G3[:, :, 31:32]
        ga = glast.ap
        glb = bass.AP(glast.tensor, glast.offset, [list(ga[0]), list(ga[1]), [0, 32]])
        tt(k, "dve", t["w"][:, :].rearrange("p (c j) -> p c j", j=32), t["u"][:, :].rearrange("p (c j) -> p c j", j=32),
           glb, ALU.subtract, [t["u"], G], [t["w"]])
        kh = khr.next()
        act(k, kh[:, :], t["w"][:, :], AF.Exp, [t["w"], cbs], [kh], bias=cbs[:, col:col + 1], scale=-1.0)
        tt(k, "pool", t["v1"][:, :], G[:, :], t["Lq"][:, :], ALU.subtract, [G, t["Lq"]], [t["v1"]])
        act(k, t["E"][:, :], t["v1"][:, :], AF.Exp, [t["v1"]], [t["E"]])
        qt = qtr.next()
        stt(k, qt[:, :], zq[:, :], sc, t["E"][:, :], ALU.mult, ALU.mult, [zq, t["E"]], [qt])
        copy(k, "pool", decs[:, h, tc * 16:(tc + 1) * 16], G3[:, :, 31], [G], [decs])
        tp = tpr.next()
        for jj in range(4):
            tr(k, tp[:, jj, :], kh[:, jj * 128:(jj + 1) * 128], ident[:, :], [kh, ident], [tp])
        ks = khs.next()
        copy(k, "dve", ks[:, :, :], tp[:, :, :], [tp], [ks])
        t0 = tc * 512
        k.dma(khD[t0:t0 + 512, h * 128:(h + 1) * 128].rearrange("(j p) d -> p j d", p=128), ks[:, :, :], ks, False)
        k.dma(qtD[h, :, t0:t0 + 512], qt[:, :], qt, False)
        k.dma(ktD[h, :, t0:t0 + 512], kt[:, :], kt, False)
    k.dma(decD[:, :, :], decs[:, :, :], decs, False)
    ph.end()


def phase_mix(k, c, qtD, ktD, khD, vD, decD, oD, T):
    ph = Phase(k, "mx")
    msk = ph.sb([128, 128], F32)
    k.dma(msk[:, :], c.trimask[:, :], msk, True)
    nch = T // 32
    SEG = 1024
    nseg = T // SEG
    decr = ph.ring(2, [128, nch], F32)
    dexr = ph.ring(2, [128, 33], F32)
    for dx in dexr.tiles:
        k.dma(dx[:, :], c.zeros[:, 0:33], dx, True)
    qr = ph.ring(2, [128, SEG], BF16)
    kr = ph.ring(2, [128, SEG], BF16)
    hr = ph.ring(2, [128, 8, 128], BF16)
    vr = ph.ring(2, [128, 8, 128], BF16)
    dsx = ph.ring(2, [128, 33, 128], F32)
    sal = ph.ring(2, [128, 33, 128], F32)
    sbr = ph.ring(2, [128, 32, 128], BF16)
    atr = ph.ring(2, [128, 128], BF16)
    osr = ph.ring(3, [128, 128], F32)
    psd = [ph.ps([128, 512], F32) for _ in range(4)]
    par = ph.ring(2, [128, 512], F32, psum=True)
    por = ph.ring(2, [128, 512], F32, psum=True)
    items = [(h, sg_) for h in range(NH) for sg_ in range(nseg)]
    tl = {}

    def load(i):
        h, sg_ = items[i]
        t0 = sg_ * SEG
        q = qr.next(); kk = kr.next(); hh = hr.next(); vv = vr.next()
        tl[i] = (q, kk, hh, vv)
        k.dma(q[:, :], qtD[h, :, t0:t0 + SEG], q, True)
        k.dma(kk[:, :], ktD[h, :, t0:t0 + SEG], kk, True)
        k.dma(hh[:, :, :], khD[t0:t0 + SEG, h * 128:(h + 1) * 128].rearrange("(b p) d -> p b d", p=128), hh, True)
        k.dma(vv[:, :, :], vD[t0:t0 + SEG, h * 128:(h + 1) * 128].rearrange("(b p) d -> p b d", p=128), vv, True)

    load(0)
    prevS = None
    dec = None
    for i, (h, sg_) in enumerate(items):
        if i + 1 < len(items):
            load(i + 1)
        q, kk, hh, vv = tl.pop(i)
        if sg_ == 0:
            dec = decr.next()
            k.dma(dec[:, :], decD[:, h, :], dec, True)
        dx = dexr.next()
        act(k, dx[:, 1:33], dec[:, sg_ * 32:(sg_ + 1) * 32], AF.Exp, [dec], [dx])
        ds = dsx.next()
        if sg_ == 0:
            k.dma(ds[:, 0, :], c.zeros[:, 0:128], ds, True)
        else:
            copy(k, "pool", ds[:, 0, :], prevS[:, 32, :], [prevS], [ds])
        ds4 = ds[:, 1:33, :].rearrange("p (b c) d -> p b c d", c=4)
        for g in range(2):
            for b in range(4):
                blk = g * 4 + b
                for c4 in range(4):
                    kw = {"tile_position": (96, 0)} if c4 == 3 else {}
                    mm(k, psd[c4][:, b * 128:(b + 1) * 128], hh[32 * c4:32 * c4 + 32, blk, :],
                       vv[32 * c4:32 * c4 + 32, blk, :], True, True, [hh, vv], [psd[c4]], **kw)
            for c4 in range(4):
                copy(k, "act", ds4[:, g * 4:(g + 1) * 4, c4, :], psd[c4][:, :].rearrange("p (b d) -> p b d", d=128),
                     [psd[c4]], [ds])
        S = sal.next()
        copy(k, "pool", S[:, 0, :], ds[:, 0, :], [ds], [S])
        for ci in range(32):
            stt(k, S[:, ci + 1, :], S[:, ci, :], dx[:, ci + 1:ci + 2], ds[:, ci + 1, :], ALU.mult, ALU.add,
                [S, dx, ds], [S])
        Sb = sbr.next()
        copy(k, "act", Sb[:, :, :], S[:, 0:32, :], [S], [Sb])
        prevS = S
        for blk in range(8):
            pa = par.next()
            mm(k, pa[:, 0:128], kk[:, blk * 128:(blk + 1) * 128], q[:, blk * 128:(blk + 1) * 128], True, True, [kk, q], [pa])
            at = atr.next()
            tt(k, "dve", at[:, :], pa[:, 0:128], msk[:, :], ALU.mult, [pa, msk], [at])
            po = por.next()
            mm(k, po[:, 0:128], at[:, :], vv[:, blk, :], True, False, [at, vv], [po])
            for c4 in range(4):
                kw = {"tile_position": (0, 96)} if c4 == 3 else {}
                mm(k, po[32 * c4:32 * c4 + 32, 0:128], q[:, blk * 128 + 32 * c4:blk * 128 + 32 * c4 + 32],
                   Sb[:, blk * 4 + c4, :], False, c4 == 3, [q, Sb], [po], **kw)
            os_ = osr.next()
            copy(k, "pool" if False else "dve", os_[:, :], po[:, 0:128], [po], [os_])
            t0 = sg_ * SEG + blk * 128
            k.dma(oD[t0:t0 + 128, h * 128:(h + 1) * 128], os_[:, :], os_, False)
    ph.end()


def phase_outn(k, c, oF, oBr, sg, ongv, ob, T):
    ph = Phase(k, "on")
    jf = ph.sb([128, 128], F32)
    k.dma(jf[:, :], c.jrevf[:, :], jf, True)
    ong = ph.sb([128, TOK_W], F32)
    k.dma(ong[:, :], ongv.partition_broadcast(128), ong, True)
    epsb = ph.sb([128, 1], F32)
    k.op("pool", lambda e: e.memset(epsb[:, :], EPS), [], [epsb])
    ofr = ph.ring(2, [128, TOK_W], F32)
    obr = ph.ring(2, [128, TOK_W], F32)
    sgr = ph.ring(2, [128, TOK_W], BF16)
    sqr = ph.ring(1, [128, TOK_W], F32)
    gsr = ph.ring(1, [128, TOK_W], F32)
    outr = ph.ring(2, [128, TOK_W], BF16)
    ssr = ph.ring(2, [128, 3, NH], F32)
    pr = ph.ring(6, [128, 512], F32, psum=True)
    nt = T // 128
    tl = {}

    def load(i):
        a = ofr.next(); b = obr.next(); g = sgr.next()
        tl[i] = (a, b, g)
        ir = nt - 1 - i
        k.dma(a[:, :], oF[i * 128:(i + 1) * 128, :], a, True)
        k.dma(b[:, :], oBr[ir * 128:(ir + 1) * 128, :], b, True)
        k.dma(g[:, :], sg[i * 128:(i + 1) * 128, 0:TOK_W], g, True)

    load(0)
    for i in range(nt):
        if i + 1 < nt:
            load(i + 1)
        a, b, g = tl.pop(i)
        for cb in range(6):
            ps = pr.next()
            mm(k, ps[:, :], jf[:, :], b[:, cb * 512:(cb + 1) * 512], True, True, [jf, b], [ps])
            tt(k, "dve", a[:, cb * 512:(cb + 1) * 512], a[:, cb * 512:(cb + 1) * 512], ps[:, :], ALU.add, [a, ps], [a])
        sq = sqr.next()
        tt(k, "pool", sq[:, :], a[:, :], a[:, :], ALU.mult, [a], [sq])
        ss = ssr.next()
        k.op("dve", lambda e, ss=ss, sq=sq: e.tensor_reduce(
            out=ss[:, 0, :], in_=sq[:, :].rearrange("p (h d) -> p h d", d=128), axis=AX.X, op=ALU.add), [sq], [ss])
        act(k, ss[:, 1, :], ss[:, 0, :], AF.Ln, [ss, epsb], [ss], bias=epsb[:, 0:1], scale=1.0 / 128)
        act(k, ss[:, 2, :], ss[:, 1, :], AF.Exp, [ss], [ss], scale=-0.5)
        gs = gsr.next()
        tt(k, "pool", gs[:, :], g[:, :], ong[:, :], ALU.mult, [g, ong], [gs])
        r2 = ss[:, 2, :]
        ra = r2.ap
        rb = bass.AP(r2.tensor, r2.offset, [list(ra[0]), list(ra[1]), [0, 128]])
        tt(k, "dve", sq[:, :].rearrange("p (h d) -> p h d", d=128), a[:, :].rearrange("p (h d) -> p h d", d=128), rb,
           ALU.mult, [a, ss], [sq])
        o = outr.next()
        tt(k, "dve", o[:, :], sq[:, :], gs[:, :], ALU.mult, [sq, gs], [o])
        k.dma(ob[i * 128:(i + 1) * 128, :], o[:, :], o, False)
    ph.end()


def phase_outw(k, c, ob, om, wo, xsrc, xdst, T):
    ph = Phase(k, "ow")
    ident = ph.sb([128, 128], BF16)
    k.dma(ident[:, :], c.ident[:, :], ident, True)
    TTo = 512
    obr = ph.ring(2, [128, MIX_W], BF16)
    otr = ph.ring(2, [128, 32, TTo], BF16)
    wr = ph.ring(2, [128, 32, 512], BF16)
    xpr = ph.ring(3, [128, 512], F32)
    xor_ = ph.ring(3, [128, 512], F32)
    tpr = ph.ring(3, [128, 4, 128], BF16, psum=True)
    pr = ph.ring(4, [128, 512], F32, psum=True)
    nst = T // TTo
    ev = 0
    items = [(st, db) for st in range(nst) for db in range(4)]
    wt = {}

    def loadw(i):
        w = wr.next()
        wt[i] = w
        db = items[i][1]
        k.dma(w[:, :, :], wo[:, :, db * 512:(db + 1) * 512], w, True)

    loadw(0)
    oT = None
    for i, (st, db) in enumerate(items):
        if db == 0:
            oT = otr.next()
            for sub in range(4):
                t0 = st * TTo + sub * 128
                o = obr.next()
                k.dma(o[:, 0:TOK_W], ob[t0:t0 + 128, :], o, True)
                k.dma(o[:, TOK_W:MIX_W], om[t0:t0 + 128, :], o, True)
                for q in range(8):
                    tp = tpr.next()
                    for cc in range(4):
                        tr(k, tp[:, cc, :], o[:, (4 * q + cc) * 128:(4 * q + cc + 1) * 128], ident[:, :], [o, ident], [tp])
                    copy(k, "act" if ev % 2 == 0 else "dve", oT[:, 4 * q:4 * q + 4, sub * 128:(sub + 1) * 128],
                         tp[:, :, :], [tp], [oT])
                    ev += 1
        if i + 1 < len(items):
            loadw(i + 1)
        w = wt.pop(i)
        for sub in range(4):
            t0 = st * TTo + sub * 128
            xp = xpr.next()
            k.dma(xp[:, :], xsrc[t0:t0 + 128, db * 512:(db + 1) * 512], xp, True)
            ps = pr.next()
            for kc in range(32):
                mm(k, ps[:, :], oT[:, kc, sub * 128:(sub + 1) * 128], w[:, kc, :], kc == 0, kc == 31, [oT, w], [ps])
            xo = xor_.next()
            tt(k, "dve", xo[:, :], ps[:, :], xp[:, :], ALU.add, [ps, xp], [xo])
            k.dma(xdst[t0:t0 + 128, db * 512:(db + 1) * 512], xo[:, :], xo, False)
    ph.end()


def na_ranges(rows):
    rs = lambda r: min(max(r - 4, 0), rows - 8)
    out = []
    for j in range(rows):
        al = [r for r in range(rows) if rs(r) <= j <= rs(r) + 7]
        assert al == list(range(al[0], al[-1] + 1))
        out.append((al[0], al[-1]))
    return out


def phase_na(k, c, fm16, vD, sg, btab, ob, T):
    ph = Phase(k, "na")
    rows = T // GW
    nt = T // 128
    rng_ = na_ranges(rows)
    cmk = ph.sb([128, GW], F32)
    k.dma(cmk[0:64, :], c.colmask[:, :], cmk, True)
    k.dma(cmk[64:128, :], c.colmask[:, :], cmk, True)
    onesb = ph.sb([128, 64], F32)
    k.dma(onesb[:, :], c.ones[:, :], onesb, True)
    qr = ph.ring(2, [128, T], BF16)
    kr = ph.ring(2, [128, T], BF16)
    vr = ph.ring(2, [128, nt, 129], BF16)
    for v in vr.tiles:
        copy(k, "dve", v[:, :, 128:129], onesb[:, 0:nt].rearrange("p (a b) -> p a b", b=1), [onesb], [v])
    sgr = ph.ring(2, [128, nt, 128], BF16)
    obr = ph.ring(2, [128, nt, 128], BF16)
    crf = ph.ring(2, [128, 15, GW], F32)
    crb = ph.ring(2, [128, 15, GW], BF16)
    exr = ph.ring(2, [128, 768], BF16)
    ptr = ph.ring(8, [128, 768], BF16)
    pss = ph.ring(2, [128, 1024], F32, psum=True)
    pso = ph.ring(2, [128, 512], F32, psum=True)
    rdr = ph.ring(4, [128, 2], F32)
    info = []
    for kt in range(nt):
        (a0, b0), (a1, b1) = rng_[2 * kt], rng_[2 * kt + 1]
        qlo = (min(a0, a1) // 2) * 2
        qhi = (max(b0, b1) // 2) * 2 + 1
        info.append((qlo, qhi, (a0, b0), (a1, b1)))
    kts_of = {}
    for kt in range(nt):
        qlo, qhi = info[kt][0], info[kt][1]
        for R in range(qlo // 2, qhi // 2 + 1):
            kts_of.setdefault(R, []).append(kt)
    for R, l in kts_of.items():
        assert l[-1] - l[0] <= 6, (R, l)
    sc = float(128 ** -0.5)
    tl = {}

    def load(h):
        q = qr.next(); kk = kr.next(); v = vr.next(); g = sgr.next(); cf = crf.next()
        tl[h] = (q, kk, v, g, cf)
        k.dma(q[:, :], fm16[h, :, :], q, True)
        k.dma(kk[:, :], fm16[NH + h, :, :], kk, True)
        k.dma(v[:, :, 0:128], vD[:, h * 128:(h + 1) * 128].rearrange("(b p) d -> p b d", p=128), v, True)
        k.dma(g[:, :, :], sg[:, h * 128:(h + 1) * 128].rearrange("(b p) d -> p b d", p=128), g, True)
        k.dma(cf[0:64, :, :], btab[h], cf, True)
        k.dma(cf[64:128, :, :], btab[h], cf, True)

    load(0)
    for h in range(NH):
        if h + 1 < NH:
            load(h + 1)
        q, kk, v, g, cf = tl.pop(h)
        act(k, cf[:, :, :], cf[:, :, :], AF.Exp, [cf], [cf])
        cb = crb.next()
        tt(k, "dve", cb[:, :, :], cf[:, :, :], bc_mid(cmk[:, :], 15), ALU.mult, [cf, cmk], [cb])
        obh = obr.next()
        pts = {}
        for kt in range(nt):
            qlo, qhi, r0, r1 = info[kt]
            nq = (qhi - qlo + 1) * GW
            ps = pss.next()
            n1 = min(512, nq)
            mm(k, ps[:, 0:n1], kk[:, kt * 128:(kt + 1) * 128], q[:, qlo * GW:qlo * GW + n1], True, True, [kk, q], [ps])
            if nq > 512:
                mm(k, ps[:, 512:nq], kk[:, kt * 128:(kt + 1) * 128], q[:, qlo * GW + 512:qlo * GW + nq], True, True,
                   [kk, q], [ps])
            ex = exr.next()
            act(k, ex[:, 0:nq], ps[:, 0:nq], AF.Exp, [ps], [ex], scale=sc)
            pt = ptr.next()
            pts[kt] = pt
            k.op("pool", lambda e, pt=pt, nq=nq: e.memset(pt[:, 0:nq], 0.0), [], [pt])
            for hf, (rlo, rhi) in enumerate((r0, r1)):
                j = 2 * kt + hf
                c0 = (rlo - qlo) * GW
                nr = rhi - rlo + 1
                d0 = 7 - j + rlo
                assert 0 <= d0 and d0 + nr <= 15
                tt(k, "dve", pt[64 * hf:64 * hf + 64, c0:c0 + nr * GW], ex[64 * hf:64 * hf + 64, c0:c0 + nr * GW],
                   cb[64 * hf:64 * hf + 64, d0:d0 + nr, :].rearrange("p a b -> p (a b)"), ALU.mult, [ex, cb], [pt])
            for R in sorted(kts_of):
                l = kts_of[R]
                if l[-1] != kt:
                    continue
                po = pso.next()
                for n_, k2 in enumerate(l):
                    off = (R - info[k2][0] // 2) * 128
                    mm(k, po[:, 0:129], pts[k2][:, off:off + 128], v[:, k2, :], n_ == 0, n_ == len(l) - 1,
                       [pts[k2], v], [po])
                rd = rdr.next()
                k.op("dve", lambda e, rd=rd, po=po: e.reciprocal(out=rd[:, 0:1], in_=po[:, 128:129]), [po], [rd])
                stt(k, obh[:, R, :], po[:, 0:128], rd[:, 0:1], g[:, R, :], ALU.mult, ALU.mult, [po, rd, g], [obh])
        k.dma(ob[:, h * 128:(h + 1) * 128].rearrange("(b p) d -> p b d", p=128), obh[:, :, :], obh, False)
    ph.end()


def phase_lb(k, c, lblT, lbD, cbD):
    ph = Phase(k, "lb")
    l0 = ph.sb([128, 48], F32)
    l1 = ph.sb([128, 48], F32)
    one = ph.sb([128, 64], F32)
    k.dma(l0[:, :], lblT[0], l0, True)
    k.dma(l1[:, :], lblT[1], l1, True)
    k.dma(one[:, :], c.ones[:, :], one, True)
    d = ph.sb([128, 48], F32)
    e = ph.sb([128, 48], F32)
    cb = ph.sb([128, 48], F32)
    lb = ph.sb([128, 48], F32)
    tt(k, "dve", d[:, :], l1[:, :], l0[:, :], ALU.subtract, [l0, l1], [d])
    act(k, e[:, :], d[:, :], AF.Exp, [d], [e])
    act(k, cb[:, :], e[:, :], AF.Ln, [e, one], [cb], bias=one[:, 0:1])
    ts(k, "dve", cb[:, :], cb[:, :], -1.0, None, ALU.mult, None, [cb], [cb])
    tt(k, "dve", d[:, :], d[:, :], cb[:, :], ALU.add, [d, cb], [d])
    act(k, lb[:, :], d[:, :], AF.Exp, [d], [lb])
    k.dma(lbD[:, :], lb[:, :], lb, False)
    k.dma(cbD[:, :], cb[:, :], cb, False)
    ph.end()


def build(T, NS, depth, debug=()):
    nc = bass.Bass("TRN2", target_bir_lowering=False)
    c = Cfg()
    c.T = T

    def din(name, shape, dt=F32):
        return nc.dram_tensor(name, list(shape), dt, kind="ExternalInput").ap()

    def dscr(name, shape, dt):
        kind = "ExternalOutput" if name in debug else "Internal"
        return nc.dram_tensor(name, list(shape), dt, kind=kind).ap()

    x_in = din("x_in", [NS, T, D])
    mem_in = din("mem_in", [NS, NMEM, D])
    w_in_a = din("w_in_a", [2, D, P_A])
    w_in_b = din("w_in_b", [2, D, P_B])
    w_mem_kv = din("w_mem_kv", [4, D, 2 * MEM_W])
    w_out = din("w_out", [4, MIX_W, D])
    norm_g = din("norm_g", [4, D])
    mem_norm_g = din("mem_norm_g", [4, D])
    final_g = din("final_g", [1, D])
    onorm_g = din("hg_onorm_g", [2, TOK_W])
    lblT = din("lblT", [2, 128, 48])
    btab = din("btab", [2, NH, GW, 15, GW])
    c.ident = din("ident", [128, 128], BF16)
    c.jrev = din("jrev", [128, 128], BF16)
    c.jrevf = din("jrevf", [128, 128])
    c.ones = din("ones", [128, 64])
    c.zeros = din("zeros", [128, 128])
    c.cmask = din("cmask", [128, 512])
    c.trimask = din("trimask", [128, 128])
    c.colmask = din("colmask", [GW, GW])
    y = nc.dram_tensor("y", [NS, T, D], F32, kind="ExternalOutput").ap()

    hT = dscr("hT", [128, KC, T], BF16)
    hTr = dscr("hTr", [128, KC, T], BF16)
    wbf = [dscr("wbf%d" % l, [128, KC, P_A if l % 2 == 0 else P_B], BF16) for l in range(4)]
    wkv = [dscr("wkv%d" % l, [128, KC, 2 * MEM_W], BF16) for l in range(4)]
    wo = [dscr("wo%d" % l, [128, 32, D], BF16) for l in range(4)]
    xres = dscr("xres", [NS, T, D], F32)
    zA = dscr("zA", [48, 128, T], F32)
    fm16 = dscr("fm16", [48, 128, T], BF16)
    qmT = dscr("qmT", [8, 128, T], BF16)
    vD = dscr("vD", [T, TOK_W], BF16)
    sg = dscr("sg", [T, MIX_W], BF16)
    om = dscr("om", [T, MEM_W], BF16)
    ob = dscr("ob", [T, TOK_W], BF16)
    kmT = dscr("kmT", [128, 8, NMEM], BF16)
    vmx = dscr("vmx", [128, 2, 4, 257], BF16)
    qtD = dscr("qtD", [NH, 128, T], BF16)
    ktD = dscr("ktD", [NH, 128, T], BF16)
    khD = dscr("khD", [T, TOK_W], BF16)
    decD = dscr("decD", [128, NH, T // 32], F32)
    oF = dscr("oF", [T, TOK_W], F32)
    oB = dscr("oB", [T, TOK_W], F32)
    lbD = [dscr("lbD%d" % a, [128, 48], F32) for a in range(2)]
    cbD = [dscr("cbD%d" % a, [128, 48], F32) for a in range(2)]

    with ExitStack() as es:
        k = KB(nc, es)
        for l in range(depth):
            phase_conv(k, (w_in_a if l % 2 == 0 else w_in_b)[l // 2], wbf[l], D, P_A if l % 2 == 0 else P_B)
            phase_conv(k, w_mem_kv[l], wkv[l], D, 2 * MEM_W)
            phase_conv(k, w_out[l], wo[l], MIX_W, D)
        ph = Phase(k, "z")
        zt = ph.sb([128, 48], F32)
        k.dma(zt[:, :], c.zeros[:, 0:48], zt, True)
        k.dma(lbD[0][:, :], zt[:, :], zt, False)
        k.dma(cbD[0][:, :], zt[:, :], zt, False)
        ph.end()
        phase_lb(k, c, lblT, lbD[1], cbD[1])
        for l in range(depth):
            for s in range(NS):
                xsrc = x_in[s] if l == 0 else xres[s]
                if l % 2 == 0:
                    a = l // 2
                    phase_prep(k, c, xsrc, norm_g[l:l + 1, :], T, hT=hT, hTr=hTr)
                    phase_memkv(k, c, mem_in[s], mem_norm_g[l:l + 1, :], wkv[l], kmT, vmx)
                    jobs = [("FM", 0, 3072, zA[0:24], "f32"), ("FM", 3072, 3072, zA[24:48], "f32"),
                            ("TM", 9216, 3072, vD, "bf16"),
                            ("FM", 12288, 1024, qmT, "bf16"), ("TM", 13312, 4096, sg, "silu")]
                    phase_proj(k, c, hT, wbf[l], jobs, T)
                    phase_chain(k, c, zA, 0, 24, lbD[a], cbD[a], 0, qtD, ktD, khD, decD, T)
                    phase_mix(k, c, qtD, ktD, khD, vD, decD, oF, T)
                    phase_mematt(k, c, qmT, kmT, vmx, sg, om, T)
                    jobs = [("FM", 0, 3072, zA[0:24], "f32"), ("FM", 6144, 3072, zA[24:48], "f32"),
                            ("TM", 9216, 3072, vD, "bf16")]
                    phase_proj(k, c, hTr, wbf[l], jobs, T)
                    phase_chain(k, c, zA, 0, 24, lbD[a], cbD[a], 24, qtD, ktD, khD, decD, T)
                    phase_mix(k, c, qtD, ktD, khD, vD, decD, oB, T)
                    phase_outn(k, c, oF, oB, sg, onorm_g[a:a + 1, :], ob, T)
                else:
                    phase_prep(k, c, xsrc, norm_g[l:l + 1, :], T, hT=hT)
                    phase_memkv(k, c, mem_in[s], mem_norm_g[l:l + 1, :], wkv[l], kmT, vmx)
                    jobs = [("FM", 0, 3072, fm16[0:24], "bf16"), ("FM", 3072, 3072, fm16[24:48], "bf16"),
                            ("TM", 6144, 3072, vD, "bf16"), ("FM", 9216, 1024, qmT, "bf16"),
                            ("TM", 10240, 4096, sg, "silu")]
                    phase_proj(k, c, hT, wbf[l], jobs, T)
                    phase_mematt(k, c, qmT, kmT, vmx, sg, om, T)
                    phase_na(k, c, fm16, vD, sg, btab[l // 2], ob, T)
                phase_outw(k, c, ob, om, wo[l], xsrc, xres[s], T)
        for s in range(NS):
            phase_prep(k, c, xres[s] if depth > 0 else x_in[s], final_g[0:1, :], T, yout=y[s])
    return nc, k


def _consts():
    cm = np.ones((128, 16, 32), np.float32)
    cm[:, :, 0] = 0
    s_ = np.arange(128)[:, None]
    t_ = np.arange(128)[None, :]
    tri = ((s_ // 32 == t_ // 32) & (s_ <= t_)).astype(np.float32)
    jr = np.eye(128, dtype=np.float32)[::-1].copy()
    kc = np.arange(GW)[:, None]
    qc = np.arange(GW)[None, :]
    cs = np.clip(qc - 8, 0, GW - 16)
    colmask = ((kc >= cs) & (kc < cs + 16)).astype(np.float32)
    return {"ident": np.eye(128, dtype=np.float32).astype(ml_dtypes.bfloat16), "ones": np.ones((128, 64), np.float32),
            "zeros": np.zeros((128, 128), np.float32), "cmask": cm.reshape(128, 512), "trimask": tri,
            "jrev": jr.astype(ml_dtypes.bfloat16), "jrevf": jr, "colmask": colmask}


def _layout_inputs(hg_lb_logits, na_rpb):
    lblT = np.ascontiguousarray(
        np.asarray(hg_lb_logits, np.float32).reshape(2, 2, NH, 128).transpose(0, 3, 1, 2).reshape(2, 128, 48))
    kc = np.arange(GW)[:, None]
    qc = np.arange(GW)[None, :]
    dc = np.clip(kc - qc + 15, 0, 30)
    rp = np.asarray(na_rpb, np.float32)[:, :, ::-1, :]
    bt = rp[:, :, :, dc]
    btab = np.ascontiguousarray(bt.transpose(0, 1, 3, 2, 4))
    return lblT, btab


def make_in_maps(NS, xs, mems, w_in_a, w_in_b, w_mem_kv, w_out, norm_g, mem_norm_g, hg_lb_logits, hg_onorm_g,
                 na_rpb, final_g):
    cst = _consts()
    lblT, btab = _layout_inputs(hg_lb_logits, na_rpb)
    maps = []
    for x, m_ in zip(xs, mems):
        m = {"x_in": x, "mem_in": m_, "w_in_a": w_in_a, "w_in_b": w_in_b, "w_mem_kv": w_mem_kv, "w_out": w_out,
             "norm_g": norm_g, "mem_norm_g": mem_norm_g, "final_g": np.asarray(final_g).reshape(1, D),
             "hg_onorm_g": hg_onorm_g, "lblT": lblT, "btab": btab}
        m.update(cst)
        maps.append(m)
    return maps


def kernel(x_prompt, x_sample, mem_prompt, mem_sample, w_in_a, w_in_b, w_mem_kv, w_out,
           norm_g, mem_norm_g, hg_lb_logits, hg_onorm_g, na_rpb, final_g):
    T = 4096
    NS = 2
    nc, k = build(T, NS, 4)
    xs = [np.ascontiguousarray(np.stack([x_sample[cid], x_prompt[cid % 2]])) for cid in range(8)]
    mems = [np.ascontiguousarray(np.stack([mem_sample[cid], mem_prompt[cid % 2]])) for cid in range(8)]
    in_maps = make_in_maps(NS, xs, mems, w_in_a, w_in_b, w_mem_kv, w_out, norm_g, mem_norm_g, hg_lb_logits,
                           hg_onorm_g, na_rpb, final_g)
    res = run_bass_kernel_spmd(nc, in_maps, core_ids=list(range(8)))
    ys = [r["y"] for r in res.results]
    y_sample = np.stack([ys[cid][0] for cid in range(8)])
    y_prompt = np.stack([ys[cid][1] for cid in range(2)])
    return (y_prompt.astype(np.float32), y_sample.astype(np.float32))
```

```python
import numpy as np
import ml_dtypes
from contextlib import ExitStack
import concourse.bass as bass
import concourse.mybir as mybir
from concourse.bass_utils import run_bass_kernel_spmd

F32 = mybir.dt.float32
BF16 = mybir.dt.bfloat16
AF = mybir.ActivationFunctionType
ALU = mybir.AluOpType
AX = mybir.AxisListType

D = 2048
KC = D // 128
TOK_W = 3072
NH = 24
MEM_W = 1024
MIX_W = 4096
P_A = 4 * TOK_W + MEM_W + MIX_W
P_B = 3 * TOK_W + MEM_W + MIX_W
NMEM = 256
EPS = 1e-6
GW = 64
CH = 32


class SemObj:
    __slots__ = ("sem", "step", "count")

    def __init__(self, sem, step):
        self.sem = sem
        self.step = step
        self.count = 0


class Buf:
    __slots__ = ("lw", "rd", "dsem")

    def __init__(self):
        self.lw = {}
        self.rd = {}
        self.dsem = None


class Tl:
    __slots__ = ("t", "b")

    def __init__(self, t):
        self.t = t
        self.b = Buf()

    def __getitem__(self, idx):
        return self.t[idx]


class Ring:
    def __init__(self, tiles):
        self.tiles = tiles
        self.i = 0

    def next(self):
        t = self.tiles[self.i % len(self.tiles)]
        self.i += 1
        return t


class KB:
    ENG = ("sp", "pe", "act", "dve", "pool")

    def __init__(self, nc, es, n_dsem=96):
        self.nc = nc
        self.esem = {n: SemObj(es.enter_context(nc.semaphore("s_" + n)), 1)
                     for n in ("pe", "act", "dve", "pool")}
        self.dsems = [SemObj(es.enter_context(nc.semaphore("d%d" % i)), 16) for i in range(n_dsem)]
        self.dfree = list(self.dsems)
        self.ops = {n: [] for n in self.ENG}
        self.waited = {n: {} for n in self.ENG}
        self.uid = 0
        self.nops = 0

    def _deps(self, en, R, W):
        need = {}
        for b in R:
            for so, v in b.lw.items():
                if need.get(so, 0) < v:
                    need[so] = v
        for b in W:
            for so, v in b.lw.items():
                if need.get(so, 0) < v:
                    need[so] = v
            for so, v in b.rd.items():
                if need.get(so, 0) < v:
                    need[so] = v
        waits = []
        wd = self.waited[en]
        pe_so = self.esem["pe"]
        for so, v in need.items():
            if en == "pe" and so is pe_so:
                continue
            if wd.get(so, 0) < v:
                waits.append((so.sem, v))
                wd[so] = v
        return waits

    def op(self, en, fn, R=(), W=()):
        R = [x.b if isinstance(x, Tl) else x for x in R]
        W = [x.b if isinstance(x, Tl) else x for x in W]
        waits = self._deps(en, R, W)
        so = self.esem[en]
        so.count += 1
        n = so.count
        self.ops[en].append((waits, fn, so.sem, 1))
        for b in R:
            b.rd[so] = n
        for b in W:
            b.lw[so] = n
        self.nops += 1

    def dma(self, out, in_, sb, load, **kw):
        b = sb.b if isinstance(sb, Tl) else sb
        if b.dsem is None:
            b.dsem = self.dfree.pop()
        waits = self._deps("sp", [] if load else [b], [b] if load else [])
        so = b.dsem
        so.count += 16
        self.ops["sp"].append((waits, lambda e: e.dma_start(out=out, in_=in_, **kw), so.sem, 16))
        if load:
            b.lw[so] = so.count
        else:
            b.rd[so] = so.count
        self.nops += 1

    def barrier(self):
        sos = list(self.esem.values()) + [d for d in self.dsems if d.count > 0]
        for en in self.ENG:
            wd = self.waited[en]
            waits = []
            for so in sos:
                if en == "pe" and so is self.esem["pe"]:
                    continue
                if wd.get(so, 0) < so.count:
                    waits.append((so.sem, so.count))
                    wd[so] = so.count
            if waits:
                self.ops[en].append((waits, None, None, 0))

    def flush(self):
        nc = self.nc
        with nc.Block() as block:
            decos = {"sp": block.sync, "pe": block.tensor, "act": block.scalar,
                     "dve": block.vector, "pool": block.gpsimd}
            for en in self.ENG:
                ops = self.ops[en]
                if not ops:
                    continue

                def body(e, ops=ops):
                    for waits, fn, sem, inc in ops:
                        for s, v in waits:
                            e.wait_ge(s, v)
                        if fn is not None:
                            fn(e).then_inc(sem, inc)
                decos[en](body)
                self.ops[en] = []


class Phase:
    def __init__(self, k, name):
        self.k = k
        k.uid += 1
        self.name = "%s%d" % (name, k.uid)
        self.es = ExitStack()
        self.tiles = []
        self.n = 0

    def _mk(self, fn, shape, dt):
        self.n += 1
        t = self.es.enter_context(fn("%s_%d" % (self.name, self.n), list(shape), dt))
        tl = Tl(t)
        self.tiles.append(tl)
        return tl

    def sb(self, shape, dt):
        return self._mk(self.k.nc.sbuf_tensor, shape, dt)

    def ps(self, shape, dt):
        return self._mk(self.k.nc.psum_tensor, shape, dt)

    def ring(self, n, shape, dt, psum=False):
        return Ring([(self.ps if psum else self.sb)(shape, dt) for _ in range(n)])

    def end(self):
        k = self.k
        k.barrier()
        k.flush()
        for tl in self.tiles:
            if tl.b.dsem is not None:
                k.dfree.append(tl.b.dsem)
                tl.b.dsem = None
        self.es.close()


def act(k, out, in_, func, R, W, bias=None, scale=1.0, accum=None):
    kw = {}
    if bias is not None:
        kw["bias"] = bias
    if accum is not None:
        kw["accum_out"] = accum
    k.op("act", lambda e: e.activation(out=out, in_=in_, func=func, scale=scale, **kw), R, W)


def tt(k, en, out, in0, in1, op, R, W):
    k.op(en, lambda e: e.tensor_tensor(out=out, in0=in0, in1=in1, op=op), R, W)


def ts(k, en, out, in0, s1, s2, op0, op1, R, W):
    if s2 is None:
        k.op(en, lambda e: e.tensor_scalar(out=out, in0=in0, scalar1=s1, scalar2=None, op0=op0), R, W)
    else:
        k.op(en, lambda e: e.tensor_scalar(out=out, in0=in0, scalar1=s1, scalar2=s2, op0=op0, op1=op1), R, W)


def stt(k, out, in0, scalar, in1, op0, op1, R, W):
    k.op("dve", lambda e: e.scalar_tensor_tensor(out=out, in0=in0, scalar=scalar, in1=in1, op0=op0, op1=op1), R, W)


def copy(k, en, out, in_, R, W):
    if en == "act":
        k.op("act", lambda e: e.copy(out=out, in_=in_), R, W)
    else:
        k.op(en, lambda e: e.tensor_copy(out=out, in_=in_), R, W)


def mm(k, out, lhsT, rhs, start, stop, R, W, **kw):
    k.op("pe", lambda e: e.matmul(out, lhsT, rhs, start=start, stop=stop, **kw), R, W)


def tr(k, out, in_, ident, R, W):
    k.op("pe", lambda e: e.transpose(out, in_, ident), R, W)


class Cfg:
    pass


def phase_conv(k, src2d, dst3d, K, P):
    ph = Phase(k, "cv")
    CW = 2048
    fr = ph.ring(3, [128, CW], F32)
    br = ph.ring(3, [128, CW], BF16)
    items = [(kc, c0) for kc in range(K // 128) for c0 in range(0, P, CW)]
    engs = ("dve", "act", "pool", "dve", "act")
    ftiles = {}

    def load(i):
        kc, c0 = items[i]
        n = min(CW, P - c0)
        f = fr.next()
        ftiles[i] = f
        k.dma(f[:, :n], src2d[kc * 128:(kc + 1) * 128, c0:c0 + n], f, True)

    load(0)
    for i, (kc, c0) in enumerate(items):
        n = min(CW, P - c0)
        if i + 1 < len(items):
            load(i + 1)
        f = ftiles.pop(i)
        b = br.next()
        copy(k, engs[i % len(engs)], b[:, :n], f[:, :n], [f], [b])
        k.dma(dst3d[:, kc, c0:c0 + n], b[:, :n], b, False)
    ph.end()


def phase_prep(k, c, xsrc, gvec, T, hT=None, yout=None, hTr=None):
    ph = Phase(k, "pp")
    gb = ph.sb([128, D], F32)
    k.dma(gb[:, :], gvec.partition_broadcast(128), gb, True)
    ident = ph.sb([128, 128], BF16)
    k.dma(ident[:, :], c.ident[:, :], ident, True)
    xr = ph.ring(3, [128, D], F32)
    ssr = ph.ring(3, [128, 4], F32)
    epsb = ph.sb([128, 1], F32)
    k.op("pool", lambda e: e.memset(epsb[:, :], EPS), [], [epsb])
    if hT is not None:
        hbr = ph.ring(2, [128, D], BF16)
        hts = ph.ring(2, [128, KC, 512], BF16)
        tpr = ph.ring(4, [128, 4, 128], BF16, psum=True)
        if hTr is not None:
            jrev = ph.sb([128, 128], BF16)
            k.dma(jrev[:, :], c.jrev[:, :], jrev, True)
            htrs = ph.ring(2, [128, KC, 512], BF16)
            rpr = ph.ring(3, [128, 4, 128], F32, psum=True)
    else:
        junk = ph.sb([128, D], BF16)
        yr = ph.ring(2, [128, D], F32)
    nt = T // 128
    xt = {}

    def load(i):
        x = xr.next()
        xt[i] = x
        k.dma(x[:, :], xsrc[i * 128:(i + 1) * 128, :], x, True)

    load(0)
    ev = 0
    for i in range(nt):
        if i + 1 < nt:
            load(i + 1)
        x = xt.pop(i)
        ss = ssr.next()
        if hT is not None:
            hb = hbr.next()
            sq = hb
        else:
            sq = junk
        act(k, sq[:, :], x[:, :], AF.Square, [x], [sq, ss], accum=ss[:, 0:1])
        act(k, ss[:, 1:2], ss[:, 0:1], AF.Ln, [ss, epsb], [ss], bias=epsb[:, 0:1], scale=1.0 / D)
        act(k, ss[:, 2:3], ss[:, 1:2], AF.Exp, [ss], [ss], scale=-0.5)
        if hT is not None:
            stt(k, hb[:, :], x[:, :], ss[:, 2:3], gb[:, :], ALU.mult, ALU.mult, [x, ss, gb], [hb])
            j = i % 4
            if j == 0:
                ht = hts.next()
            for q in range(4):
                tp = tpr.next()
                for cc in range(4):
                    tr(k, tp[:, cc, :], hb[:, (4 * q + cc) * 128:(4 * q + cc + 1) * 128], ident[:, :],
                       [hb, ident], [tp])
                copy(k, "act" if ev % 2 == 0 else "dve", ht[:, 4 * q:4 * q + 4, j * 128:(j + 1) * 128],
                     tp[:, :, :], [tp], [ht])
                ev += 1
            if hTr is not None:
                if j == 0:
                    htr = htrs.next()
                for q in range(4):
                    rp = rpr.next()
                    for cc in range(4):
                        mm(k, rp[:, cc, :], hb[:, (4 * q + cc) * 128:(4 * q + cc + 1) * 128], jrev[:, :], True, True,
                           [hb, jrev], [rp])
                    copy(k, "act" if ev % 2 == 0 else "dve", htr[:, 4 * q:4 * q + 4, (3 - j) * 128:(4 - j) * 128],
                         rp[:, :, :], [rp], [htr])
                    ev += 1
            if j == 3:
                g4 = i // 4
                k.dma(hT[:, :, g4 * 512:(g4 + 1) * 512], ht[:, :, :], ht, False)
                if hTr is not None:
                    gr = nt // 4 - 1 - g4
                    k.dma(hTr[:, :, gr * 512:(gr + 1) * 512], htr[:, :, :], htr, False)
        else:
            y = yr.next()
            stt(k, y[:, :], x[:, :], ss[:, 2:3], gb[:, :], ALU.mult, ALU.mult, [x, ss, gb], [y])
            k.dma(yout[i * 128:(i + 1) * 128, :], y[:, :], y, False)
    ph.end()


def phase_proj(k, c, hT, wbf, jobs, T):
    ph = Phase(k, "pj")
    TT = 1024
    hts = ph.sb([128, KC, TT], BF16)
    wr = ph.ring(2, [128, KC, 512], BF16)
    pr = ph.ring(6, [128, 512], F32, psum=True)
    ofr = ph.ring(3, [128, 512], F32)
    obr = ph.ring(4, [128, 512], BF16)
    items = []
    for (mode, c0, ncols, dst, kind) in jobs:
        for blk in range(ncols // 512):
            items.append((mode, c0 + blk * 512, blk, dst, kind))
    ev = [0]

    def evac(ps, kind):
        if kind == "f32":
            o = ofr.next()
        else:
            o = obr.next()
        if kind == "silu":
            act(k, o[:, :], ps[:, :], AF.Silu, [ps], [o])
        else:
            copy(k, "act" if ev[0] % 2 == 0 else "dve", o[:, :], ps[:, :], [ps], [o])
            ev[0] += 1
        return o

    for st in range(T // TT):
        k.dma(hts[:, :, :], hT[:, :, st * TT:(st + 1) * TT], hts, True)
        wt = {}

        def loadw(i):
            w = wr.next()
            wt[i] = w
            k.dma(w[:, :, :], wbf[:, :, items[i][1]:items[i][1] + 512], w, True)

        loadw(0)
        for i, (mode, cc, blk, dst, kind) in enumerate(items):
            if i + 1 < len(items):
                loadw(i + 1)
            w = wt.pop(i)
            if mode == "FM":
                for j in range(4):
                    for tc in range(TT // 512):
                        ps = pr.next()
                        for kc in range(KC):
                            mm(k, ps[:, :], w[:, kc, j * 128:(j + 1) * 128], hts[:, kc, tc * 512:(tc + 1) * 512],
                               kc == 0, kc == KC - 1, [w, hts], [ps])
                        o = evac(ps, kind)
                        t0 = st * TT + tc * 512
                        k.dma(dst[blk * 4 + j, :, t0:t0 + 512], o[:, :], o, False)
            else:
                for sub in range(TT // 128):
                    ps = pr.next()
                    for kc in range(KC):
                        mm(k, ps[:, :], hts[:, kc, sub * 128:(sub + 1) * 128], w[:, kc, :],
                           kc == 0, kc == KC - 1, [w, hts], [ps])
                    o = evac(ps, kind)
                    t0 = st * TT + sub * 128
                    k.dma(dst[t0:t0 + 128, blk * 512:(blk + 1) * 512], o[:, :], o, False)
    ph.end()


def phase_memkv(k, c, memsrc, gvec, wkv, kmT, vmx):
    ph = Phase(k, "mk")
    gb = ph.sb([128, D], F32)
    k.dma(gb[:, :], gvec.partition_broadcast(128), gb, True)
    ident = ph.sb([128, 128], BF16)
    k.dma(ident[:, :], c.ident[:, :], ident, True)
    epsb = ph.sb([128, 1], F32)
    k.op("pool", lambda e: e.memset(epsb[:, :], EPS), [], [epsb])
    w = ph.sb([128, KC, 2048], BF16)
    k.dma(w[:, :, :], wkv[:, :, :], w, True)
    hmT = ph.sb([128, KC, NMEM], BF16)
    xr = ph.ring(2, [128, D], F32)
    hbr = ph.ring(2, [128, D], BF16)
    ssr = ph.ring(2, [128, 4], F32)
    tpr = ph.ring(2, [128, 4, 128], BF16, psum=True)
    pr = ph.ring(2, [128, 512], F32, psum=True)
    for i in range(2):
        x = xr.next()
        k.dma(x[:, :], memsrc[i * 128:(i + 1) * 128, :], x, True)
        ss = ssr.next()
        hb = hbr.next()
        act(k, hb[:, :], x[:, :], AF.Square, [x], [hb, ss], accum=ss[:, 0:1])
        act(k, ss[:, 1:2], ss[:, 0:1], AF.Ln, [ss, epsb], [ss], bias=epsb[:, 0:1], scale=1.0 / D)
        act(k, ss[:, 2:3], ss[:, 1:2], AF.Exp, [ss], [ss], scale=-0.5)
        stt(k, hb[:, :], x[:, :], ss[:, 2:3], gb[:, :], ALU.mult, ALU.mult, [x, ss, gb], [hb])
        for q in range(4):
            tp = tpr.next()
            for cc in range(4):
                tr(k, tp[:, cc, :], hb[:, (4 * q + cc) * 128:(4 * q + cc + 1) * 128], ident[:, :], [hb, ident], [tp])
            copy(k, "dve", hmT[:, 4 * q:4 * q + 4, i * 128:(i + 1) * 128], tp[:, :, :], [tp], [hmT])
    kms = ph.sb([128, 8, NMEM], BF16)
    for j in range(8):
        ps = pr.next()
        for kc in range(KC):
            mm(k, ps[:, 0:NMEM], w[:, kc, j * 128:(j + 1) * 128], hmT[:, kc, :], kc == 0, kc == KC - 1, [w, hmT], [ps])
        copy(k, "act", kms[:, j, :], ps[:, 0:NMEM], [ps], [kms])
    k.dma(kmT[:, :, :], kms[:, :, :], kms, False)
    vms = ph.sb([128, 2, 4, 257], BF16)
    k.op("pool", lambda e: e.memset(vms[:, :, :, :], 1.0), [], [vms])
    for mt in range(2):
        for cb in range(2):
            ps = pr.next()
            for kc in range(KC):
                mm(k, ps[:, :], hmT[:, kc, mt * 128:(mt + 1) * 128], w[:, kc, 1024 + cb * 512:1024 + (cb + 1) * 512],
                   kc == 0, kc == KC - 1, [w, hmT], [ps])
            copy(k, "dve", vms[:, mt, 2 * cb:2 * cb + 2, 0:256], ps[:, :].rearrange("p (h d) -> p h d", h=2), [ps], [vms])
    k.dma(vmx[:, :, :, :], vms[:, :, :, :], vms, False)
    ph.end()


def bc_mid(ap2d, n):
    a = ap2d.ap
    return bass.AP(ap2d.tensor, ap2d.offset, [list(a[0]), [0, n], list(a[1])])


def phase_mematt(k, c, qmT, kmT, vmx, sg, om, T):
    ph = Phase(k, "ma")
    kms = ph.sb([128, 8, NMEM], BF16)
    k.dma(kms[:, :, :], kmT[:, :, :], kms, True)
    vms = ph.sb([128, 2, 4, 257], BF16)
    k.dma(vms[:, :, :, :], vmx[:, :, :, :], vms, True)
    qr = ph.ring(2, [128, 8, 512], BF16)
    sgr = ph.ring(2, [128, 4, 1024], BF16)
    ptr = ph.ring(2, [128, 2, 512], BF16)
    pss = ph.ring(3, [128, 512], F32, psum=True)
    pso = ph.ring(3, [128, 512], F32, psum=True)
    rdr = ph.ring(4, [128, 2], F32)
    otr = ph.ring(8, [128, 1024], BF16)
    n5 = T // 512
    tl = {}

    def load(i):
        q = qr.next()
        g = sgr.next()
        tl[i] = (q, g)
        k.dma(q[:, :, :], qmT[:, :, i * 512:(i + 1) * 512].rearrange("j p t -> p j t"), q, True)
        k.dma(g[:, :, :], sg[i * 512:(i + 1) * 512, 3072:4096].rearrange("(s p) d -> p s d", p=128), g, True)

    load(0)
    for i in range(n5):
        if i + 1 < n5:
            load(i + 1)
        q, g = tl.pop(i)
        ots = [otr.next() for _ in range(4)]
        for hm in range(4):
            pt = ptr.next()
            for mt in range(2):
                ps = pss.next()
                for dc in range(2):
                    mm(k, ps[:, :], kms[:, 2 * hm + dc, mt * 128:(mt + 1) * 128], q[:, 2 * hm + dc, :],
                       dc == 0, dc == 1, [kms, q], [ps])
                act(k, pt[:, mt, :], ps[:, :], AF.Exp, [ps], [pt], scale=1.0 / 16.0)
            for sub in range(4):
                po = pso.next()
                for mt in range(2):
                    mm(k, po[:, 0:257], pt[:, mt, sub * 128:(sub + 1) * 128], vms[:, mt, hm, :],
                       mt == 0, mt == 1, [pt, vms], [po])
                rd = rdr.next()
                k.op("dve", lambda e, rd=rd, po=po: e.reciprocal(out=rd[:, 0:1], in_=po[:, 256:257]), [po], [rd])
                stt(k, ots[sub][:, hm * 256:(hm + 1) * 256], po[:, 0:256], rd[:, 0:1],
                    g[:, sub, hm * 256:(hm + 1) * 256], ALU.mult, ALU.mult, [po, rd, g], [ots[sub]])
        for sub in range(4):
            t0 = i * 512 + sub * 128
            k.dma(om[t0:t0 + 128, :], ots[sub][:, :], ots[sub], False)
    ph.end()


def phase_chain(k, c, zsrc, qoff, foff, lbD, cbD, lcol0, qtD, ktD, khD, decD, T):
    ph = Phase(k, "ch")
    lbs = ph.sb([128, 48], F32)
    k.dma(lbs[:, :], lbD[:, :], lbs, True)
    cbs = ph.sb([128, 48], F32)
    k.dma(cbs[:, :], cbD[:, :], cbs, True)
    one = ph.sb([128, 64], F32)
    k.dma(one[:, :], c.ones[:, :], one, True)
    msk = ph.sb([128, 512], F32)
    k.dma(msk[:, :], c.cmask[:, :], msk, True)
    ident = ph.sb([128, 128], BF16)
    k.dma(ident[:, :], c.ident[:, :], ident, True)
    zr = ph.ring(6, [128, 512], F32)
    names = ("eq", "Lq", "e", "L1", "L2", "lf", "G", "t", "u", "w", "v1", "E")
    R = {n: ph.ring(2, [128, 512], F32) for n in names}
    qtr = ph.ring(2, [128, 512], BF16)
    ktr = ph.ring(2, [128, 512], BF16)
    khr = ph.ring(2, [128, 512], BF16)
    khs = ph.ring(2, [128, 4, 128], BF16)
    tpr = ph.ring(2, [128, 4, 128], BF16, psum=True)
    nch = T // 32
    decs = ph.sb([128, NH, nch], F32)
    items = [(h, tc) for h in range(NH) for tc in range(T // 512)]
    zt = {}
    tmp = {}
    sc = float(128 ** -0.5)
    o1 = one[:, 0:1]

    def load(i):
        h, tc = items[i]
        zq = zr.next()
        zf = zr.next()
        zt[i] = (zq, zf)
        k.dma(zq[:, :], zsrc[qoff + h, :, tc * 512:(tc + 1) * 512], zq, True)
        k.dma(zf[:, :], zsrc[foff + h, :, tc * 512:(tc + 1) * 512], zf, True)

    def stage1(i):
        h, tc = items[i]
        zq, zf = zt[i]
        col = lcol0 + h
        t = {n: R[n].next() for n in names}
        tmp[i] = t
        act(k, t["eq"][:, :], zq[:, :], AF.Exp, [zq], [t["eq"]], scale=-1.0)
        act(k, t["e"][:, :], zf[:, :], AF.Exp, [zf], [t["e"]], scale=-1.0)
        act(k, t["L1"][:, :], t["e"][:, :], AF.Ln, [t["e"], one], [t["L1"]], bias=o1)
        act(k, t["L2"][:, :], t["e"][:, :], AF.Ln, [t["e"], one, lbs], [t["L2"]], bias=o1, scale=lbs[:, col:col + 1])
        act(k, t["Lq"][:, :], t["eq"][:, :], AF.Ln, [t["eq"], one], [t["Lq"]], bias=o1)
        tt(k, "dve", t["lf"][:, :], t["L2"][:, :], t["L1"][:, :], ALU.subtract, [t["L2"], t["L1"]], [t["lf"]])
        G = t["G"]
        k.op("dve", lambda e, G=G, lf=t["lf"]: e.tensor_tensor_scan(
            out=G[:, :], data0=msk[:, :], data1=lf[:, :], initial=0.0, op0=ALU.mult, op1=ALU.add), [msk, t["lf"]], [G])
        tt(k, "pool", t["t"][:, :], zf[:, :], t["L1"][:, :], ALU.add, [zf, t["L1"]], [t["t"]])
        tt(k, "dve", t["u"][:, :], t["t"][:, :], G[:, :], ALU.add, [t["t"], G], [t["u"]])
        tt(k, "pool", t["v1"][:, :], G[:, :], t["Lq"][:, :], ALU.subtract, [G, t["Lq"]], [t["v1"]])

    def stage2(i):
        h, tc = items[i]
        zq, zf = zt.pop(i)
        col = lcol0 + h
        t = tmp.pop(i)
        G = t["G"]
        kt = ktr.next()
        act(k, kt[:, :], t["u"][:, :], AF.Exp, [t["u"], cbs], [kt], bias=cbs[:, col:col + 1], scale=-1.0)
        act(k, t["E"][:, :], t["v1"][:, :], AF.Exp, [t["v1"]], [t["E"]])
        G3 = G[:, :].rearrange("p (c j) -> p c j", j=32)
        glast = G3[:, :, 31:32]
        ga = glast.ap
        glb = bass.AP(glast.tensor, glast.offset, [list(ga[0]), list(ga[1]), [0, 32]])
        tt(k, "dve", t["w"][:, :].rearrange("p (c j) -> p c j", j=32), t["u"][:, :].rearrange("p (c j) -> p c j", j=32),
           glb, ALU.subtract, [t["u"], G], [t["w"]])
        kh = khr.next()
        act(k, kh[:, :], t["w"][:, :], AF.Exp, [t["w"], cbs], [kh], bias=cbs[:, col:col + 1], scale=-1.0)
        qt = qtr.next()
        stt(k, qt[:, :], zq[:, :], sc, t["E"][:, :], ALU.mult, ALU.mult, [zq, t["E"]], [qt])
        copy(k, "pool", decs[:, h, tc * 16:(tc + 1) * 16], G3[:, :, 31], [G], [decs])
        tp = tpr.next()
        for jj in range(4):
            tr(k, tp[:, jj, :], kh[:, jj * 128:(jj + 1) * 128], ident[:, :], [kh, ident], [tp])
        ks = khs.next()
        copy(k, "dve", ks[:, :, :], tp[:, :, :], [tp], [ks])
        t0 = tc * 512
        k.dma(khD[t0:t0 + 512, h * 128:(h + 1) * 128].rearrange("(j p) d -> p j d", p=128), ks[:, :, :], ks, False)
        k.dma(qtD[h, :, t0:t0 + 512], qt[:, :], qt, False)
        k.dma(ktD[h, :, t0:t0 + 512], kt[:, :], kt, False)

    n = len(items)
    load(0)
    if n > 1:
        load(1)
    stage1(0)
    for i in range(n):
        if i + 2 < n:
            load(i + 2)
        if i + 1 < n:
            stage1(i + 1)
        stage2(i)
    k.dma(decD[:, :, :], decs[:, :, :], decs, False)
    ph.end()


def phase_mix(k, c, qtD, ktD, khD, vD, decD, oD, T):
    ph = Phase(k, "mx")
    msk = ph.sb([128, 128], F32)
    k.dma(msk[:, :], c.trimask[:, :], msk, True)
    nch = T // 32
    SEG = 512
    NB = SEG // 128
    NCH = SEG // 32
    nseg = T // SEG
    HP = NH // 2
    decr = ph.ring(4, [128, nch], F32)
    dexr = ph.ring(4, [128, NCH + 1], F32)
    qr = ph.ring(4, [128, SEG], BF16)
    kr = ph.ring(4, [128, SEG], BF16)
    hr = ph.ring(4, [128, NB, 128], BF16)
    vr = ph.ring(4, [128, NB, 128], BF16)
    dsx = ph.ring(4, [128, NCH + 1, 128], F32)
    sal = ph.ring(4, [128, NCH + 1, 128], F32)
    sbr = ph.ring(4, [128, NCH, 128], BF16)
    atr = ph.ring(4, [128, 128], BF16)
    osr = ph.ring(4, [128, 128], F32)
    psd = [ph.ps([128, 512], F32) for _ in range(4)]
    par = ph.ring(2, [128, 512], F32, psum=True)
    por = ph.ring(2, [128, 512], F32, psum=True)
    items = [(p, sg_) for p in range(HP) for sg_ in range(nseg)]
    st = {}
    lane_dec = {}

    def prep(i):
        p, sg_ = items[i]
        t0 = sg_ * SEG
        lanes = []
        for ln in range(2):
            h = p + ln * HP
            q = qr.next(); kk = kr.next(); hh = hr.next(); vv = vr.next()
            k.dma(q[:, :], qtD[h, :, t0:t0 + SEG], q, True)
            k.dma(kk[:, :], ktD[h, :, t0:t0 + SEG], kk, True)
            k.dma(hh[:, :, :], khD[t0:t0 + SEG, h * 128:(h + 1) * 128].rearrange("(b p) d -> p b d", p=128), hh, True)
            k.dma(vv[:, :, :], vD[t0:t0 + SEG, h * 128:(h + 1) * 128].rearrange("(b p) d -> p b d", p=128), vv, True)
            if sg_ == 0:
                dec = decr.next()
                k.dma(dec[:, :], decD[:, h, :], dec, True)
                lane_dec[(p, ln)] = dec
            dec = lane_dec[(p, ln)]
            dx = dexr.next()
            act(k, dx[:, 1:NCH + 1], dec[:, sg_ * NCH:(sg_ + 1) * NCH], AF.Exp, [dec], [dx])
            ds = dsx.next()
            ds4 = ds[:, 1:NCH + 1, :].rearrange("p (b c) d -> p b c d", c=4)
            for b in range(NB):
                for c4 in range(4):
                    kw = {"tile_position": (96, 0)} if c4 == 3 else {}
                    mm(k, psd[c4][:, b * 128:(b + 1) * 128], hh[32 * c4:32 * c4 + 32, b, :],
                       vv[32 * c4:32 * c4 + 32, b, :], True, True, [hh, vv], [psd[c4]], **kw)
            for c4 in range(4):
                copy(k, "act", ds4[:, :, c4, :], psd[c4][:, :].rearrange("p (b d) -> p b d", d=128), [psd[c4]], [ds])
            lanes.append((h, q, kk, hh, vv, dx, ds))
        st[i] = lanes

    prevS = {}
    prep(0)
    for i, (p, sg_) in enumerate(items):
        if i + 1 < len(items):
            prep(i + 1)
        lanes = st.pop(i)
        Ss = []
        for ln in range(2):
            S = sal.next()
            if sg_ == 0:
                k.op("pool", lambda e, S=S: e.memset(S[:, 0, :], 0.0), [], [S])
            else:
                copy(k, "pool", S[:, 0, :], prevS[ln][:, NCH, :], [prevS[ln]], [S])
            Ss.append(S)
        for ci in range(NCH):
            for ln in range(2):
                S = Ss[ln]
                dx, ds = lanes[ln][5], lanes[ln][6]
                stt(k, S[:, ci + 1, :], S[:, ci, :], dx[:, ci + 1:ci + 2], ds[:, ci + 1, :], ALU.mult, ALU.add,
                    [S, dx, ds], [S])
        Sbs = []
        for ln in range(2):
            Sb = sbr.next()
            copy(k, "act", Sb[:, :, :], Ss[ln][:, 0:NCH, :], [Ss[ln]], [Sb])
            Sbs.append(Sb)
            prevS[ln] = Ss[ln]
        for blk in range(NB):
            pas = []
            for ln in range(2):
                h, q, kk, hh, vv, dx, ds = lanes[ln]
                pa = par.next()
                mm(k, pa[:, 0:128], kk[:, blk * 128:(blk + 1) * 128], q[:, blk * 128:(blk + 1) * 128], True, True,
                   [kk, q], [pa])
                at = atr.next()
                tt(k, "dve", at[:, :], pa[:, 0:128], msk[:, :], ALU.mult, [pa, msk], [at])
                pas.append(at)
            for ln in range(2):
                h, q, kk, hh, vv, dx, ds = lanes[ln]
                at = pas[ln]
                po = por.next()
                mm(k, po[:, 0:128], at[:, :], vv[:, blk, :], True, False, [at, vv], [po])
                for c4 in range(4):
                    kw = {"tile_position": (0, 96)} if c4 == 3 else {}
                    mm(k, po[32 * c4:32 * c4 + 32, 0:128], q[:, blk * 128 + 32 * c4:blk * 128 + 32 * c4 + 32],
                       Sbs[ln][:, blk * 4 + c4, :], False, c4 == 3, [q, Sbs[ln]], [po], **kw)
                os_ = osr.next()
                copy(k, "act", os_[:, :], po[:, 0:128], [po], [os_])
                t0 = sg_ * SEG + blk * 128
                k.dma(oD[t0:t0 + 128, h * 128:(h + 1) * 128], os_[:, :], os_, False)
    ph.end()


def phase_outn(k, c, oF, oBr, sg, ongv, ob, T):
    ph = Phase(k, "on")
    jf = ph.sb([128, 128], F32)
    k.dma(jf[:, :], c.jrevf[:, :], jf, True)
    ong = ph.sb([128, TOK_W], F32)
    k.dma(ong[:, :], ongv.partition_broadcast(128), ong, True)
    epsb = ph.sb([128, 1], F32)
    k.op("pool", lambda e: e.memset(epsb[:, :], EPS), [], [epsb])
    ofr = ph.ring(2, [128, TOK_W], F32)
    obr = ph.ring(2, [128, TOK_W], F32)
    sgr = ph.ring(2, [128, TOK_W], BF16)
    sqr = ph.ring(1, [128, TOK_W], F32)
    gsr = ph.ring(1, [128, TOK_W], F32)
    outr = ph.ring(2, [128, TOK_W], BF16)
    ssr = ph.ring(2, [128, 3, NH], F32)
    pr = ph.ring(6, [128, 512], F32, psum=True)
    nt = T // 128
    tl = {}

    def load(i):
        a = ofr.next(); b = obr.next(); g = sgr.next()
        tl[i] = (a, b, g)
        ir = nt - 1 - i
        k.dma(a[:, :], oF[i * 128:(i + 1) * 128, :], a, True)
        k.dma(b[:, :], oBr[ir * 128:(ir + 1) * 128, :], b, True)
        k.dma(g[:, :], sg[i * 128:(i + 1) * 128, 0:TOK_W], g, True)

    load(0)
    for i in range(nt):
        if i + 1 < nt:
            load(i + 1)
        a, b, g = tl.pop(i)
        for cb in range(6):
            ps = pr.next()
            mm(k, ps[:, :], jf[:, :], b[:, cb * 512:(cb + 1) * 512], True, True, [jf, b], [ps])
            tt(k, "dve", a[:, cb * 512:(cb + 1) * 512], a[:, cb * 512:(cb + 1) * 512], ps[:, :], ALU.add, [a, ps], [a])
        sq = sqr.next()
        tt(k, "pool", sq[:, :], a[:, :], a[:, :], ALU.mult, [a], [sq])
        ss = ssr.next()
        k.op("dve", lambda e, ss=ss, sq=sq: e.tensor_reduce(
            out=ss[:, 0, :], in_=sq[:, :].rearrange("p (h d) -> p h d", d=128), axis=AX.X, op=ALU.add), [sq], [ss])
        act(k, ss[:, 1, :], ss[:, 0, :], AF.Ln, [ss, epsb], [ss], bias=epsb[:, 0:1], scale=1.0 / 128)
        act(k, ss[:, 2, :], ss[:, 1, :], AF.Exp, [ss], [ss], scale=-0.5)
        gs = gsr.next()
        tt(k, "pool", gs[:, :], g[:, :], ong[:, :], ALU.mult, [g, ong], [gs])
        r2 = ss[:, 2, :]
        ra = r2.ap
        rb = bass.AP(r2.tensor, r2.offset, [list(ra[0]), list(ra[1]), [0, 128]])
        tt(k, "dve", sq[:, :].rearrange("p (h d) -> p h d", d=128), a[:, :].rearrange("p (h d) -> p h d", d=128), rb,
           ALU.mult, [a, ss], [sq])
        o = outr.next()
        tt(k, "dve", o[:, :], sq[:, :], gs[:, :], ALU.mult, [sq, gs], [o])
        k.dma(ob[i * 128:(i + 1) * 128, :], o[:, :], o, False)
    ph.end()


def phase_outw(k, c, ob, om, wo, xsrc, xdst, T):
    ph = Phase(k, "ow")
    ident = ph.sb([128, 128], BF16)
    k.dma(ident[:, :], c.ident[:, :], ident, True)
    TTo = 512
    obr = ph.ring(2, [128, MIX_W], BF16)
    otr = ph.ring(2, [128, 32, TTo], BF16)
    wr = ph.ring(2, [128, 32, 512], BF16)
    xpr = ph.ring(3, [128, 512], F32)
    xor_ = ph.ring(3, [128, 512], F32)
    tpr = ph.ring(3, [128, 4, 128], BF16, psum=True)
    pr = ph.ring(4, [128, 512], F32, psum=True)
    nst = T // TTo
    ev = 0
    items = [(st, db) for st in range(nst) for db in range(4)]
    wt = {}

    def loadw(i):
        w = wr.next()
        wt[i] = w
        db = items[i][1]
        k.dma(w[:, :, :], wo[:, :, db * 512:(db + 1) * 512], w, True)

    loadw(0)
    oT = None
    for i, (st, db) in enumerate(items):
        if db == 0:
            oT = otr.next()
            for sub in range(4):
                t0 = st * TTo + sub * 128
                o = obr.next()
                k.dma(o[:, 0:TOK_W], ob[t0:t0 + 128, :], o, True)
                k.dma(o[:, TOK_W:MIX_W], om[t0:t0 + 128, :], o, True)
                for q in range(8):
                    tp = tpr.next()
                    for cc in range(4):
                        tr(k, tp[:, cc, :], o[:, (4 * q + cc) * 128:(4 * q + cc + 1) * 128], ident[:, :], [o, ident], [tp])
                    copy(k, "act" if ev % 2 == 0 else "dve", oT[:, 4 * q:4 * q + 4, sub * 128:(sub + 1) * 128],
                         tp[:, :, :], [tp], [oT])
                    ev += 1
        if i + 1 < len(items):
            loadw(i + 1)
        w = wt.pop(i)
        for sub in range(4):
            t0 = st * TTo + sub * 128
            xp = xpr.next()
            k.dma(xp[:, :], xsrc[t0:t0 + 128, db * 512:(db + 1) * 512], xp, True)
            ps = pr.next()
            for kc in range(32):
                mm(k, ps[:, :], oT[:, kc, sub * 128:(sub + 1) * 128], w[:, kc, :], kc == 0, kc == 31, [oT, w], [ps])
            xo = xor_.next()
            tt(k, "dve", xo[:, :], ps[:, :], xp[:, :], ALU.add, [ps, xp], [xo])
            k.dma(xdst[t0:t0 + 128, db * 512:(db + 1) * 512], xo[:, :], xo, False)
    ph.end()


def na_ranges(rows):
    rs = lambda r: min(max(r - 4, 0), rows - 8)
    out = []
    for j in range(rows):
        al = [r for r in range(rows) if rs(r) <= j <= rs(r) + 7]
        assert al == list(range(al[0], al[-1] + 1))
        out.append((al[0], al[-1]))
    return out


def phase_na(k, c, fm16, vD, sg, btab, ob, T):
    ph = Phase(k, "na")
    rows = T // GW
    nt = T // 128
    rng_ = na_ranges(rows)
    cmk = ph.sb([128, GW], F32)
    k.dma(cmk[0:64, :], c.colmask[:, :], cmk, True)
    k.dma(cmk[64:128, :], c.colmask[:, :], cmk, True)
    onesb = ph.sb([128, 64], F32)
    k.dma(onesb[:, :], c.ones[:, :], onesb, True)
    qr = ph.ring(2, [128, T], BF16)
    kr = ph.ring(2, [128, T], BF16)
    vr = ph.ring(2, [128, nt, 129], BF16)
    for v in vr.tiles:
        copy(k, "dve", v[:, :, 128:129], onesb[:, 0:nt].rearrange("p (a b) -> p a b", b=1), [onesb], [v])
    sgr = ph.ring(2, [128, nt, 128], BF16)
    obr = ph.ring(2, [128, nt, 128], BF16)
    crf = ph.ring(2, [128, 15, GW], F32)
    crb = ph.ring(2, [128, 15, GW], BF16)
    exr = ph.ring(2, [128, 768], BF16)
    ptr = ph.ring(8, [128, 768], BF16)
    pss = ph.ring(2, [128, 1024], F32, psum=True)
    pso = ph.ring(2, [128, 512], F32, psum=True)
    rdr = ph.ring(4, [128, 2], F32)
    info = []
    for kt in range(nt):
        (a0, b0), (a1, b1) = rng_[2 * kt], rng_[2 * kt + 1]
        qlo = (min(a0, a1) // 2) * 2
        qhi = (max(b0, b1) // 2) * 2 + 1
        info.append((qlo, qhi, (a0, b0), (a1, b1)))
    kts_of = {}
    for kt in range(nt):
        qlo, qhi = info[kt][0], info[kt][1]
        for R in range(qlo // 2, qhi // 2 + 1):
            kts_of.setdefault(R, []).append(kt)
    for R, l in kts_of.items():
        assert l[-1] - l[0] <= 6, (R, l)
    sc = float(128 ** -0.5)
    tl = {}

    def load(h):
        q = qr.next(); kk = kr.next(); v = vr.next(); g = sgr.next(); cf = crf.next()
        tl[h] = (q, kk, v, g, cf)
        k.dma(q[:, :], fm16[h, :, :], q, True)
        k.dma(kk[:, :], fm16[NH + h, :, :], kk, True)
        k.dma(v[:, :, 0:128], vD[:, h * 128:(h + 1) * 128].rearrange("(b p) d -> p b d", p=128), v, True)
        k.dma(g[:, :, :], sg[:, h * 128:(h + 1) * 128].rearrange("(b p) d -> p b d", p=128), g, True)
        k.dma(cf[0:64, :, :], btab[h], cf, True)
        k.dma(cf[64:128, :, :], btab[h], cf, True)

    def finalize(kt_done, pts, v, g, obh):
        for R in sorted(kts_of):
            l = kts_of[R]
            if l[-1] != kt_done:
                continue
            po = pso.next()
            for n_, k2 in enumerate(l):
                off = (R - info[k2][0] // 2) * 128
                mm(k, po[:, 0:129], pts[k2][:, off:off + 128], v[:, k2, :], n_ == 0, n_ == len(l) - 1,
                   [pts[k2], v], [po])
            rd = rdr.next()
            k.op("dve", lambda e, rd=rd, po=po: e.reciprocal(out=rd[:, 0:1], in_=po[:, 128:129]), [po], [rd])
            stt(k, obh[:, R, :], po[:, 0:128], rd[:, 0:1], g[:, R, :], ALU.mult, ALU.mult, [po, rd, g], [obh])

    load(0)
    for h in range(NH):
        if h + 1 < NH:
            load(h + 1)
        q, kk, v, g, cf = tl.pop(h)
        act(k, cf[:, :, :], cf[:, :, :], AF.Exp, [cf], [cf])
        cb = crb.next()
        tt(k, "dve", cb[:, :, :], cf[:, :, :], bc_mid(cmk[:, :], 15), ALU.mult, [cf, cmk], [cb])
        obh = obr.next()
        pts = {}
        for kt in range(nt):
            qlo, qhi, r0, r1 = info[kt]
            nq = (qhi - qlo + 1) * GW
            ps = pss.next()
            n1 = min(512, nq)
            mm(k, ps[:, 0:n1], kk[:, kt * 128:(kt + 1) * 128], q[:, qlo * GW:qlo * GW + n1], True, True, [kk, q], [ps])
            if nq > 512:
                mm(k, ps[:, 512:nq], kk[:, kt * 128:(kt + 1) * 128], q[:, qlo * GW + 512:qlo * GW + nq], True, True,
                   [kk, q], [ps])
            ex = exr.next()
            act(k, ex[:, 0:nq], ps[:, 0:nq], AF.Exp, [ps], [ex], scale=sc)
            pt = ptr.next()
            pts[kt] = pt
            k.op("pool", lambda e, pt=pt, nq=nq: e.memset(pt[:, 0:nq], 0.0), [], [pt])
            for hf, (rlo, rhi) in enumerate((r0, r1)):
                j = 2 * kt + hf
                c0 = (rlo - qlo) * GW
                nr = rhi - rlo + 1
                d0 = 7 - j + rlo
                assert 0 <= d0 and d0 + nr <= 15
                tt(k, "dve", pt[64 * hf:64 * hf + 64, c0:c0 + nr * GW], ex[64 * hf:64 * hf + 64, c0:c0 + nr * GW],
                   cb[64 * hf:64 * hf + 64, d0:d0 + nr, :].rearrange("p a b -> p (a b)"), ALU.mult, [ex, cb], [pt])
            if kt >= 1:
                finalize(kt - 1, pts, v, g, obh)
        finalize(nt - 1, pts, v, g, obh)
        k.dma(ob[:, h * 128:(h + 1) * 128].rearrange("(b p) d -> p b d", p=128), obh[:, :, :], obh, False)
    ph.end()


def phase_lb(k, c, lblT, lbD, cbD):
    ph = Phase(k, "lb")
    l0 = ph.sb([128, 48], F32)
    l1 = ph.sb([128, 48], F32)
    one = ph.sb([128, 64], F32)
    k.dma(l0[:, :], lblT[0], l0, True)
    k.dma(l1[:, :], lblT[1], l1, True)
    k.dma(one[:, :], c.ones[:, :], one, True)
    d = ph.sb([128, 48], F32)
    e = ph.sb([128, 48], F32)
    cb = ph.sb([128, 48], F32)
    lb = ph.sb([128, 48], F32)
    tt(k, "dve", d[:, :], l1[:, :], l0[:, :], ALU.subtract, [l0, l1], [d])
    act(k, e[:, :], d[:, :], AF.Exp, [d], [e])
    act(k, cb[:, :], e[:, :], AF.Ln, [e, one], [cb], bias=one[:, 0:1])
    ts(k, "dve", cb[:, :], cb[:, :], -1.0, None, ALU.mult, None, [cb], [cb])
    tt(k, "dve", d[:, :], d[:, :], cb[:, :], ALU.add, [d, cb], [d])
    act(k, lb[:, :], d[:, :], AF.Exp, [d], [lb])
    k.dma(lbD[:, :], lb[:, :], lb, False)
    k.dma(cbD[:, :], cb[:, :], cb, False)
    ph.end()


def build(T, NS, depth, debug=()):
    nc = bass.Bass("TRN2", target_bir_lowering=False)
    c = Cfg()
    c.T = T

    def din(name, shape, dt=F32):
        return nc.dram_tensor(name, list(shape), dt, kind="ExternalInput").ap()

    def dscr(name, shape, dt):
        kind = "ExternalOutput" if name in debug else "Internal"
        return nc.dram_tensor(name, list(shape), dt, kind=kind).ap()

    x_in = din("x_in", [NS, T, D])
    mem_in = din("mem_in", [NS, NMEM, D])
    w_in_a = din("w_in_a", [2, D, P_A])
    w_in_b = din("w_in_b", [2, D, P_B])
    w_mem_kv = din("w_mem_kv", [4, D, 2 * MEM_W])
    w_out = din("w_out", [4, MIX_W, D])
    norm_g = din("norm_g", [4, D])
    mem_norm_g = din("mem_norm_g", [4, D])
    final_g = din("final_g", [1, D])
    onorm_g = din("hg_onorm_g", [2, TOK_W])
    lblT = din("lblT", [2, 128, 48])
    btab = din("btab", [2, NH, GW, 15, GW])
    c.ident = din("ident", [128, 128], BF16)
    c.jrev = din("jrev", [128, 128], BF16)
    c.jrevf = din("jrevf", [128, 128])
    c.ones = din("ones", [128, 64])
    c.zeros = din("zeros", [128, 128])
    c.cmask = din("cmask", [128, 512])
    c.trimask = din("trimask", [128, 128])
    c.colmask = din("colmask", [GW, GW])
    y = nc.dram_tensor("y", [NS, T, D], F32, kind="ExternalOutput").ap()

    hT = dscr("hT", [128, KC, T], BF16)
    hTr = dscr("hTr", [128, KC, T], BF16)
    wbf = [dscr("wbf%d" % l, [128, KC, P_A if l % 2 == 0 else P_B], BF16) for l in range(4)]
    wkv = [dscr("wkv%d" % l, [128, KC, 2 * MEM_W], BF16) for l in range(4)]
    wo = [dscr("wo%d" % l, [128, 32, D], BF16) for l in range(4)]
    xres = dscr("xres", [NS, T, D], F32)
    zA = dscr("zA", [48, 128, T], F32)
    fm16 = dscr("fm16", [48, 128, T], BF16)
    qmT = dscr("qmT", [8, 128, T], BF16)
    vD = dscr("vD", [T, TOK_W], BF16)
    sg = dscr("sg", [T, MIX_W], BF16)
    om = dscr("om", [T, MEM_W], BF16)
    ob = dscr("ob", [T, TOK_W], BF16)
    kmT = dscr("kmT", [128, 8, NMEM], BF16)
    vmx = dscr("vmx", [128, 2, 4, 257], BF16)
    qtD = dscr("qtD", [NH, 128, T], BF16)
    ktD = dscr("ktD", [NH, 128, T], BF16)
    khD = dscr("khD", [T, TOK_W], BF16)
    decD = dscr("decD", [128, NH, T // 32], F32)
    oF = dscr("oF", [T, TOK_W], F32)
    oB = dscr("oB", [T, TOK_W], F32)
    lbD = [dscr("lbD%d" % a, [128, 48], F32) for a in range(2)]
    cbD = [dscr("cbD%d" % a, [128, 48], F32) for a in range(2)]

    with ExitStack() as es:
        k = KB(nc, es)
        for l in range(depth):
            phase_conv(k, (w_in_a if l % 2 == 0 else w_in_b)[l // 2], wbf[l], D, P_A if l % 2 == 0 else P_B)
            phase_conv(k, w_mem_kv[l], wkv[l], D, 2 * MEM_W)
            phase_conv(k, w_out[l], wo[l], MIX_W, D)
        ph = Phase(k, "z")
        zt = ph.sb([128, 48], F32)
        k.dma(zt[:, :], c.zeros[:, 0:48], zt, True)
        k.dma(lbD[0][:, :], zt[:, :], zt, False)
        k.dma(cbD[0][:, :], zt[:, :], zt, False)
        ph.end()
        phase_lb(k, c, lblT, lbD[1], cbD[1])
        for l in range(depth):
            for s in range(NS):
                xsrc = x_in[s] if l == 0 else xres[s]
                if l % 2 == 0:
                    a = l // 2
                    phase_prep(k, c, xsrc, norm_g[l:l + 1, :], T, hT=hT, hTr=hTr)
                    phase_memkv(k, c, mem_in[s], mem_norm_g[l:l + 1, :], wkv[l], kmT, vmx)
                    jobs = [("FM", 0, 3072, zA[0:24], "f32"), ("FM", 3072, 3072, zA[24:48], "f32"),
                            ("TM", 9216, 3072, vD, "bf16"),
                            ("FM", 12288, 1024, qmT, "bf16"), ("TM", 13312, 4096, sg, "silu")]
                    phase_proj(k, c, hT, wbf[l], jobs, T)
                    phase_chain(k, c, zA, 0, 24, lbD[a], cbD[a], 0, qtD, ktD, khD, decD, T)
                    phase_mix(k, c, qtD, ktD, khD, vD, decD, oF, T)
                    phase_mematt(k, c, qmT, kmT, vmx, sg, om, T)
                    jobs = [("FM", 0, 3072, zA[0:24], "f32"), ("FM", 6144, 3072, zA[24:48], "f32"),
                            ("TM", 9216, 3072, vD, "bf16")]
                    phase_proj(k, c, hTr, wbf[l], jobs, T)
                    phase_chain(k, c, zA, 0, 24, lbD[a], cbD[a], 24, qtD, ktD, khD, decD, T)
                    phase_mix(k, c, qtD, ktD, khD, vD, decD, oB, T)
                    phase_outn(k, c, oF, oB, sg, onorm_g[a:a + 1, :], ob, T)
                else:
                    phase_prep(k, c, xsrc, norm_g[l:l + 1, :], T, hT=hT)
                    phase_memkv(k, c, mem_in[s], mem_norm_g[l:l + 1, :], wkv[l], kmT, vmx)
                    jobs = [("FM", 0, 3072, fm16[0:24], "bf16"), ("FM", 3072, 3072, fm16[24:48], "bf16"),
                            ("TM", 6144, 3072, vD, "bf16"), ("FM", 9216, 1024, qmT, "bf16"),
                            ("TM", 10240, 4096, sg, "silu")]
                    phase_proj(k, c, hT, wbf[l], jobs, T)
                    phase_mematt(k, c, qmT, kmT, vmx, sg, om, T)
                    phase_na(k, c, fm16, vD, sg, btab[l // 2], ob, T)
                phase_outw(k, c, ob, om, wo[l], xsrc, xres[s], T)
        for s in range(NS):
            phase_prep(k, c, xres[s] if depth > 0 else x_in[s], final_g[0:1, :], T, yout=y[s])
    return nc, k


def _consts():
    cm = np.ones((128, 16, 32), np.float32)
    cm[:, :, 0] = 0
    s_ = np.arange(128)[:, None]
    t_ = np.arange(128)[None, :]
    tri = ((s_ // 32 == t_ // 32) & (s_ <= t_)).astype(np.float32)
    jr = np.eye(128, dtype=np.float32)[::-1].copy()
    kc = np.arange(GW)[:, None]
    qc = np.arange(GW)[None, :]
    cs = np.clip(qc - 8, 0, GW - 16)
    colmask = ((kc >= cs) & (kc < cs + 16)).astype(np.float32)
    return {"ident": np.eye(128, dtype=np.float32).astype(ml_dtypes.bfloat16), "ones": np.ones((128, 64), np.float32),
            "zeros": np.zeros((128, 128), np.float32), "cmask": cm.reshape(128, 512), "trimask": tri,
            "jrev": jr.astype(ml_dtypes.bfloat16), "jrevf": jr, "colmask": colmask}


def _layout_inputs(hg_lb_logits, na_rpb):
    lblT = np.ascontiguousarray(
        np.asarray(hg_lb_logits, np.float32).reshape(2, 2, NH, 128).transpose(0, 3, 1, 2).reshape(2, 128, 48))
    kc = np.arange(GW)[:, None]
    qc = np.arange(GW)[None, :]
    dc = np.clip(kc - qc + 15, 0, 30)
    rp = np.asarray(na_rpb, np.float32)[:, :, ::-1, :]
    bt = rp[:, :, :, dc]
    btab = np.ascontiguousarray(bt.transpose(0, 1, 3, 2, 4))
    return lblT, btab


def make_in_maps(NS, xs, mems, w_in_a, w_in_b, w_mem_kv, w_out, norm_g, mem_norm_g, hg_lb_logits, hg_onorm_g,
                 na_rpb, final_g):
    cst = _consts()
    lblT, btab = _layout_inputs(hg_lb_logits, na_rpb)
    maps = []
    for x, m_ in zip(xs, mems):
        m = {"x_in": x, "mem_in": m_, "w_in_a": w_in_a, "w_in_b": w_in_b, "w_mem_kv": w_mem_kv, "w_out": w_out,
             "norm_g": norm_g, "mem_norm_g": mem_norm_g, "final_g": np.asarray(final_g).reshape(1, D),
             "hg_onorm_g": hg_onorm_g, "lblT": lblT, "btab": btab}
        m.update(cst)
        maps.append(m)
    return maps


def kernel(x_prompt, x_sample, mem_prompt, mem_sample, w_in_a, w_in_b, w_mem_kv, w_out,
           norm_g, mem_norm_g, hg_lb_logits, hg_onorm_g, na_rpb, final_g):
    T = 4096
    NS = 2
    nc, k = build(T, NS, 4)
    xs = [np.ascontiguousarray(np.stack([x_sample[cid], x_prompt[cid % 2]])) for cid in range(8)]
    mems = [np.ascontiguousarray(np.stack([mem_sample[cid], mem_prompt[cid % 2]])) for cid in range(8)]
    in_maps = make_in_maps(NS, xs, mems, w_in_a, w_in_b, w_mem_kv, w_out, norm_g, mem_norm_g, hg_lb_logits,
                           hg_onorm_g, na_rpb, final_g)
    res = run_bass_kernel_spmd(nc, in_maps, core_ids=list(range(8)))
    ys = [r["y"] for r in res.results]
    y_sample = np.stack([ys[cid][0] for cid in range(8)])
    y_prompt = np.stack([ys[cid][1] for cid in range(2)])
    return (y_prompt.astype(np.float32), y_sample.astype(np.float32))
```

```python
import numpy as np
import ml_dtypes
from contextlib import ExitStack
import concourse.bass as bass
import concourse.mybir as mybir
from concourse.bass_utils import run_bass_kernel_spmd

F32 = mybir.dt.float32
BF16 = mybir.dt.bfloat16
AF = mybir.ActivationFunctionType
ALU = mybir.AluOpType
AX = mybir.AxisListType

D = 2048
KC = D // 128
TOK_W = 3072
NH = 24
MEM_W = 1024
MIX_W = 4096
P_A = 4 * TOK_W + MEM_W + MIX_W
P_B = 3 * TOK_W + MEM_W + MIX_W
NMEM = 256
EPS = 1e-6
GW = 64
CH = 32
CO_MODE = 3
DBG_EV = 3
DBG_S3 = 2


class SemObj:
    __slots__ = ("sem", "step", "count")

    def __init__(self, sem, step):
        self.sem = sem
        self.step = step
        self.count = 0


class Buf:
    __slots__ = ("lw", "rd", "dsem")

    def __init__(self):
        self.lw = {}
        self.rd = {}
        self.dsem = None


class Tl:
    __slots__ = ("t", "b")

    def __init__(self, t):
        self.t = t
        self.b = Buf()

    def __getitem__(self, idx):
        return self.t[idx]


class Ring:
    def __init__(self, tiles):
        self.tiles = tiles
        self.i = 0

    def next(self):
        t = self.tiles[self.i % len(self.tiles)]
        self.i += 1
        return t


class KB:
    ENG = ("sp", "pe", "act", "dve", "pool")

    def __init__(self, nc, es, n_dsem=96):
        self.nc = nc
        self.esem = {n: SemObj(es.enter_context(nc.semaphore("s_" + n)), 1)
                     for n in ("pe", "act", "dve", "pool")}
        self.dsems = [SemObj(es.enter_context(nc.semaphore("d%d" % i)), 16) for i in range(n_dsem)]
        self.dfree = list(self.dsems)
        self.ops = {n: [] for n in self.ENG}
        self.waited = {n: {} for n in self.ENG}
        self.uid = 0
        self.nops = 0

    def _deps(self, en, R, W):
        need = {}
        for b in R:
            for so, v in b.lw.items():
                if need.get(so, 0) < v:
                    need[so] = v
        for b in W:
            for so, v in b.lw.items():
                if need.get(so, 0) < v:
                    need[so] = v
            for so, v in b.rd.items():
                if need.get(so, 0) < v:
                    need[so] = v
        waits = []
        wd = self.waited[en]
        pe_so = self.esem["pe"]
        for so, v in need.items():
            if en == "pe" and so is pe_so:
                continue
            if wd.get(so, 0) < v:
                waits.append((so.sem, v))
                wd[so] = v
        return waits

    def op(self, en, fn, R=(), W=()):
        R = [x.b if isinstance(x, Tl) else x for x in R]
        W = [x.b if isinstance(x, Tl) else x for x in W]
        waits = self._deps(en, R, W)
        so = self.esem[en]
        so.count += 1
        n = so.count
        self.ops[en].append((waits, fn, so.sem, 1))
        for b in R:
            b.rd[so] = n
        for b in W:
            b.lw[so] = n
        self.nops += 1

    def dma(self, out, in_, sb, load, **kw):
        b = sb.b if isinstance(sb, Tl) else sb
        if b.dsem is None:
            b.dsem = self.dfree.pop()
        waits = self._deps("sp", [] if load else [b], [b] if load else [])
        so = b.dsem
        so.count += 16
        self.ops["sp"].append((waits, lambda e: e.dma_start(out=out, in_=in_, **kw), so.sem, 16))
        if load:
            b.lw[so] = so.count
        else:
            b.rd[so] = so.count
        self.nops += 1

    def barrier(self):
        sos = list(self.esem.values()) + [d for d in self.dsems if d.count > 0]
        for en in self.ENG:
            wd = self.waited[en]
            waits = []
            for so in sos:
                if en == "pe" and so is self.esem["pe"]:
                    continue
                if wd.get(so, 0) < so.count:
                    waits.append((so.sem, so.count))
                    wd[so] = so.count
            if waits:
                self.ops[en].append((waits, None, None, 0))

    def flush(self):
        nc = self.nc
        with nc.Block() as block:
            decos = {"sp": block.sync, "pe": block.tensor, "act": block.scalar,
                     "dve": block.vector, "pool": block.gpsimd}
            for en in self.ENG:
                ops = self.ops[en]
                if not ops:
                    continue

                def body(e, ops=ops):
                    for waits, fn, sem, inc in ops:
                        for s, v in waits:
                            e.wait_ge(s, v)
                        if fn is not None:
                            fn(e).then_inc(sem, inc)
                decos[en](body)
                self.ops[en] = []


class Phase:
    def __init__(self, k, name):
        self.k = k
        k.uid += 1
        self.name = "%s%d" % (name, k.uid)
        self.es = ExitStack()
        self.tiles = []
        self.n = 0

    def _mk(self, fn, shape, dt):
        self.n += 1
        t = self.es.enter_context(fn("%s_%d" % (self.name, self.n), list(shape), dt))
        tl = Tl(t)
        self.tiles.append(tl)
        return tl

    def sb(self, shape, dt):
        return self._mk(self.k.nc.sbuf_tensor, shape, dt)

    def ps(self, shape, dt):
        return self._mk(self.k.nc.psum_tensor, shape, dt)

    def ring(self, n, shape, dt, psum=False):
        return Ring([(self.ps if psum else self.sb)(shape, dt) for _ in range(n)])

    def end(self):
        k = self.k
        k.barrier()
        k.flush()
        for tl in self.tiles:
            if tl.b.dsem is not None:
                k.dfree.append(tl.b.dsem)
                tl.b.dsem = None
        self.es.close()


def act(k, out, in_, func, R, W, bias=None, scale=1.0, accum=None):
    kw = {}
    if bias is not None:
        kw["bias"] = bias
    if accum is not None:
        kw["accum_out"] = accum
    k.op("act", lambda e: e.activation(out=out, in_=in_, func=func, scale=scale, **kw), R, W)


def tt(k, en, out, in0, in1, op, R, W):
    k.op(en, lambda e: e.tensor_tensor(out=out, in0=in0, in1=in1, op=op), R, W)


def ts(k, en, out, in0, s1, s2, op0, op1, R, W):
    if s2 is None:
        k.op(en, lambda e: e.tensor_scalar(out=out, in0=in0, scalar1=s1, scalar2=None, op0=op0), R, W)
    else:
        k.op(en, lambda e: e.tensor_scalar(out=out, in0=in0, scalar1=s1, scalar2=s2, op0=op0, op1=op1), R, W)


def stt(k, out, in0, scalar, in1, op0, op1, R, W):
    k.op("dve", lambda e: e.scalar_tensor_tensor(out=out, in0=in0, scalar=scalar, in1=in1, op0=op0, op1=op1), R, W)


def copy(k, en, out, in_, R, W):
    if en == "act":
        k.op("act", lambda e: e.copy(out=out, in_=in_), R, W)
    else:
        k.op(en, lambda e: e.tensor_copy(out=out, in_=in_), R, W)


def mm(k, out, lhsT, rhs, start, stop, R, W, **kw):
    k.op("pe", lambda e: e.matmul(out, lhsT, rhs, start=start, stop=stop, **kw), R, W)


def tr(k, out, in_, ident, R, W):
    k.op("pe", lambda e: e.transpose(out, in_, ident), R, W)


class Cfg:
    pass


def co_run(k, name, makers):
    ph = Phase(k, name)
    gens = [m(ph) for m in makers]
    counts = [max(1, next(g)) for g in gens]
    base = max(counts)
    rate = [cn / base for cn in counts]
    acc = [0.0] * len(gens)
    alive = set(range(len(gens)))
    while alive:
        for gi in range(len(gens)):
            if gi not in alive:
                continue
            acc[gi] += rate[gi]
            while acc[gi] >= 1.0 - 1e-9 and gi in alive:
                acc[gi] -= 1.0
                try:
                    next(gens[gi])
                except StopIteration:
                    alive.discard(gi)
    ph.end()


def phase_conv(k, src2d, dst3d, K, P):
    ph = Phase(k, "cv")
    CW = 2048
    fr = ph.ring(3, [128, CW], F32)
    br = ph.ring(3, [128, CW], BF16)
    items = [(kc, c0) for kc in range(K // 128) for c0 in range(0, P, CW)]
    engs = ("dve", "act", "pool", "dve", "act")
    ftiles = {}

    def load(i):
        kc, c0 = items[i]
        n = min(CW, P - c0)
        f = fr.next()
        ftiles[i] = f
        k.dma(f[:, :n], src2d[kc * 128:(kc + 1) * 128, c0:c0 + n], f, True)

    load(0)
    for i, (kc, c0) in enumerate(items):
        n = min(CW, P - c0)
        if i + 1 < len(items):
            load(i + 1)
        f = ftiles.pop(i)
        b = br.next()
        copy(k, engs[i % len(engs)], b[:, :n], f[:, :n], [f], [b])
        k.dma(dst3d[:, kc, c0:c0 + n], b[:, :n], b, False)
    ph.end()


def phase_prep(k, c, xsrc, gvec, T, hT=None, yout=None, hTr=None):
    ph = Phase(k, "pp")
    gb = ph.sb([128, D], F32)
    k.dma(gb[:, :], gvec.partition_broadcast(128), gb, True)
    ident = ph.sb([128, 128], BF16)
    k.dma(ident[:, :], c.ident[:, :], ident, True)
    xr = ph.ring(3, [128, D], F32)
    ssr = ph.ring(3, [128, 4], F32)
    epsb = ph.sb([128, 1], F32)
    k.op("pool", lambda e: e.memset(epsb[:, :], EPS), [], [epsb])
    if hT is not None:
        hbr = ph.ring(2, [128, D], BF16)
        hts = ph.ring(2, [128, KC, 512], BF16)
        tpr = ph.ring(4, [128, 4, 128], BF16, psum=True)
        if hTr is not None:
            jrev = ph.sb([128, 128], BF16)
            k.dma(jrev[:, :], c.jrev[:, :], jrev, True)
            htrs = ph.ring(2, [128, KC, 512], BF16)
            rpr = ph.ring(3, [128, 4, 128], F32, psum=True)
    else:
        junk = ph.sb([128, D], BF16)
        yr = ph.ring(2, [128, D], F32)
    nt = T // 128
    xt = {}

    def load(i):
        x = xr.next()
        xt[i] = x
        k.dma(x[:, :], xsrc[i * 128:(i + 1) * 128, :], x, True)

    load(0)
    ev = 0
    for i in range(nt):
        if i + 1 < nt:
            load(i + 1)
        x = xt.pop(i)
        ss = ssr.next()
        if hT is not None:
            hb = hbr.next()
            sq = hb
        else:
            sq = junk
        act(k, sq[:, :], x[:, :], AF.Square, [x], [sq, ss], accum=ss[:, 0:1])
        act(k, ss[:, 1:2], ss[:, 0:1], AF.Ln, [ss, epsb], [ss], bias=epsb[:, 0:1], scale=1.0 / D)
        act(k, ss[:, 2:3], ss[:, 1:2], AF.Exp, [ss], [ss], scale=-0.5)
        if hT is not None:
            stt(k, hb[:, :], x[:, :], ss[:, 2:3], gb[:, :], ALU.mult, ALU.mult, [x, ss, gb], [hb])
            j = i % 4
            if j == 0:
                ht = hts.next()
            for q in range(4):
                tp = tpr.next()
                for cc in range(4):
                    tr(k, tp[:, cc, :], hb[:, (4 * q + cc) * 128:(4 * q + cc + 1) * 128], ident[:, :],
                       [hb, ident], [tp])
                copy(k, "act" if ev % 2 == 0 else "dve", ht[:, 4 * q:4 * q + 4, j * 128:(j + 1) * 128],
                     tp[:, :, :], [tp], [ht])
                ev += 1
            if hTr is not None:
                if j == 0:
                    htr = htrs.next()
                for q in range(4):
                    rp = rpr.next()
                    for cc in range(4):
                        mm(k, rp[:, cc, :], hb[:, (4 * q + cc) * 128:(4 * q + cc + 1) * 128], jrev[:, :], True, True,
                           [hb, jrev], [rp])
                    copy(k, "act" if ev % 2 == 0 else "dve", htr[:, 4 * q:4 * q + 4, (3 - j) * 128:(4 - j) * 128],
                         rp[:, :, :], [rp], [htr])
                    ev += 1
            if j == 3:
                g4 = i // 4
                k.dma(hT[:, :, g4 * 512:(g4 + 1) * 512], ht[:, :, :], ht, False)
                if hTr is not None:
                    gr = nt // 4 - 1 - g4
                    k.dma(hTr[:, :, gr * 512:(gr + 1) * 512], htr[:, :, :], htr, False)
        else:
            y = yr.next()
            stt(k, y[:, :], x[:, :], ss[:, 2:3], gb[:, :], ALU.mult, ALU.mult, [x, ss, gb], [y])
            k.dma(yout[i * 128:(i + 1) * 128, :], y[:, :], y, False)
    ph.end()


def gen_proj(k, ph, c, hT, wbf, jobs, T):
    TT = 1024
    hts = ph.sb([128, KC, TT], BF16)
    wr = ph.ring(2, [128, KC, 512], BF16)
    pr = ph.ring(6, [128, 512], F32, psum=True)
    ofr = ph.ring(3, [128, 512], F32)
    obr = ph.ring(4, [128, 512], BF16)
    items = []
    for (mode, c0, ncols, dst, kind) in jobs:
        for blk in range(ncols // 512):
            items.append((mode, c0 + blk * 512, blk, dst, kind))
    yield (T // TT) * len(items)
    ev = [0]
    pend = []

    def evac(ps, kind, dst_ap):
        if kind == "f32":
            o = ofr.next()
        else:
            o = obr.next()
        if kind == "silu":
            act(k, o[:, :], ps[:, :], AF.Silu, [ps], [o])
        else:
            copy(k, "act" if ev[0] % 2 == 0 else "dve", o[:, :], ps[:, :], [ps], [o])
            ev[0] += 1
        k.dma(dst_ap, o[:, :], o, False)

    def drain(keep):
        while len(pend) > keep:
            evac(*pend.pop(0))

    for st in range(T // TT):
        k.dma(hts[:, :, :], hT[:, :, st * TT:(st + 1) * TT], hts, True)
        wt = {}

        def loadw(i):
            w = wr.next()
            wt[i] = w
            k.dma(w[:, :, :], wbf[:, :, items[i][1]:items[i][1] + 512], w, True)

        loadw(0)
        for i, (mode, cc, blk, dst, kind) in enumerate(items):
            if i + 1 < len(items):
                loadw(i + 1)
            w = wt.pop(i)
            if mode == "FM":
                for j in range(4):
                    for tc in range(TT // 512):
                        ps = pr.next()
                        for kc in range(KC):
                            mm(k, ps[:, :], w[:, kc, j * 128:(j + 1) * 128], hts[:, kc, tc * 512:(tc + 1) * 512],
                               kc == 0, kc == KC - 1, [w, hts], [ps])
                        t0 = st * TT + tc * 512
                        pend.append((ps, kind, dst[blk * 4 + j, :, t0:t0 + 512]))
                        drain(DBG_EV)
            else:
                for sub in range(TT // 128):
                    ps = pr.next()
                    for kc in range(KC):
                        mm(k, ps[:, :], hts[:, kc, sub * 128:(sub + 1) * 128], w[:, kc, :],
                           kc == 0, kc == KC - 1, [w, hts], [ps])
                    t0 = st * TT + sub * 128
                    pend.append((ps, kind, dst[t0:t0 + 128, blk * 512:(blk + 1) * 512]))
                    drain(DBG_EV)
            yield
        drain(0)


def phase_proj(k, c, hT, wbf, jobs, T):
    co_run(k, "pj", [lambda ph: gen_proj(k, ph, c, hT, wbf, jobs, T)])


def phase_memkv(k, c, memsrc, gvec, wkv, kmT, vmx):
    ph = Phase(k, "mk")
    gb = ph.sb([128, D], F32)
    k.dma(gb[:, :], gvec.partition_broadcast(128), gb, True)
    ident = ph.sb([128, 128], BF16)
    k.dma(ident[:, :], c.ident[:, :], ident, True)
    epsb = ph.sb([128, 1], F32)
    k.op("pool", lambda e: e.memset(epsb[:, :], EPS), [], [epsb])
    w = ph.sb([128, KC, 2048], BF16)
    k.dma(w[:, :, :], wkv[:, :, :], w, True)
    hmT = ph.sb([128, KC, NMEM], BF16)
    xr = ph.ring(2, [128, D], F32)
    hbr = ph.ring(2, [128, D], BF16)
    ssr = ph.ring(2, [128, 4], F32)
    tpr = ph.ring(2, [128, 4, 128], BF16, psum=True)
    pr = ph.ring(2, [128, 512], F32, psum=True)
    for i in range(2):
        x = xr.next()
        k.dma(x[:, :], memsrc[i * 128:(i + 1) * 128, :], x, True)
        ss = ssr.next()
        hb = hbr.next()
        act(k, hb[:, :], x[:, :], AF.Square, [x], [hb, ss], accum=ss[:, 0:1])
        act(k, ss[:, 1:2], ss[:, 0:1], AF.Ln, [ss, epsb], [ss], bias=epsb[:, 0:1], scale=1.0 / D)
        act(k, ss[:, 2:3], ss[:, 1:2], AF.Exp, [ss], [ss], scale=-0.5)
        stt(k, hb[:, :], x[:, :], ss[:, 2:3], gb[:, :], ALU.mult, ALU.mult, [x, ss, gb], [hb])
        for q in range(4):
            tp = tpr.next()
            for cc in range(4):
                tr(k, tp[:, cc, :], hb[:, (4 * q + cc) * 128:(4 * q + cc + 1) * 128], ident[:, :], [hb, ident], [tp])
            copy(k, "dve", hmT[:, 4 * q:4 * q + 4, i * 128:(i + 1) * 128], tp[:, :, :], [tp], [hmT])
    kms = ph.sb([128, 8, NMEM], BF16)
    for j in range(8):
        ps = pr.next()
        for kc in range(KC):
            mm(k, ps[:, 0:NMEM], w[:, kc, j * 128:(j + 1) * 128], hmT[:, kc, :], kc == 0, kc == KC - 1, [w, hmT], [ps])
        copy(k, "act", kms[:, j, :], ps[:, 0:NMEM], [ps], [kms])
    k.dma(kmT[:, :, :], kms[:, :, :], kms, False)
    vms = ph.sb([128, 2, 4, 257], BF16)
    k.op("pool", lambda e: e.memset(vms[:, :, :, :], 1.0), [], [vms])
    for mt in range(2):
        for cb in range(2):
            ps = pr.next()
            for kc in range(KC):
                mm(k, ps[:, :], hmT[:, kc, mt * 128:(mt + 1) * 128], w[:, kc, 1024 + cb * 512:1024 + (cb + 1) * 512],
                   kc == 0, kc == KC - 1, [w, hmT], [ps])
            copy(k, "dve", vms[:, mt, 2 * cb:2 * cb + 2, 0:256], ps[:, :].rearrange("p (h d) -> p h d", h=2), [ps], [vms])
    k.dma(vmx[:, :, :, :], vms[:, :, :, :], vms, False)
    ph.end()


def bc_mid(ap2d, n):
    a = ap2d.ap
    return bass.AP(ap2d.tensor, ap2d.offset, [list(a[0]), [0, n], list(a[1])])


def phase_mematt(k, c, qmT, kmT, vmx, sg, om, T):
    ph = Phase(k, "ma")
    kms = ph.sb([128, 8, NMEM], BF16)
    k.dma(kms[:, :, :], kmT[:, :, :], kms, True)
    vms = ph.sb([128, 2, 4, 257], BF16)
    k.dma(vms[:, :, :, :], vmx[:, :, :, :], vms, True)
    qr = ph.ring(2, [128, 8, 512], BF16)
    sgr = ph.ring(2, [128, 4, 1024], BF16)
    ptr = ph.ring(2, [128, 2, 512], BF16)
    pss = ph.ring(3, [128, 512], F32, psum=True)
    pso = ph.ring(3, [128, 512], F32, psum=True)
    rdr = ph.ring(4, [128, 2], F32)
    otr = ph.ring(8, [128, 1024], BF16)
    n5 = T // 512
    tl = {}

    def load(i):
        q = qr.next()
        g = sgr.next()
        tl[i] = (q, g)
        k.dma(q[:, :, :], qmT[:, :, i * 512:(i + 1) * 512].rearrange("j p t -> p j t"), q, True)
        k.dma(g[:, :, :], sg[i * 512:(i + 1) * 512, 3072:4096].rearrange("(s p) d -> p s d", p=128), g, True)

    load(0)
    for i in range(n5):
        if i + 1 < n5:
            load(i + 1)
        q, g = tl.pop(i)
        ots = [otr.next() for _ in range(4)]
        for hm in range(4):
            pt = ptr.next()
            for mt in range(2):
                ps = pss.next()
                for dc in range(2):
                    mm(k, ps[:, :], kms[:, 2 * hm + dc, mt * 128:(mt + 1) * 128], q[:, 2 * hm + dc, :],
                       dc == 0, dc == 1, [kms, q], [ps])
                act(k, pt[:, mt, :], ps[:, :], AF.Exp, [ps], [pt], scale=1.0 / 16.0)
            for sub in range(4):
                po = pso.next()
                for mt in range(2):
                    mm(k, po[:, 0:257], pt[:, mt, sub * 128:(sub + 1) * 128], vms[:, mt, hm, :],
                       mt == 0, mt == 1, [pt, vms], [po])
                rd = rdr.next()
                k.op("dve", lambda e, rd=rd, po=po: e.reciprocal(out=rd[:, 0:1], in_=po[:, 256:257]), [po], [rd])
                stt(k, ots[sub][:, hm * 256:(hm + 1) * 256], po[:, 0:256], rd[:, 0:1],
                    g[:, sub, hm * 256:(hm + 1) * 256], ALU.mult, ALU.mult, [po, rd, g], [ots[sub]])
        for sub in range(4):
            t0 = i * 512 + sub * 128
            k.dma(om[t0:t0 + 128, :], ots[sub][:, :], ots[sub], False)
    ph.end()


def gen_chain(k, ph, c, zsrc, qoff, foff, lbD, cbD, lcol0, qtD, ktD, khD, decD, T):
    lbs = ph.sb([128, 48], F32)
    k.dma(lbs[:, :], lbD[:, :], lbs, True)
    cbs = ph.sb([128, 48], F32)
    k.dma(cbs[:, :], cbD[:, :], cbs, True)
    one = ph.sb([128, 64], F32)
    k.dma(one[:, :], c.ones[:, :], one, True)
    msk = ph.sb([128, 512], F32)
    k.dma(msk[:, :], c.cmask[:, :], msk, True)
    ident = ph.sb([128, 128], BF16)
    k.dma(ident[:, :], c.ident[:, :], ident, True)
    zr = ph.ring(6, [128, 512], F32)
    names = ("eq", "Lq", "e", "L1", "L2", "lf", "G", "t", "u", "w", "v1", "E")
    R = {n: ph.ring(2, [128, 512], F32) for n in names}
    qtr = ph.ring(2, [128, 512], BF16)
    ktr = ph.ring(2, [128, 512], BF16)
    khr = ph.ring(4, [128, 512], BF16)
    khs = ph.ring(2, [128, 4, 128], BF16)
    tpr = ph.ring(1, [128, 4, 128], BF16, psum=True)
    nch = T // 32
    decs = ph.sb([128, NH, nch], F32)
    items = [(h, tc) for h in range(NH) for tc in range(T // 512)]
    yield len(items) + 2
    khq = {}
    zt = {}
    tmp = {}
    sc = float(128 ** -0.5)
    o1 = one[:, 0:1]

    def load(i):
        h, tc = items[i]
        zq = zr.next()
        zf = zr.next()
        zt[i] = (zq, zf)
        k.dma(zq[:, :], zsrc[qoff + h, :, tc * 512:(tc + 1) * 512], zq, True)
        k.dma(zf[:, :], zsrc[foff + h, :, tc * 512:(tc + 1) * 512], zf, True)

    def stage1(i):
        h, tc = items[i]
        zq, zf = zt[i]
        col = lcol0 + h
        t = {n: R[n].next() for n in names}
        tmp[i] = t
        act(k, t["eq"][:, :], zq[:, :], AF.Exp, [zq], [t["eq"]], scale=-1.0)
        act(k, t["e"][:, :], zf[:, :], AF.Exp, [zf], [t["e"]], scale=-1.0)
        act(k, t["L1"][:, :], t["e"][:, :], AF.Ln, [t["e"], one], [t["L1"]], bias=o1)
        act(k, t["L2"][:, :], t["e"][:, :], AF.Ln, [t["e"], one, lbs], [t["L2"]], bias=o1, scale=lbs[:, col:col + 1])
        act(k, t["Lq"][:, :], t["eq"][:, :], AF.Ln, [t["eq"], one], [t["Lq"]], bias=o1)
        tt(k, "dve", t["lf"][:, :], t["L2"][:, :], t["L1"][:, :], ALU.subtract, [t["L2"], t["L1"]], [t["lf"]])
        G = t["G"]
        k.op("dve", lambda e, G=G, lf=t["lf"]: e.tensor_tensor_scan(
            out=G[:, :], data0=msk[:, :], data1=lf[:, :], initial=0.0, op0=ALU.mult, op1=ALU.add), [msk, t["lf"]], [G])
        tt(k, "pool", t["t"][:, :], zf[:, :], t["L1"][:, :], ALU.add, [zf, t["L1"]], [t["t"]])
        tt(k, "dve", t["u"][:, :], t["t"][:, :], G[:, :], ALU.add, [t["t"], G], [t["u"]])
        tt(k, "pool", t["v1"][:, :], G[:, :], t["Lq"][:, :], ALU.subtract, [G, t["Lq"]], [t["v1"]])

    def stage2(i):
        h, tc = items[i]
        zq, zf = zt.pop(i)
        col = lcol0 + h
        t = tmp.pop(i)
        G = t["G"]
        kt = ktr.next()
        act(k, kt[:, :], t["u"][:, :], AF.Exp, [t["u"], cbs], [kt], bias=cbs[:, col:col + 1], scale=-1.0)
        act(k, t["E"][:, :], t["v1"][:, :], AF.Exp, [t["v1"]], [t["E"]])
        G3 = G[:, :].rearrange("p (c j) -> p c j", j=32)
        glast = G3[:, :, 31:32]
        ga = glast.ap
        glb = bass.AP(glast.tensor, glast.offset, [list(ga[0]), list(ga[1]), [0, 32]])
        tt(k, "dve", t["w"][:, :].rearrange("p (c j) -> p c j", j=32), t["u"][:, :].rearrange("p (c j) -> p c j", j=32),
           glb, ALU.subtract, [t["u"], G], [t["w"]])
        kh = khr.next()
        act(k, kh[:, :], t["w"][:, :], AF.Exp, [t["w"], cbs], [kh], bias=cbs[:, col:col + 1], scale=-1.0)
        qt = qtr.next()
        stt(k, qt[:, :], zq[:, :], sc, t["E"][:, :], ALU.mult, ALU.mult, [zq, t["E"]], [qt])
        copy(k, "pool", decs[:, h, tc * 16:(tc + 1) * 16], G3[:, :, 31], [G], [decs])
        khq[i] = kh
        t0 = tc * 512
        k.dma(qtD[h, :, t0:t0 + 512], qt[:, :], qt, False)
        k.dma(ktD[h, :, t0:t0 + 512], kt[:, :], kt, False)

    def stage3(i):
        h, tc = items[i]
        kh = khq.pop(i)
        tp = tpr.next()
        for jj in range(4):
            tr(k, tp[:, jj, :], kh[:, jj * 128:(jj + 1) * 128], ident[:, :], [kh, ident], [tp])
        ks = khs.next()
        copy(k, "dve", ks[:, :, :], tp[:, :, :], [tp], [ks])
        t0 = tc * 512
        k.dma(khD[t0:t0 + 512, h * 128:(h + 1) * 128].rearrange("(j p) d -> p j d", p=128), ks[:, :, :], ks, False)

    n = len(items)
    load(0)
    if n > 1:
        load(1)
    stage1(0)
    for i in range(n + 2):
        if i + 2 < n:
            load(i + 2)
        if i + 1 < n:
            stage1(i + 1)
        if i < n:
            stage2(i)
        if 0 <= i - DBG_S3 < n:
            stage3(i - DBG_S3)
        yield
    k.dma(decD[:, :, :], decs[:, :, :], decs, False)


def phase_chain(k, c, *a):
    co_run(k, "ch", [lambda ph: gen_chain(k, ph, c, *a)])


def gen_mix(k, ph, c, qtD, ktD, khD, vD, decD, oD, T):
    msk = ph.sb([128, 128], F32)
    k.dma(msk[:, :], c.trimask[:, :], msk, True)
    nch = T // 32
    SEG = 512
    NB = SEG // 128
    NCH = SEG // 32
    nseg = T // SEG
    HP = NH // 2
    decr = ph.ring(4, [128, nch], F32)
    dexr = ph.ring(4, [128, NCH + 1], F32)
    qr = ph.ring(4, [128, SEG], BF16)
    kr = ph.ring(4, [128, SEG], BF16)
    hr = ph.ring(4, [128, NB, 128], BF16)
    vr = ph.ring(4, [128, NB, 128], BF16)
    dsx = ph.ring(4, [128, NCH + 1, 128], F32)
    sal = ph.ring(4, [128, NCH + 1, 128], F32)
    sbr = ph.ring(4, [128, NCH, 128], BF16)
    atr = ph.ring(4, [128, 128], BF16)
    osr = ph.ring(4, [128, 128], F32)
    psd = [ph.ps([128, 512], F32) for _ in range(4)]
    par = ph.ring(1, [128, 512], F32, psum=True)
    por = ph.ring(2, [128, 512], F32, psum=True)
    items = [(p, sg_) for p in range(HP) for sg_ in range(nseg)]
    yield len(items)
    st = {}
    lane_dec = {}

    def prep(i):
        p, sg_ = items[i]
        t0 = sg_ * SEG
        lanes = []
        for ln in range(2):
            h = p + ln * HP
            q = qr.next(); kk = kr.next(); hh = hr.next(); vv = vr.next()
            k.dma(q[:, :], qtD[h, :, t0:t0 + SEG], q, True)
            k.dma(kk[:, :], ktD[h, :, t0:t0 + SEG], kk, True)
            k.dma(hh[:, :, :], khD[t0:t0 + SEG, h * 128:(h + 1) * 128].rearrange("(b p) d -> p b d", p=128), hh, True)
            k.dma(vv[:, :, :], vD[t0:t0 + SEG, h * 128:(h + 1) * 128].rearrange("(b p) d -> p b d", p=128), vv, True)
            if sg_ == 0:
                dec = decr.next()
                k.dma(dec[:, :], decD[:, h, :], dec, True)
                lane_dec[(p, ln)] = dec
            dec = lane_dec[(p, ln)]
            dx = dexr.next()
            act(k, dx[:, 1:NCH + 1], dec[:, sg_ * NCH:(sg_ + 1) * NCH], AF.Exp, [dec], [dx])
            ds = dsx.next()
            ds4 = ds[:, 1:NCH + 1, :].rearrange("p (b c) d -> p b c d", c=4)
            for b in range(NB):
                for c4 in range(4):
                    kw = {"tile_position": (96, 0)} if c4 == 3 else {}
                    mm(k, psd[c4][:, b * 128:(b + 1) * 128], hh[32 * c4:32 * c4 + 32, b, :],
                       vv[32 * c4:32 * c4 + 32, b, :], True, True, [hh, vv], [psd[c4]], **kw)
            for c4 in range(4):
                copy(k, "act", ds4[:, :, c4, :], psd[c4][:, :].rearrange("p (b d) -> p b d", d=128), [psd[c4]], [ds])
            lanes.append((h, q, kk, hh, vv, dx, ds))
        st[i] = lanes

    prevS = {}
    prep(0)
    for i, (p, sg_) in enumerate(items):
        if i + 1 < len(items):
            prep(i + 1)
        lanes = st.pop(i)
        Ss = []
        for ln in range(2):
            S = sal.next()
            if sg_ == 0:
                k.op("pool", lambda e, S=S: e.memset(S[:, 0, :], 0.0), [], [S])
            else:
                copy(k, "pool", S[:, 0, :], prevS[ln][:, NCH, :], [prevS[ln]], [S])
            Ss.append(S)
        for ci in range(NCH):
            for ln in range(2):
                S = Ss[ln]
                dx, ds = lanes[ln][5], lanes[ln][6]
                stt(k, S[:, ci + 1, :], S[:, ci, :], dx[:, ci + 1:ci + 2], ds[:, ci + 1, :], ALU.mult, ALU.add,
                    [S, dx, ds], [S])
        Sbs = []
        for ln in range(2):
            Sb = sbr.next()
            copy(k, "act", Sb[:, :, :], Ss[ln][:, 0:NCH, :], [Ss[ln]], [Sb])
            Sbs.append(Sb)
            prevS[ln] = Ss[ln]
        for blk in range(NB):
            pas = []
            for ln in range(2):
                h, q, kk, hh, vv, dx, ds = lanes[ln]
                pa = par.next()
                mm(k, pa[:, 0:128], kk[:, blk * 128:(blk + 1) * 128], q[:, blk * 128:(blk + 1) * 128], True, True,
                   [kk, q], [pa])
                at = atr.next()
                tt(k, "dve", at[:, :], pa[:, 0:128], msk[:, :], ALU.mult, [pa, msk], [at])
                pas.append(at)
            for ln in range(2):
                h, q, kk, hh, vv, dx, ds = lanes[ln]
                at = pas[ln]
                po = por.next()
                mm(k, po[:, 0:128], at[:, :], vv[:, blk, :], True, False, [at, vv], [po])
                for c4 in range(4):
                    kw = {"tile_position": (0, 96)} if c4 == 3 else {}
                    mm(k, po[32 * c4:32 * c4 + 32, 0:128], q[:, blk * 128 + 32 * c4:blk * 128 + 32 * c4 + 32],
                       Sbs[ln][:, blk * 4 + c4, :], False, c4 == 3, [q, Sbs[ln]], [po], **kw)
                os_ = osr.next()
                copy(k, "act", os_[:, :], po[:, 0:128], [po], [os_])
                t0 = sg_ * SEG + blk * 128
                k.dma(oD[t0:t0 + 128, h * 128:(h + 1) * 128], os_[:, :], os_, False)
        yield


def phase_mix(k, c, *a):
    co_run(k, "mx", [lambda ph: gen_mix(k, ph, c, *a)])


def phase_outn(k, c, oF, oBr, sg, ongv, ob, T):
    ph = Phase(k, "on")
    jf = ph.sb([128, 128], F32)
    k.dma(jf[:, :], c.jrevf[:, :], jf, True)
    ong = ph.sb([128, TOK_W], F32)
    k.dma(ong[:, :], ongv.partition_broadcast(128), ong, True)
    epsb = ph.sb([128, 1], F32)
    k.op("pool", lambda e: e.memset(epsb[:, :], EPS), [], [epsb])
    ofr = ph.ring(2, [128, TOK_W], F32)
    obr = ph.ring(2, [128, TOK_W], F32)
    sgr = ph.ring(2, [128, TOK_W], BF16)
    sqr = ph.ring(1, [128, TOK_W], F32)
    gsr = ph.ring(1, [128, TOK_W], F32)
    outr = ph.ring(2, [128, TOK_W], BF16)
    ssr = ph.ring(2, [128, 3, NH], F32)
    pr = ph.ring(6, [128, 512], F32, psum=True)
    nt = T // 128
    tl = {}

    def load(i):
        a = ofr.next(); b = obr.next(); g = sgr.next()
        tl[i] = (a, b, g)
        ir = nt - 1 - i
        k.dma(a[:, :], oF[i * 128:(i + 1) * 128, :], a, True)
        k.dma(b[:, :], oBr[ir * 128:(ir + 1) * 128, :], b, True)
        k.dma(g[:, :], sg[i * 128:(i + 1) * 128, 0:TOK_W], g, True)

    load(0)
    for i in range(nt):
        if i + 1 < nt:
            load(i + 1)
        a, b, g = tl.pop(i)
        for cb in range(6):
            ps = pr.next()
            mm(k, ps[:, :], jf[:, :], b[:, cb * 512:(cb + 1) * 512], True, True, [jf, b], [ps])
            tt(k, "dve", a[:, cb * 512:(cb + 1) * 512], a[:, cb * 512:(cb + 1) * 512], ps[:, :], ALU.add, [a, ps], [a])
        sq = sqr.next()
        tt(k, "pool", sq[:, :], a[:, :], a[:, :], ALU.mult, [a], [sq])
        ss = ssr.next()
        k.op("dve", lambda e, ss=ss, sq=sq: e.tensor_reduce(
            out=ss[:, 0, :], in_=sq[:, :].rearrange("p (h d) -> p h d", d=128), axis=AX.X, op=ALU.add), [sq], [ss])
        act(k, ss[:, 1, :], ss[:, 0, :], AF.Ln, [ss, epsb], [ss], bias=epsb[:, 0:1], scale=1.0 / 128)
        act(k, ss[:, 2, :], ss[:, 1, :], AF.Exp, [ss], [ss], scale=-0.5)
        gs = gsr.next()
        tt(k, "pool", gs[:, :], g[:, :], ong[:, :], ALU.mult, [g, ong], [gs])
        r2 = ss[:, 2, :]
        ra = r2.ap
        rb = bass.AP(r2.tensor, r2.offset, [list(ra[0]), list(ra[1]), [0, 128]])
        tt(k, "dve", sq[:, :].rearrange("p (h d) -> p h d", d=128), a[:, :].rearrange("p (h d) -> p h d", d=128), rb,
           ALU.mult, [a, ss], [sq])
        o = outr.next()
        tt(k, "dve", o[:, :], sq[:, :], gs[:, :], ALU.mult, [sq, gs], [o])
        k.dma(ob[i * 128:(i + 1) * 128, :], o[:, :], o, False)
    ph.end()


def phase_outw(k, c, ob, om, wo, xsrc, xdst, T):
    ph = Phase(k, "ow")
    ident = ph.sb([128, 128], BF16)
    k.dma(ident[:, :], c.ident[:, :], ident, True)
    TTo = 512
    obr = ph.ring(2, [128, MIX_W], BF16)
    otr = ph.ring(2, [128, 32, TTo], BF16)
    wr = ph.ring(2, [128, 32, 512], BF16)
    xpr = ph.ring(3, [128, 512], F32)
    xor_ = ph.ring(3, [128, 512], F32)
    tpr = ph.ring(3, [128, 4, 128], BF16, psum=True)
    pr = ph.ring(4, [128, 512], F32, psum=True)
    nst = T // TTo
    ev = 0
    items = [(st, db) for st in range(nst) for db in range(4)]
    wt = {}

    def loadw(i):
        w = wr.next()
        wt[i] = w
        db = items[i][1]
        k.dma(w[:, :, :], wo[:, :, db * 512:(db + 1) * 512], w, True)

    loadw(0)
    oT = None
    for i, (st, db) in enumerate(items):
        if db == 0:
            oT = otr.next()
            for sub in range(4):
                t0 = st * TTo + sub * 128
                o = obr.next()
                k.dma(o[:, 0:TOK_W], ob[t0:t0 + 128, :], o, True)
                k.dma(o[:, TOK_W:MIX_W], om[t0:t0 + 128, :], o, True)
                for q in range(8):
                    tp = tpr.next()
                    for cc in range(4):
                        tr(k, tp[:, cc, :], o[:, (4 * q + cc) * 128:(4 * q + cc + 1) * 128], ident[:, :], [o, ident], [tp])
                    copy(k, "act" if ev % 2 == 0 else "dve", oT[:, 4 * q:4 * q + 4, sub * 128:(sub + 1) * 128],
                         tp[:, :, :], [tp], [oT])
                    ev += 1
        if i + 1 < len(items):
            loadw(i + 1)
        w = wt.pop(i)
        for sub in range(4):
            t0 = st * TTo + sub * 128
            xp = xpr.next()
            k.dma(xp[:, :], xsrc[t0:t0 + 128, db * 512:(db + 1) * 512], xp, True)
            ps = pr.next()
            for kc in range(32):
                mm(k, ps[:, :], oT[:, kc, sub * 128:(sub + 1) * 128], w[:, kc, :], kc == 0, kc == 31, [oT, w], [ps])
            xo = xor_.next()
            tt(k, "dve", xo[:, :], ps[:, :], xp[:, :], ALU.add, [ps, xp], [xo])
            k.dma(xdst[t0:t0 + 128, db * 512:(db + 1) * 512], xo[:, :], xo, False)
    ph.end()


def na_ranges(rows):
    rs = lambda r: min(max(r - 4, 0), rows - 8)
    out = []
    for j in range(rows):
        al = [r for r in range(rows) if rs(r) <= j <= rs(r) + 7]
        assert al == list(range(al[0], al[-1] + 1))
        out.append((al[0], al[-1]))
    return out


def phase_na(k, c, fm16, vD, sg, btab, ob, T):
    ph = Phase(k, "na")
    rows = T // GW
    nt = T // 128
    rng_ = na_ranges(rows)
    cmk = ph.sb([128, GW], F32)
    k.dma(cmk[0:64, :], c.colmask[:, :], cmk, True)
    k.dma(cmk[64:128, :], c.colmask[:, :], cmk, True)
    onesb = ph.sb([128, 64], F32)
    k.dma(onesb[:, :], c.ones[:, :], onesb, True)
    qr = ph.ring(2, [128, T], BF16)
    kr = ph.ring(2, [128, T], BF16)
    vr = ph.ring(2, [128, nt, 129], BF16)
    for v in vr.tiles:
        copy(k, "dve", v[:, :, 128:129], onesb[:, 0:nt].rearrange("p (a b) -> p a b", b=1), [onesb], [v])
    sgr = ph.ring(2, [128, nt, 128], BF16)
    obr = ph.ring(2, [128, nt, 128], BF16)
    crf = ph.ring(2, [128, 15, GW], F32)
    crb = ph.ring(2, [128, 15, GW], BF16)
    exr = ph.ring(2, [128, 768], BF16)
    ptr = ph.ring(8, [128, 768], BF16)
    pss = ph.ring(2, [128, 1024], F32, psum=True)
    pso = ph.ring(2, [128, 512], F32, psum=True)
    rdr = ph.ring(4, [128, 2], F32)
    info = []
    for kt in range(nt):
        (a0, b0), (a1, b1) = rng_[2 * kt], rng_[2 * kt + 1]
        qlo = (min(a0, a1) // 2) * 2
        qhi = (max(b0, b1) // 2) * 2 + 1
        info.append((qlo, qhi, (a0, b0), (a1, b1)))
    kts_of = {}
    for kt in range(nt):
        qlo, qhi = info[kt][0], info[kt][1]
        for R in range(qlo // 2, qhi // 2 + 1):
            kts_of.setdefault(R, []).append(kt)
    for R, l in kts_of.items():
        assert l[-1] - l[0] <= 6, (R, l)
    sc = float(128 ** -0.5)
    tl = {}

    def load(h):
        q = qr.next(); kk = kr.next(); v = vr.next(); g = sgr.next(); cf = crf.next()
        tl[h] = (q, kk, v, g, cf)
        k.dma(q[:, :], fm16[h, :, :], q, True)
        k.dma(kk[:, :], fm16[NH + h, :, :], kk, True)
        k.dma(v[:, :, 0:128], vD[:, h * 128:(h + 1) * 128].rearrange("(b p) d -> p b d", p=128), v, True)
        k.dma(g[:, :, :], sg[:, h * 128:(h + 1) * 128].rearrange("(b p) d -> p b d", p=128), g, True)
        k.dma(cf[0:64, :, :], btab[h], cf, True)
        k.dma(cf[64:128, :, :], btab[h], cf, True)

    def finalize(kt_done, pts, v, g, obh):
        for R in sorted(kts_of):
            l = kts_of[R]
            if l[-1] != kt_done:
                continue
            po = pso.next()
            for n_, k2 in enumerate(l):
                off = (R - info[k2][0] // 2) * 128
                mm(k, po[:, 0:129], pts[k2][:, off:off + 128], v[:, k2, :], n_ == 0, n_ == len(l) - 1,
                   [pts[k2], v], [po])
            rd = rdr.next()
            k.op("dve", lambda e, rd=rd, po=po: e.reciprocal(out=rd[:, 0:1], in_=po[:, 128:129]), [po], [rd])
            stt(k, obh[:, R, :], po[:, 0:128], rd[:, 0:1], g[:, R, :], ALU.mult, ALU.mult, [po, rd, g], [obh])

    load(0)
    for h in range(NH):
        if h + 1 < NH:
            load(h + 1)
        q, kk, v, g, cf = tl.pop(h)
        act(k, cf[:, :, :], cf[:, :, :], AF.Exp, [cf], [cf])
        cb = crb.next()
        tt(k, "dve", cb[:, :, :], cf[:, :, :], bc_mid(cmk[:, :], 15), ALU.mult, [cf, cmk], [cb])
        obh = obr.next()
        pts = {}
        for kt in range(nt):
            qlo, qhi, r0, r1 = info[kt]
            nq = (qhi - qlo + 1) * GW
            ps = pss.next()
            n1 = min(512, nq)
            mm(k, ps[:, 0:n1], kk[:, kt * 128:(kt + 1) * 128], q[:, qlo * GW:qlo * GW + n1], True, True, [kk, q], [ps])
            if nq > 512:
                mm(k, ps[:, 512:nq], kk[:, kt * 128:(kt + 1) * 128], q[:, qlo * GW + 512:qlo * GW + nq], True, True,
                   [kk, q], [ps])
            ex = exr.next()
            act(k, ex[:, 0:nq], ps[:, 0:nq], AF.Exp, [ps], [ex], scale=sc)
            pt = ptr.next()
            pts[kt] = pt
            k.op("pool", lambda e, pt=pt, nq=nq: e.memset(pt[:, 0:nq], 0.0), [], [pt])
            for hf, (rlo, rhi) in enumerate((r0, r1)):
                j = 2 * kt + hf
                c0 = (rlo - qlo) * GW
                nr = rhi - rlo + 1
                d0 = 7 - j + rlo
                assert 0 <= d0 and d0 + nr <= 15
                tt(k, "dve", pt[64 * hf:64 * hf + 64, c0:c0 + nr * GW], ex[64 * hf:64 * hf + 64, c0:c0 + nr * GW],
                   cb[64 * hf:64 * hf + 64, d0:d0 + nr, :].rearrange("p a b -> p (a b)"), ALU.mult, [ex, cb], [pt])
            if kt >= 1:
                finalize(kt - 1, pts, v, g, obh)
        finalize(nt - 1, pts, v, g, obh)
        k.dma(ob[:, h * 128:(h + 1) * 128].rearrange("(b p) d -> p b d", p=128), obh[:, :, :], obh, False)
    ph.end()


def phase_lb(k, c, lblT, lbD, cbD):
    ph = Phase(k, "lb")
    l0 = ph.sb([128, 48], F32)
    l1 = ph.sb([128, 48], F32)
    one = ph.sb([128, 64], F32)
    k.dma(l0[:, :], lblT[0], l0, True)
    k.dma(l1[:, :], lblT[1], l1, True)
    k.dma(one[:, :], c.ones[:, :], one, True)
    d = ph.sb([128, 48], F32)
    e = ph.sb([128, 48], F32)
    cb = ph.sb([128, 48], F32)
    lb = ph.sb([128, 48], F32)
    tt(k, "dve", d[:, :], l1[:, :], l0[:, :], ALU.subtract, [l0, l1], [d])
    act(k, e[:, :], d[:, :], AF.Exp, [d], [e])
    act(k, cb[:, :], e[:, :], AF.Ln, [e, one], [cb], bias=one[:, 0:1])
    ts(k, "dve", cb[:, :], cb[:, :], -1.0, None, ALU.mult, None, [cb], [cb])
    tt(k, "dve", d[:, :], d[:, :], cb[:, :], ALU.add, [d, cb], [d])
    act(k, lb[:, :], d[:, :], AF.Exp, [d], [lb])
    k.dma(lbD[:, :], lb[:, :], lb, False)
    k.dma(cbD[:, :], cb[:, :], cb, False)
    ph.end()


def build(T, NS, depth, debug=()):
    nc = bass.Bass("TRN2", target_bir_lowering=False)
    c = Cfg()
    c.T = T

    def din(name, shape, dt=F32):
        return nc.dram_tensor(name, list(shape), dt, kind="ExternalInput").ap()

    def dscr(name, shape, dt):
        kind = "ExternalOutput" if name in debug else "Internal"
        return nc.dram_tensor(name, list(shape), dt, kind=kind).ap()

    x_in = din("x_in", [NS, T, D])
    mem_in = din("mem_in", [NS, NMEM, D])
    w_in_a = din("w_in_a", [2, D, P_A])
    w_in_b = din("w_in_b", [2, D, P_B])
    w_mem_kv = din("w_mem_kv", [4, D, 2 * MEM_W])
    w_out = din("w_out", [4, MIX_W, D])
    norm_g = din("norm_g", [4, D])
    mem_norm_g = din("mem_norm_g", [4, D])
    final_g = din("final_g", [1, D])
    onorm_g = din("hg_onorm_g", [2, TOK_W])
    lblT = din("lblT", [2, 128, 48])
    btab = din("btab", [2, NH, GW, 15, GW])
    c.ident = din("ident", [128, 128], BF16)
    c.jrev = din("jrev", [128, 128], BF16)
    c.jrevf = din("jrevf", [128, 128])
    c.ones = din("ones", [128, 64])
    c.zeros = din("zeros", [128, 128])
    c.cmask = din("cmask", [128, 512])
    c.trimask = din("trimask", [128, 128])
    c.colmask = din("colmask", [GW, GW])
    y = nc.dram_tensor("y", [NS, T, D], F32, kind="ExternalOutput").ap()

    hT = dscr("hT", [128, KC, T], BF16)
    hTr = dscr("hTr", [128, KC, T], BF16)
    wbf = [dscr("wbf%d" % l, [128, KC, P_A if l % 2 == 0 else P_B], BF16) for l in range(4)]
    wkv = [dscr("wkv%d" % l, [128, KC, 2 * MEM_W], BF16) for l in range(4)]
    wo = [dscr("wo%d" % l, [128, 32, D], BF16) for l in range(4)]
    xres = dscr("xres", [NS, T, D], F32)
    zA = dscr("zA", [48, 128, T], F32)
    fm16 = dscr("fm16", [48, 128, T], BF16)
    qmT = dscr("qmT", [8, 128, T], BF16)
    vD = dscr("vD", [T, TOK_W], BF16)
    sg = dscr("sg", [T, MIX_W], BF16)
    om = dscr("om", [T, MEM_W], BF16)
    ob = dscr("ob", [T, TOK_W], BF16)
    kmT = dscr("kmT", [128, 8, NMEM], BF16)
    vmx = dscr("vmx", [128, 2, 4, 257], BF16)
    qtD = dscr("qtD", [NH, 128, T], BF16)
    ktD = dscr("ktD", [NH, 128, T], BF16)
    khD = dscr("khD", [T, TOK_W], BF16)
    decD = dscr("decD", [128, NH, T // 32], F32)
    zB = dscr("zB", [48, 128, T], F32)
    vD2 = dscr("vD2", [T, TOK_W], BF16)
    qtD2 = dscr("qtD2", [NH, 128, T], BF16)
    ktD2 = dscr("ktD2", [NH, 128, T], BF16)
    khD2 = dscr("khD2", [T, TOK_W], BF16)
    decD2 = dscr("decD2", [128, NH, T // 32], F32)
    oF = dscr("oF", [T, TOK_W], F32)
    oB = dscr("oB", [T, TOK_W], F32)
    lbD = [dscr("lbD%d" % a, [128, 48], F32) for a in range(2)]
    cbD = [dscr("cbD%d" % a, [128, 48], F32) for a in range(2)]

    with ExitStack() as es:
        k = KB(nc, es)
        for l in range(depth):
            phase_conv(k, (w_in_a if l % 2 == 0 else w_in_b)[l // 2], wbf[l], D, P_A if l % 2 == 0 else P_B)
            phase_conv(k, w_mem_kv[l], wkv[l], D, 2 * MEM_W)
            phase_conv(k, w_out[l], wo[l], MIX_W, D)
        ph = Phase(k, "z")
        zt = ph.sb([128, 48], F32)
        k.dma(zt[:, :], c.zeros[:, 0:48], zt, True)
        k.dma(lbD[0][:, :], zt[:, :], zt, False)
        k.dma(cbD[0][:, :], zt[:, :], zt, False)
        ph.end()
        phase_lb(k, c, lblT, lbD[1], cbD[1])
        for l in range(depth):
            for s in range(NS):
                xsrc = x_in[s] if l == 0 else xres[s]
                if l % 2 == 0:
                    a = l // 2
                    phase_prep(k, c, xsrc, norm_g[l:l + 1, :], T, hT=hT, hTr=hTr)
                    phase_memkv(k, c, mem_in[s], mem_norm_g[l:l + 1, :], wkv[l], kmT, vmx)
                    jobs = [("FM", 0, 3072, zA[0:24], "f32"), ("FM", 3072, 3072, zA[24:48], "f32"),
                            ("TM", 9216, 3072, vD, "bf16"),
                            ("FM", 12288, 1024, qmT, "bf16"), ("TM", 13312, 4096, sg, "silu")]
                    phase_proj(k, c, hT, wbf[l], jobs, T)
                    phase_mematt(k, c, qmT, kmT, vmx, sg, om, T)
                    jobs_b = [("FM", 0, 3072, zB[0:24], "f32"), ("FM", 6144, 3072, zB[24:48], "f32"),
                              ("TM", 9216, 3072, vD2, "bf16")]
                    g1 = [lambda ph: gen_proj(k, ph, c, hTr, wbf[l], jobs_b, T),
                          lambda ph: gen_chain(k, ph, c, zA, 0, 24, lbD[a], cbD[a], 0, qtD, ktD, khD, decD, T)]
                    g2 = [lambda ph: gen_mix(k, ph, c, qtD, ktD, khD, vD, decD, oF, T),
                          lambda ph: gen_chain(k, ph, c, zB, 0, 24, lbD[a], cbD[a], 24, qtD2, ktD2, khD2, decD2, T)]
                    if CO_MODE & 1:
                        co_run(k, "pc", g1)
                    else:
                        co_run(k, "pc", g1[:1])
                        co_run(k, "pc", g1[1:])
                    if CO_MODE & 2:
                        co_run(k, "mc", g2)
                    else:
                        co_run(k, "mc", g2[:1])
                        co_run(k, "mc", g2[1:])
                    phase_mix(k, c, qtD2, ktD2, khD2, vD2, decD2, oB, T)
                    phase_outn(k, c, oF, oB, sg, onorm_g[a:a + 1, :], ob, T)
                else:
                    phase_prep(k, c, xsrc, norm_g[l:l + 1, :], T, hT=hT)
                    phase_memkv(k, c, mem_in[s], mem_norm_g[l:l + 1, :], wkv[l], kmT, vmx)
                    jobs = [("FM", 0, 3072, fm16[0:24], "bf16"), ("FM", 3072, 3072, fm16[24:48], "bf16"),
                            ("TM", 6144, 3072, vD, "bf16"), ("FM", 9216, 1024, qmT, "bf16"),
                            ("TM", 10240, 4096, sg, "silu")]
                    phase_proj(k, c, hT, wbf[l], jobs, T)
                    phase_mematt(k, c, qmT, kmT, vmx, sg, om, T)
                    phase_na(k, c, fm16, vD, sg, btab[l // 2], ob, T)
                phase_outw(k, c, ob, om, wo[l], xsrc, xres[s], T)
        for s in range(NS):
            phase_prep(k, c, xres[s] if depth > 0 else x_in[s], final_g[0:1, :], T, yout=y[s])
    return nc, k


def _consts():
    cm = np.ones((128, 16, 32), np.float32)
    cm[:, :, 0] = 0
    s_ = np.arange(128)[:, None]
    t_ = np.arange(128)[None, :]
    tri = ((s_ // 32 == t_ // 32) & (s_ <= t_)).astype(np.float32)
    jr = np.eye(128, dtype=np.float32)[::-1].copy()
    kc = np.arange(GW)[:, None]
    qc = np.arange(GW)[None, :]
    cs = np.clip(qc - 8, 0, GW - 16)
    colmask = ((kc >= cs) & (kc < cs + 16)).astype(np.float32)
    return {"ident": np.eye(128, dtype=np.float32).astype(ml_dtypes.bfloat16), "ones": np.ones((128, 64), np.float32),
            "zeros": np.zeros((128, 128), np.float32), "cmask": cm.reshape(128, 512), "trimask": tri,
            "jrev": jr.astype(ml_dtypes.bfloat16), "jrevf": jr, "colmask": colmask}


def _layout_inputs(hg_lb_logits, na_rpb):
    lblT = np.ascontiguousarray(
        np.asarray(hg_lb_logits, np.float32).reshape(2, 2, NH, 128).transpose(0, 3, 1, 2).reshape(2, 128, 48))
    kc = np.arange(GW)[:, None]
    qc = np.arange(GW)[None, :]
    dc = np.clip(kc - qc + 15, 0, 30)
    rp = np.asarray(na_rpb, np.float32)[:, :, ::-1, :]
    bt = rp[:, :, :, dc]
    btab = np.ascontiguousarray(bt.transpose(0, 1, 3, 2, 4))
    return lblT, btab


def make_in_maps(NS, xs, mems, w_in_a, w_in_b, w_mem_kv, w_out, norm_g, mem_norm_g, hg_lb_logits, hg_onorm_g,
                 na_rpb, final_g):
    cst = _consts()
    lblT, btab = _layout_inputs(hg_lb_logits, na_rpb)
    maps = []
    for x, m_ in zip(xs, mems):
        m = {"x_in": x, "mem_in": m_, "w_in_a": w_in_a, "w_in_b": w_in_b, "w_mem_kv": w_mem_kv, "w_out": w_out,
             "norm_g": norm_g, "mem_norm_g": mem_norm_g, "final_g": np.asarray(final_g).reshape(1, D),
             "hg_onorm_g": hg_onorm_g, "lblT": lblT, "btab": btab}
        m.update(cst)
        maps.append(m)
    return maps


def kernel(x_prompt, x_sample, mem_prompt, mem_sample, w_in_a, w_in_b, w_mem_kv, w_out,
           norm_g, mem_norm_g, hg_lb_logits, hg_onorm_g, na_rpb, final_g):
    T = 4096
    NS = 2
    nc, k = build(T, NS, 4)
    xs = [np.ascontiguousarray(np.stack([x_sample[cid], x_prompt[cid % 2]])) for cid in range(8)]
    mems = [np.ascontiguousarray(np.stack([mem_sample[cid], mem_prompt[cid % 2]])) for cid in range(8)]
    in_maps = make_in_maps(NS, xs, mems, w_in_a, w_in_b, w_mem_kv, w_out, norm_g, mem_norm_g, hg_lb_logits,
                           hg_onorm_g, na_rpb, final_g)
    res = run_bass_kernel_spmd(nc, in_maps, core_ids=list(range(8)))
    ys = [r["y"] for r in res.results]
    y_sample = np.stack([ys[cid][0] for cid in range(8)])
    y_prompt = np.stack([ys[cid][1] for cid in range(2)])
    return (y_prompt.astype(np.float32), y_sample.astype(np.float32))
```

```python
import numpy as np
import ml_dtypes
from contextlib import ExitStack
import concourse.bass as bass
import concourse.mybir as mybir
from concourse.bass_utils import run_bass_kernel_spmd

F32 = mybir.dt.float32
BF16 = mybir.dt.bfloat16
AF = mybir.ActivationFunctionType
ALU = mybir.AluOpType
AX = mybir.AxisListType

D = 2048
KC = D // 128
TOK_W = 3072
NH = 24
MEM_W = 1024
MIX_W = 4096
P_A = 4 * TOK_W + MEM_W + MIX_W
P_B = 3 * TOK_W + MEM_W + MIX_W
NMEM = 256
EPS = 1e-6
GW = 64
CH = 32
CO_MODE = 3
DBG_EV = 3
DBG_S3 = 2


class SemObj:
    __slots__ = ("sem", "step", "count")

    def __init__(self, sem, step):
        self.sem = sem
        self.step = step
        self.count = 0


class Buf:
    __slots__ = ("lw", "rd", "dsem")

    def __init__(self):
        self.lw = {}
        self.rd = {}
        self.dsem = None


class Tl:
    __slots__ = ("t", "b")

    def __init__(self, t):
        self.t = t
        self.b = Buf()

    def __getitem__(self, idx):
        return self.t[idx]


class Ring:
    def __init__(self, tiles):
        self.tiles = tiles
        self.i = 0

    def next(self):
        t = self.tiles[self.i % len(self.tiles)]
        self.i += 1
        return t


class KB:
    ENG = ("sp", "pe", "act", "dve", "pool")

    def __init__(self, nc, es, n_dsem=96):
        self.nc = nc
        self.esem = {n: SemObj(es.enter_context(nc.semaphore("s_" + n)), 1)
                     for n in ("pe", "act", "dve", "pool")}
        self.dsems = [SemObj(es.enter_context(nc.semaphore("d%d" % i)), 16) for i in range(n_dsem)]
        self.dfree = list(self.dsems)
        self.ops = {n: [] for n in self.ENG}
        self.waited = {n: {} for n in self.ENG}
        self.uid = 0
        self.nops = 0

    def _deps(self, en, R, W):
        need = {}
        for b in R:
            for so, v in b.lw.items():
                if need.get(so, 0) < v:
                    need[so] = v
        for b in W:
            for so, v in b.lw.items():
                if need.get(so, 0) < v:
                    need[so] = v
            for so, v in b.rd.items():
                if need.get(so, 0) < v:
                    need[so] = v
        waits = []
        wd = self.waited[en]
        pe_so = self.esem["pe"]
        for so, v in need.items():
            if en == "pe" and so is pe_so:
                continue
            if wd.get(so, 0) < v:
                waits.append((so.sem, v))
                wd[so] = v
        return waits

    def op(self, en, fn, R=(), W=()):
        R = [x.b if isinstance(x, Tl) else x for x in R]
        W = [x.b if isinstance(x, Tl) else x for x in W]
        waits = self._deps(en, R, W)
        so = self.esem[en]
        so.count += 1
        n = so.count
        self.ops[en].append((waits, fn, so.sem, 1))
        for b in R:
            b.rd[so] = n
        for b in W:
            b.lw[so] = n
        self.nops += 1

    def dma(self, out, in_, sb, load, **kw):
        b = sb.b if isinstance(sb, Tl) else sb
        if b.dsem is None:
            b.dsem = self.dfree.pop()
        waits = self._deps("sp", [] if load else [b], [b] if load else [])
        so = b.dsem
        so.count += 16
        self.ops["sp"].append((waits, lambda e: e.dma_start(out=out, in_=in_, **kw), so.sem, 16))
        if load:
            b.lw[so] = so.count
        else:
            b.rd[so] = so.count
        self.nops += 1

    def barrier(self):
        sos = list(self.esem.values()) + [d for d in self.dsems if d.count > 0]
        for en in self.ENG:
            wd = self.waited[en]
            waits = []
            for so in sos:
                if en == "pe" and so is self.esem["pe"]:
                    continue
                if wd.get(so, 0) < so.count:
                    waits.append((so.sem, so.count))
                    wd[so] = so.count
            if waits:
                self.ops[en].append((waits, None, None, 0))

    def flush(self):
        nc = self.nc
        with nc.Block() as block:
            decos = {"sp": block.sync, "pe": block.tensor, "act": block.scalar,
                     "dve": block.vector, "pool": block.gpsimd}
            for en in self.ENG:
                ops = self.ops[en]
                if not ops:
                    continue

                def body(e, ops=ops):
                    for waits, fn, sem, inc in ops:
                        for s, v in waits:
                            e.wait_ge(s, v)
                        if fn is not None:
                            fn(e).then_inc(sem, inc)
                decos[en](body)
                self.ops[en] = []


class Phase:
    def __init__(self, k, name):
        self.k = k
        k.uid += 1
        self.name = "%s%d" % (name, k.uid)
        self.es = ExitStack()
        self.tiles = []
        self.n = 0

    def _mk(self, fn, shape, dt):
        self.n += 1
        t = self.es.enter_context(fn("%s_%d" % (self.name, self.n), list(shape), dt))
        tl = Tl(t)
        self.tiles.append(tl)
        return tl

    def sb(self, shape, dt):
        return self._mk(self.k.nc.sbuf_tensor, shape, dt)

    def ps(self, shape, dt):
        return self._mk(self.k.nc.psum_tensor, shape, dt)

    def ring(self, n, shape, dt, psum=False):
        return Ring([(self.ps if psum else self.sb)(shape, dt) for _ in range(n)])

    def end(self):
        k = self.k
        k.barrier()
        k.flush()
        for tl in self.tiles:
            if tl.b.dsem is not None:
                k.dfree.append(tl.b.dsem)
                tl.b.dsem = None
        self.es.close()


def act(k, out, in_, func, R, W, bias=None, scale=1.0, accum=None):
    kw = {}
    if bias is not None:
        kw["bias"] = bias
    if accum is not None:
        kw["accum_out"] = accum
    k.op("act", lambda e: e.activation(out=out, in_=in_, func=func, scale=scale, **kw), R, W)


def tt(k, en, out, in0, in1, op, R, W):
    k.op(en, lambda e: e.tensor_tensor(out=out, in0=in0, in1=in1, op=op), R, W)


def ts(k, en, out, in0, s1, s2, op0, op1, R, W):
    if s2 is None:
        k.op(en, lambda e: e.tensor_scalar(out=out, in0=in0, scalar1=s1, scalar2=None, op0=op0), R, W)
    else:
        k.op(en, lambda e: e.tensor_scalar(out=out, in0=in0, scalar1=s1, scalar2=s2, op0=op0, op1=op1), R, W)


def stt(k, out, in0, scalar, in1, op0, op1, R, W):
    k.op("dve", lambda e: e.scalar_tensor_tensor(out=out, in0=in0, scalar=scalar, in1=in1, op0=op0, op1=op1), R, W)


def copy(k, en, out, in_, R, W):
    if en == "act":
        k.op("act", lambda e: e.copy(out=out, in_=in_), R, W)
    else:
        k.op(en, lambda e: e.tensor_copy(out=out, in_=in_), R, W)


def mm(k, out, lhsT, rhs, start, stop, R, W, **kw):
    k.op("pe", lambda e: e.matmul(out, lhsT, rhs, start=start, stop=stop, **kw), R, W)


def tr(k, out, in_, ident, R, W):
    k.op("pe", lambda e: e.transpose(out, in_, ident), R, W)


class Cfg:
    pass


def co_run(k, name, makers):
    ph = Phase(k, name)
    gens = [m(ph) for m in makers]
    counts = [max(1, next(g)) for g in gens]
    base = max(counts)
    rate = [cn / base for cn in counts]
    acc = [0.0] * len(gens)
    alive = set(range(len(gens)))
    while alive:
        for gi in range(len(gens)):
            if gi not in alive:
                continue
            acc[gi] += rate[gi]
            while acc[gi] >= 1.0 - 1e-9 and gi in alive:
                acc[gi] -= 1.0
                try:
                    next(gens[gi])
                except StopIteration:
                    alive.discard(gi)
    ph.end()


def phase_conv(k, src2d, dst3d, K, P):
    ph = Phase(k, "cv")
    CW = 2048
    fr = ph.ring(3, [128, CW], F32)
    br = ph.ring(3, [128, CW], BF16)
    items = [(kc, c0) for kc in range(K // 128) for c0 in range(0, P, CW)]
    engs = ("dve", "act", "pool", "dve", "act")
    ftiles = {}

    def load(i):
        kc, c0 = items[i]
        n = min(CW, P - c0)
        f = fr.next()
        ftiles[i] = f
        k.dma(f[:, :n], src2d[kc * 128:(kc + 1) * 128, c0:c0 + n], f, True)

    load(0)
    for i, (kc, c0) in enumerate(items):
        n = min(CW, P - c0)
        if i + 1 < len(items):
            load(i + 1)
        f = ftiles.pop(i)
        b = br.next()
        copy(k, engs[i % len(engs)], b[:, :n], f[:, :n], [f], [b])
        k.dma(dst3d[:, kc, c0:c0 + n], b[:, :n], b, False)
    ph.end()


def gen_conv_multi(k, ph, specs):
    CW = 2048
    fr = ph.ring(3, [128, CW], F32)
    br = ph.ring(3, [128, CW], BF16)
    items = []
    for (src2d, dst3d, K_, P_) in specs:
        for kc in range(K_ // 128):
            for c0 in range(0, P_, CW):
                items.append((src2d, dst3d, kc, c0, min(CW, P_ - c0)))
    yield len(items)
    engs = ("pool", "dve", "pool", "act")
    ft = {}

    def load(i):
        src2d, dst3d, kc, c0, n = items[i]
        f = fr.next()
        ft[i] = f
        k.dma(f[:, :n], src2d[kc * 128:(kc + 1) * 128, c0:c0 + n], f, True)

    load(0)
    for i, (src2d, dst3d, kc, c0, n) in enumerate(items):
        if i + 1 < len(items):
            load(i + 1)
        f = ft.pop(i)
        b = br.next()
        copy(k, engs[i % len(engs)], b[:, :n], f[:, :n], [f], [b])
        k.dma(dst3d[:, kc, c0:c0 + n], b[:, :n], b, False)
        yield


def phase_prep(k, c, xsrc, gvec, T, hT=None, yout=None, hTr=None):
    ph = Phase(k, "pp")
    gb = ph.sb([128, D], F32)
    k.dma(gb[:, :], gvec.partition_broadcast(128), gb, True)
    ident = ph.sb([128, 128], BF16)
    k.dma(ident[:, :], c.ident[:, :], ident, True)
    xr = ph.ring(3, [128, D], F32)
    ssr = ph.ring(3, [128, 4], F32)
    epsb = ph.sb([128, 1], F32)
    k.op("pool", lambda e: e.memset(epsb[:, :], EPS), [], [epsb])
    if hT is not None:
        hbr = ph.ring(2, [128, D], BF16)
        hts = ph.ring(2, [128, KC, 512], BF16)
        tpr = ph.ring(4, [128, 4, 128], BF16, psum=True)
        if hTr is not None:
            jrev = ph.sb([128, 128], BF16)
            k.dma(jrev[:, :], c.jrev[:, :], jrev, True)
            htrs = ph.ring(2, [128, KC, 512], BF16)
            rpr = ph.ring(3, [128, 4, 128], F32, psum=True)
    else:
        junk = ph.sb([128, D], BF16)
        yr = ph.ring(2, [128, D], F32)
    nt = T // 128
    xt = {}

    def load(i):
        x = xr.next()
        xt[i] = x
        k.dma(x[:, :], xsrc[i * 128:(i + 1) * 128, :], x, True)

    load(0)
    ev = 0
    for i in range(nt):
        if i + 1 < nt:
            load(i + 1)
        x = xt.pop(i)
        ss = ssr.next()
        if hT is not None:
            hb = hbr.next()
            sq = hb
        else:
            sq = junk
        act(k, sq[:, :], x[:, :], AF.Square, [x], [sq, ss], accum=ss[:, 0:1])
        act(k, ss[:, 1:2], ss[:, 0:1], AF.Ln, [ss, epsb], [ss], bias=epsb[:, 0:1], scale=1.0 / D)
        act(k, ss[:, 2:3], ss[:, 1:2], AF.Exp, [ss], [ss], scale=-0.5)
        if hT is not None:
            stt(k, hb[:, :], x[:, :], ss[:, 2:3], gb[:, :], ALU.mult, ALU.mult, [x, ss, gb], [hb])
            j = i % 4
            if j == 0:
                ht = hts.next()
            for q in range(4):
                tp = tpr.next()
                for cc in range(4):
                    tr(k, tp[:, cc, :], hb[:, (4 * q + cc) * 128:(4 * q + cc + 1) * 128], ident[:, :],
                       [hb, ident], [tp])
                copy(k, "act" if ev % 2 == 0 else "dve", ht[:, 4 * q:4 * q + 4, j * 128:(j + 1) * 128],
                     tp[:, :, :], [tp], [ht])
                ev += 1
            if hTr is not None:
                if j == 0:
                    htr = htrs.next()
                for q in range(4):
                    rp = rpr.next()
                    for cc in range(4):
                        mm(k, rp[:, cc, :], hb[:, (4 * q + cc) * 128:(4 * q + cc + 1) * 128], jrev[:, :], True, True,
                           [hb, jrev], [rp])
                    copy(k, "act" if ev % 2 == 0 else "dve", htr[:, 4 * q:4 * q + 4, (3 - j) * 128:(4 - j) * 128],
                         rp[:, :, :], [rp], [htr])
                    ev += 1
            if j == 3:
                g4 = i // 4
                k.dma(hT[:, :, g4 * 512:(g4 + 1) * 512], ht[:, :, :], ht, False)
                if hTr is not None:
                    gr = nt // 4 - 1 - g4
                    k.dma(hTr[:, :, gr * 512:(gr + 1) * 512], htr[:, :, :], htr, False)
        else:
            y = yr.next()
            stt(k, y[:, :], x[:, :], ss[:, 2:3], gb[:, :], ALU.mult, ALU.mult, [x, ss, gb], [y])
            k.dma(yout[i * 128:(i + 1) * 128, :], y[:, :], y, False)
    ph.end()


def gen_proj(k, ph, c, hT, wbf, jobs, T):
    TT = 1024
    hts = ph.sb([128, KC, TT], BF16)
    wr = ph.ring(2, [128, KC, 512], BF16)
    pr = ph.ring(6, [128, 512], F32, psum=True)
    ofr = ph.ring(3, [128, 512], F32)
    obr = ph.ring(4, [128, 512], BF16)
    items = []
    for (mode, c0, ncols, dst, kind) in jobs:
        for blk in range(ncols // 512):
            items.append((mode, c0 + blk * 512, blk, dst, kind))
    yield (T // TT) * len(items) * 8
    ev = [0]
    pend = []

    def evac(ps, kind, dst_ap):
        if kind == "f32":
            o = ofr.next()
        else:
            o = obr.next()
        if kind == "silu":
            act(k, o[:, :], ps[:, :], AF.Silu, [ps], [o])
        else:
            copy(k, "act" if ev[0] % 2 == 0 else "dve", o[:, :], ps[:, :], [ps], [o])
            ev[0] += 1
        k.dma(dst_ap, o[:, :], o, False)

    def drain(keep):
        while len(pend) > keep:
            evac(*pend.pop(0))

    for st in range(T // TT):
        k.dma(hts[:, :, :], hT[:, :, st * TT:(st + 1) * TT], hts, True)
        wt = {}

        def loadw(i):
            w = wr.next()
            wt[i] = w
            k.dma(w[:, :, :], wbf[:, :, items[i][1]:items[i][1] + 512], w, True)

        loadw(0)
        for i, (mode, cc, blk, dst, kind) in enumerate(items):
            if i + 1 < len(items):
                loadw(i + 1)
            w = wt.pop(i)
            if mode == "FM":
                for j in range(4):
                    for tc in range(TT // 512):
                        ps = pr.next()
                        for kc in range(KC):
                            mm(k, ps[:, :], w[:, kc, j * 128:(j + 1) * 128], hts[:, kc, tc * 512:(tc + 1) * 512],
                               kc == 0, kc == KC - 1, [w, hts], [ps])
                        t0 = st * TT + tc * 512
                        pend.append((ps, kind, dst[blk * 4 + j, :, t0:t0 + 512]))
                        drain(DBG_EV)
                        yield
            else:
                for sub in range(TT // 128):
                    ps = pr.next()
                    for kc in range(KC):
                        mm(k, ps[:, :], hts[:, kc, sub * 128:(sub + 1) * 128], w[:, kc, :],
                           kc == 0, kc == KC - 1, [w, hts], [ps])
                    t0 = st * TT + sub * 128
                    pend.append((ps, kind, dst[t0:t0 + 128, blk * 512:(blk + 1) * 512]))
                    drain(DBG_EV)
                    yield
        drain(0)


def phase_proj(k, c, hT, wbf, jobs, T):
    co_run(k, "pj", [lambda ph: gen_proj(k, ph, c, hT, wbf, jobs, T)])


def phase_memkv(k, c, memsrc, gvec, wkv, kmT, vmx):
    ph = Phase(k, "mk")
    gb = ph.sb([128, D], F32)
    k.dma(gb[:, :], gvec.partition_broadcast(128), gb, True)
    ident = ph.sb([128, 128], BF16)
    k.dma(ident[:, :], c.ident[:, :], ident, True)
    epsb = ph.sb([128, 1], F32)
    k.op("pool", lambda e: e.memset(epsb[:, :], EPS), [], [epsb])
    w = ph.sb([128, KC, 2048], BF16)
    k.dma(w[:, :, :], wkv[:, :, :], w, True)
    hmT = ph.sb([128, KC, NMEM], BF16)
    xr = ph.ring(2, [128, D], F32)
    hbr = ph.ring(2, [128, D], BF16)
    ssr = ph.ring(2, [128, 4], F32)
    tpr = ph.ring(2, [128, 4, 128], BF16, psum=True)
    pr = ph.ring(2, [128, 512], F32, psum=True)
    for i in range(2):
        x = xr.next()
        k.dma(x[:, :], memsrc[i * 128:(i + 1) * 128, :], x, True)
        ss = ssr.next()
        hb = hbr.next()
        act(k, hb[:, :], x[:, :], AF.Square, [x], [hb, ss], accum=ss[:, 0:1])
        act(k, ss[:, 1:2], ss[:, 0:1], AF.Ln, [ss, epsb], [ss], bias=epsb[:, 0:1], scale=1.0 / D)
        act(k, ss[:, 2:3], ss[:, 1:2], AF.Exp, [ss], [ss], scale=-0.5)
        stt(k, hb[:, :], x[:, :], ss[:, 2:3], gb[:, :], ALU.mult, ALU.mult, [x, ss, gb], [hb])
        for q in range(4):
            tp = tpr.next()
            for cc in range(4):
                tr(k, tp[:, cc, :], hb[:, (4 * q + cc) * 128:(4 * q + cc + 1) * 128], ident[:, :], [hb, ident], [tp])
            copy(k, "dve", hmT[:, 4 * q:4 * q + 4, i * 128:(i + 1) * 128], tp[:, :, :], [tp], [hmT])
    kms = ph.sb([128, 8, NMEM], BF16)
    for j in range(8):
        ps = pr.next()
        for kc in range(KC):
            mm(k, ps[:, 0:NMEM], w[:, kc, j * 128:(j + 1) * 128], hmT[:, kc, :], kc == 0, kc == KC - 1, [w, hmT], [ps])
        copy(k, "act", kms[:, j, :], ps[:, 0:NMEM], [ps], [kms])
    k.dma(kmT[:, :, :], kms[:, :, :], kms, False)
    vms = ph.sb([128, 2, 4, 257], BF16)
    k.op("pool", lambda e: e.memset(vms[:, :, :, :], 1.0), [], [vms])
    for mt in range(2):
        for cb in range(2):
            ps = pr.next()
            for kc in range(KC):
                mm(k, ps[:, :], hmT[:, kc, mt * 128:(mt + 1) * 128], w[:, kc, 1024 + cb * 512:1024 + (cb + 1) * 512],
                   kc == 0, kc == KC - 1, [w, hmT], [ps])
            copy(k, "dve", vms[:, mt, 2 * cb:2 * cb + 2, 0:256], ps[:, :].rearrange("p (h d) -> p h d", h=2), [ps], [vms])
    k.dma(vmx[:, :, :, :], vms[:, :, :, :], vms, False)
    ph.end()


def bc_mid(ap2d, n):
    a = ap2d.ap
    return bass.AP(ap2d.tensor, ap2d.offset, [list(a[0]), [0, n], list(a[1])])


def phase_mematt(k, c, qmT, kmT, vmx, sg, om, T):
    ph = Phase(k, "ma")
    kms = ph.sb([128, 8, NMEM], BF16)
    k.dma(kms[:, :, :], kmT[:, :, :], kms, True)
    vms = ph.sb([128, 2, 4, 257], BF16)
    k.dma(vms[:, :, :, :], vmx[:, :, :, :], vms, True)
    qr = ph.ring(2, [128, 8, 512], BF16)
    sgr = ph.ring(2, [128, 4, 1024], BF16)
    ptr = ph.ring(2, [128, 2, 512], BF16)
    pss = ph.ring(3, [128, 512], F32, psum=True)
    pso = ph.ring(3, [128, 512], F32, psum=True)
    rdr = ph.ring(4, [128, 2], F32)
    otr = ph.ring(8, [128, 1024], BF16)
    n5 = T // 512
    tl = {}

    def load(i):
        q = qr.next()
        g = sgr.next()
        tl[i] = (q, g)
        k.dma(q[:, :, :], qmT[:, :, i * 512:(i + 1) * 512].rearrange("j p t -> p j t"), q, True)
        k.dma(g[:, :, :], sg[i * 512:(i + 1) * 512, 3072:4096].rearrange("(s p) d -> p s d", p=128), g, True)

    load(0)
    for i in range(n5):
        if i + 1 < n5:
            load(i + 1)
        q, g = tl.pop(i)
        ots = [otr.next() for _ in range(4)]
        for hm in range(4):
            pt = ptr.next()
            for mt in range(2):
                ps = pss.next()
                for dc in range(2):
                    mm(k, ps[:, :], kms[:, 2 * hm + dc, mt * 128:(mt + 1) * 128], q[:, 2 * hm + dc, :],
                       dc == 0, dc == 1, [kms, q], [ps])
                act(k, pt[:, mt, :], ps[:, :], AF.Exp, [ps], [pt], scale=1.0 / 16.0)
            for sub in range(4):
                po = pso.next()
                for mt in range(2):
                    mm(k, po[:, 0:257], pt[:, mt, sub * 128:(sub + 1) * 128], vms[:, mt, hm, :],
                       mt == 0, mt == 1, [pt, vms], [po])
                rd = rdr.next()
                k.op("dve", lambda e, rd=rd, po=po: e.reciprocal(out=rd[:, 0:1], in_=po[:, 256:257]), [po], [rd])
                stt(k, ots[sub][:, hm * 256:(hm + 1) * 256], po[:, 0:256], rd[:, 0:1],
                    g[:, sub, hm * 256:(hm + 1) * 256], ALU.mult, ALU.mult, [po, rd, g], [ots[sub]])
        for sub in range(4):
            t0 = i * 512 + sub * 128
            k.dma(om[t0:t0 + 128, :], ots[sub][:, :], ots[sub], False)
    ph.end()


def gen_chain(k, ph, c, zsrc, qoff, foff, lbD, cbD, lcol0, qtD, ktD, khD, decD, T):
    lbs = ph.sb([128, 48], F32)
    k.dma(lbs[:, :], lbD[:, :], lbs, True)
    cbs = ph.sb([128, 48], F32)
    k.dma(cbs[:, :], cbD[:, :], cbs, True)
    one = ph.sb([128, 64], F32)
    k.dma(one[:, :], c.ones[:, :], one, True)
    msk = ph.sb([128, 512], F32)
    k.dma(msk[:, :], c.cmask[:, :], msk, True)
    ident = ph.sb([128, 128], BF16)
    k.dma(ident[:, :], c.ident[:, :], ident, True)
    zr = ph.ring(6, [128, 512], F32)
    names = ("eq", "Lq", "e", "L1", "L2", "lf", "G", "t", "u", "w", "v1", "E")
    R = {n: ph.ring(2, [128, 512], F32) for n in names}
    qtr = ph.ring(2, [128, 512], BF16)
    ktr = ph.ring(2, [128, 512], BF16)
    khr = ph.ring(4, [128, 512], BF16)
    khs = ph.ring(2, [128, 4, 128], BF16)
    tpr = ph.ring(1, [128, 4, 128], BF16, psum=True)
    nch = T // 32
    decs = ph.sb([128, NH, nch], F32)
    items = [(h, tc) for h in range(NH) for tc in range(T // 512)]
    yield 3 * (len(items) + 2)
    khq = {}
    zt = {}
    tmp = {}
    sc = float(128 ** -0.5)
    o1 = one[:, 0:1]

    def load(i):
        h, tc = items[i]
        zq = zr.next()
        zf = zr.next()
        zt[i] = (zq, zf)
        k.dma(zq[:, :], zsrc[qoff + h, :, tc * 512:(tc + 1) * 512], zq, True)
        k.dma(zf[:, :], zsrc[foff + h, :, tc * 512:(tc + 1) * 512], zf, True)

    def stage1(i):
        h, tc = items[i]
        zq, zf = zt[i]
        col = lcol0 + h
        t = {n: R[n].next() for n in names}
        tmp[i] = t
        act(k, t["eq"][:, :], zq[:, :], AF.Exp, [zq], [t["eq"]], scale=-1.0)
        act(k, t["e"][:, :], zf[:, :], AF.Exp, [zf], [t["e"]], scale=-1.0)
        act(k, t["L1"][:, :], t["e"][:, :], AF.Ln, [t["e"], one], [t["L1"]], bias=o1)
        act(k, t["L2"][:, :], t["e"][:, :], AF.Ln, [t["e"], one, lbs], [t["L2"]], bias=o1, scale=lbs[:, col:col + 1])
        act(k, t["Lq"][:, :], t["eq"][:, :], AF.Ln, [t["eq"], one], [t["Lq"]], bias=o1)
        tt(k, "dve", t["lf"][:, :], t["L2"][:, :], t["L1"][:, :], ALU.subtract, [t["L2"], t["L1"]], [t["lf"]])
        G = t["G"]
        k.op("dve", lambda e, G=G, lf=t["lf"]: e.tensor_tensor_scan(
            out=G[:, :], data0=msk[:, :], data1=lf[:, :], initial=0.0, op0=ALU.mult, op1=ALU.add), [msk, t["lf"]], [G])
        tt(k, "pool", t["t"][:, :], zf[:, :], t["L1"][:, :], ALU.add, [zf, t["L1"]], [t["t"]])
        tt(k, "dve", t["u"][:, :], t["t"][:, :], G[:, :], ALU.add, [t["t"], G], [t["u"]])
        tt(k, "pool", t["v1"][:, :], G[:, :], t["Lq"][:, :], ALU.subtract, [G, t["Lq"]], [t["v1"]])

    def stage2(i):
        h, tc = items[i]
        zq, zf = zt.pop(i)
        col = lcol0 + h
        t = tmp.pop(i)
        G = t["G"]
        kt = ktr.next()
        act(k, kt[:, :], t["u"][:, :], AF.Exp, [t["u"], cbs], [kt], bias=cbs[:, col:col + 1], scale=-1.0)
        act(k, t["E"][:, :], t["v1"][:, :], AF.Exp, [t["v1"]], [t["E"]])
        G3 = G[:, :].rearrange("p (c j) -> p c j", j=32)
        glast = G3[:, :, 31:32]
        ga = glast.ap
        glb = bass.AP(glast.tensor, glast.offset, [list(ga[0]), list(ga[1]), [0, 32]])
        tt(k, "dve", t["w"][:, :].rearrange("p (c j) -> p c j", j=32), t["u"][:, :].rearrange("p (c j) -> p c j", j=32),
           glb, ALU.subtract, [t["u"], G], [t["w"]])
        kh = khr.next()
        act(k, kh[:, :], t["w"][:, :], AF.Exp, [t["w"], cbs], [kh], bias=cbs[:, col:col + 1], scale=-1.0)
        qt = qtr.next()
        stt(k, qt[:, :], zq[:, :], sc, t["E"][:, :], ALU.mult, ALU.mult, [zq, t["E"]], [qt])
        copy(k, "pool", decs[:, h, tc * 16:(tc + 1) * 16], G3[:, :, 31], [G], [decs])
        khq[i] = kh
        t0 = tc * 512
        k.dma(qtD[h, :, t0:t0 + 512], qt[:, :], qt, False)
        k.dma(ktD[h, :, t0:t0 + 512], kt[:, :], kt, False)

    def stage3(i):
        h, tc = items[i]
        kh = khq.pop(i)
        tp = tpr.next()
        for jj in range(4):
            tr(k, tp[:, jj, :], kh[:, jj * 128:(jj + 1) * 128], ident[:, :], [kh, ident], [tp])
        ks = khs.next()
        copy(k, "dve", ks[:, :, :], tp[:, :, :], [tp], [ks])
        t0 = tc * 512
        k.dma(khD[t0:t0 + 512, h * 128:(h + 1) * 128].rearrange("(j p) d -> p j d", p=128), ks[:, :, :], ks, False)

    n = len(items)
    load(0)
    if n > 1:
        load(1)
    stage1(0)
    for i in range(n + 2):
        if i + 2 < n:
            load(i + 2)
        if i + 1 < n:
            stage1(i + 1)
        yield
        if i < n:
            stage2(i)
        yield
        if 0 <= i - DBG_S3 < n:
            stage3(i - DBG_S3)
        yield
    k.dma(decD[:, :, :], decs[:, :, :], decs, False)


def phase_chain(k, c, *a):
    co_run(k, "ch", [lambda ph: gen_chain(k, ph, c, *a)])


def gen_mix(k, ph, c, qtD, ktD, khD, vD, decD, oD, T):
    msk = ph.sb([128, 128], F32)
    k.dma(msk[:, :], c.trimask[:, :], msk, True)
    nch = T // 32
    SEG = 512
    NB = SEG // 128
    NCH = SEG // 32
    nseg = T // SEG
    HP = NH // 2
    decr = ph.ring(4, [128, nch], F32)
    dexr = ph.ring(4, [128, NCH + 1], F32)
    qr = ph.ring(4, [128, SEG], BF16)
    kr = ph.ring(4, [128, SEG], BF16)
    hr = ph.ring(4, [128, NB, 128], BF16)
    vr = ph.ring(4, [128, NB, 128], BF16)
    dsx = ph.ring(4, [128, NCH + 1, 128], F32)
    sal = ph.ring(4, [128, NCH + 1, 128], F32)
    sbr = ph.ring(4, [128, NCH, 128], BF16)
    atr = ph.ring(4, [128, 128], BF16)
    osr = ph.ring(4, [128, 128], F32)
    psd = [ph.ps([128, 512], F32) for _ in range(4)]
    par = ph.ring(1, [128, 512], F32, psum=True)
    por = ph.ring(2, [128, 512], F32, psum=True)
    items = [(p, sg_) for p in range(HP) for sg_ in range(nseg)]
    yield len(items)
    st = {}
    lane_dec = {}

    def prep(i):
        p, sg_ = items[i]
        t0 = sg_ * SEG
        lanes = []
        for ln in range(2):
            h = p + ln * HP
            q = qr.next(); kk = kr.next(); hh = hr.next(); vv = vr.next()
            k.dma(q[:, :], qtD[h, :, t0:t0 + SEG], q, True)
            k.dma(kk[:, :], ktD[h, :, t0:t0 + SEG], kk, True)
            k.dma(hh[:, :, :], khD[t0:t0 + SEG, h * 128:(h + 1) * 128].rearrange("(b p) d -> p b d", p=128), hh, True)
            k.dma(vv[:, :, :], vD[t0:t0 + SEG, h * 128:(h + 1) * 128].rearrange("(b p) d -> p b d", p=128), vv, True)
            if sg_ == 0:
                dec = decr.next()
                k.dma(dec[:, :], decD[:, h, :], dec, True)
                lane_dec[(p, ln)] = dec
            dec = lane_dec[(p, ln)]
            dx = dexr.next()
            act(k, dx[:, 1:NCH + 1], dec[:, sg_ * NCH:(sg_ + 1) * NCH], AF.Exp, [dec], [dx])
            ds = dsx.next()
            ds4 = ds[:, 1:NCH + 1, :].rearrange("p (b c) d -> p b c d", c=4)
            for b in range(NB):
                for c4 in range(4):
                    kw = {"tile_position": (96, 0)} if c4 == 3 else {}
                    mm(k, psd[c4][:, b * 128:(b + 1) * 128], hh[32 * c4:32 * c4 + 32, b, :],
                       vv[32 * c4:32 * c4 + 32, b, :], True, True, [hh, vv], [psd[c4]], **kw)
            for c4 in range(4):
                copy(k, "act", ds4[:, :, c4, :], psd[c4][:, :].rearrange("p (b d) -> p b d", d=128), [psd[c4]], [ds])
            lanes.append((h, q, kk, hh, vv, dx, ds))
        st[i] = lanes

    prevS = {}
    prep(0)
    for i, (p, sg_) in enumerate(items):
        if i + 1 < len(items):
            prep(i + 1)
        lanes = st.pop(i)
        Ss = []
        for ln in range(2):
            S = sal.next()
            if sg_ == 0:
                k.op("pool", lambda e, S=S: e.memset(S[:, 0, :], 0.0), [], [S])
            else:
                copy(k, "pool", S[:, 0, :], prevS[ln][:, NCH, :], [prevS[ln]], [S])
            Ss.append(S)
        for ci in range(NCH):
            for ln in range(2):
                S = Ss[ln]
                dx, ds = lanes[ln][5], lanes[ln][6]
                stt(k, S[:, ci + 1, :], S[:, ci, :], dx[:, ci + 1:ci + 2], ds[:, ci + 1, :], ALU.mult, ALU.add,
                    [S, dx, ds], [S])
        Sbs = []
        for ln in range(2):
            Sb = sbr.next()
            copy(k, "act", Sb[:, :, :], Ss[ln][:, 0:NCH, :], [Ss[ln]], [Sb])
            Sbs.append(Sb)
            prevS[ln] = Ss[ln]
        for blk in range(NB):
            pas = []
            for ln in range(2):
                h, q, kk, hh, vv, dx, ds = lanes[ln]
                pa = par.next()
                mm(k, pa[:, 0:128], kk[:, blk * 128:(blk + 1) * 128], q[:, blk * 128:(blk + 1) * 128], True, True,
                   [kk, q], [pa])
                at = atr.next()
                tt(k, "dve", at[:, :], pa[:, 0:128], msk[:, :], ALU.mult, [pa, msk], [at])
                pas.append(at)
            for ln in range(2):
                h, q, kk, hh, vv, dx, ds = lanes[ln]
                at = pas[ln]
                po = por.next()
                mm(k, po[:, 0:128], at[:, :], vv[:, blk, :], True, False, [at, vv], [po])
                for c4 in range(4):
                    kw = {"tile_position": (0, 96)} if c4 == 3 else {}
                    mm(k, po[32 * c4:32 * c4 + 32, 0:128], q[:, blk * 128 + 32 * c4:blk * 128 + 32 * c4 + 32],
                       Sbs[ln][:, blk * 4 + c4, :], False, c4 == 3, [q, Sbs[ln]], [po], **kw)
                os_ = osr.next()
                copy(k, "act", os_[:, :], po[:, 0:128], [po], [os_])
                t0 = sg_ * SEG + blk * 128
                k.dma(oD[t0:t0 + 128, h * 128:(h + 1) * 128], os_[:, :], os_, False)
        yield


def phase_mix(k, c, *a):
    co_run(k, "mx", [lambda ph: gen_mix(k, ph, c, *a)])


def phase_outn(k, c, oF, oBr, sg, ongv, ob, T):
    ph = Phase(k, "on")
    jf = ph.sb([128, 128], F32)
    k.dma(jf[:, :], c.jrevf[:, :], jf, True)
    ong = ph.sb([128, TOK_W], F32)
    k.dma(ong[:, :], ongv.partition_broadcast(128), ong, True)
    epsb = ph.sb([128, 1], F32)
    k.op("pool", lambda e: e.memset(epsb[:, :], EPS), [], [epsb])
    ofr = ph.ring(2, [128, TOK_W], F32)
    obr = ph.ring(2, [128, TOK_W], F32)
    sgr = ph.ring(2, [128, TOK_W], BF16)
    sqr = ph.ring(1, [128, TOK_W], F32)
    gsr = ph.ring(1, [128, TOK_W], F32)
    outr = ph.ring(2, [128, TOK_W], BF16)
    ssr = ph.ring(2, [128, 3, NH], F32)
    pr = ph.ring(6, [128, 512], F32, psum=True)
    nt = T // 128
    tl = {}

    def load(i):
        a = ofr.next(); b = obr.next(); g = sgr.next()
        tl[i] = (a, b, g)
        ir = nt - 1 - i
        k.dma(a[:, :], oF[i * 128:(i + 1) * 128, :], a, True)
        k.dma(b[:, :], oBr[ir * 128:(ir + 1) * 128, :], b, True)
        k.dma(g[:, :], sg[i * 128:(i + 1) * 128, 0:TOK_W], g, True)

    load(0)
    for i in range(nt):
        if i + 1 < nt:
            load(i + 1)
        a, b, g = tl.pop(i)
        for cb in range(6):
            ps = pr.next()
            mm(k, ps[:, :], jf[:, :], b[:, cb * 512:(cb + 1) * 512], True, True, [jf, b], [ps])
            tt(k, "dve", a[:, cb * 512:(cb + 1) * 512], a[:, cb * 512:(cb + 1) * 512], ps[:, :], ALU.add, [a, ps], [a])
        sq = sqr.next()
        tt(k, "pool", sq[:, :], a[:, :], a[:, :], ALU.mult, [a], [sq])
        ss = ssr.next()
        k.op("dve", lambda e, ss=ss, sq=sq: e.tensor_reduce(
            out=ss[:, 0, :], in_=sq[:, :].rearrange("p (h d) -> p h d", d=128), axis=AX.X, op=ALU.add), [sq], [ss])
        act(k, ss[:, 1, :], ss[:, 0, :], AF.Ln, [ss, epsb], [ss], bias=epsb[:, 0:1], scale=1.0 / 128)
        act(k, ss[:, 2, :], ss[:, 1, :], AF.Exp, [ss], [ss], scale=-0.5)
        gs = gsr.next()
        tt(k, "pool", gs[:, :], g[:, :], ong[:, :], ALU.mult, [g, ong], [gs])
        r2 = ss[:, 2, :]
        ra = r2.ap
        rb = bass.AP(r2.tensor, r2.offset, [list(ra[0]), list(ra[1]), [0, 128]])
        tt(k, "dve", sq[:, :].rearrange("p (h d) -> p h d", d=128), a[:, :].rearrange("p (h d) -> p h d", d=128), rb,
           ALU.mult, [a, ss], [sq])
        o = outr.next()
        tt(k, "dve", o[:, :], sq[:, :], gs[:, :], ALU.mult, [sq, gs], [o])
        k.dma(ob[i * 128:(i + 1) * 128, :], o[:, :], o, False)
    ph.end()


def phase_outw(k, c, ob, om, wo, xsrc, xdst, T):
    ph = Phase(k, "ow")
    ident = ph.sb([128, 128], BF16)
    k.dma(ident[:, :], c.ident[:, :], ident, True)
    TTo = 512
    obr = ph.ring(2, [128, MIX_W], BF16)
    otr = ph.ring(2, [128, 32, TTo], BF16)
    wr = ph.ring(2, [128, 32, 512], BF16)
    xpr = ph.ring(3, [128, 512], F32)
    xor_ = ph.ring(3, [128, 512], F32)
    tpr = ph.ring(3, [128, 4, 128], BF16, psum=True)
    pr = ph.ring(4, [128, 512], F32, psum=True)
    nst = T // TTo
    ev = 0
    items = [(st, db) for st in range(nst) for db in range(4)]
    wt = {}

    def loadw(i):
        w = wr.next()
        wt[i] = w
        db = items[i][1]
        k.dma(w[:, :, :], wo[:, :, db * 512:(db + 1) * 512], w, True)

    loadw(0)
    oT = None
    for i, (st, db) in enumerate(items):
        if db == 0:
            oT = otr.next()
            for sub in range(4):
                t0 = st * TTo + sub * 128
                o = obr.next()
                k.dma(o[:, 0:TOK_W], ob[t0:t0 + 128, :], o, True)
                k.dma(o[:, TOK_W:MIX_W], om[t0:t0 + 128, :], o, True)
                for q in range(8):
                    tp = tpr.next()
                    for cc in range(4):
                        tr(k, tp[:, cc, :], o[:, (4 * q + cc) * 128:(4 * q + cc + 1) * 128], ident[:, :], [o, ident], [tp])
                    copy(k, "act" if ev % 2 == 0 else "dve", oT[:, 4 * q:4 * q + 4, sub * 128:(sub + 1) * 128],
                         tp[:, :, :], [tp], [oT])
                    ev += 1
        if i + 1 < len(items):
            loadw(i + 1)
        w = wt.pop(i)
        for sub in range(4):
            t0 = st * TTo + sub * 128
            xp = xpr.next()
            k.dma(xp[:, :], xsrc[t0:t0 + 128, db * 512:(db + 1) * 512], xp, True)
            ps = pr.next()
            for kc in range(32):
                mm(k, ps[:, :], oT[:, kc, sub * 128:(sub + 1) * 128], w[:, kc, :], kc == 0, kc == 31, [oT, w], [ps])
            xo = xor_.next()
            tt(k, "dve", xo[:, :], ps[:, :], xp[:, :], ALU.add, [ps, xp], [xo])
            k.dma(xdst[t0:t0 + 128, db * 512:(db + 1) * 512], xo[:, :], xo, False)
    ph.end()


def na_ranges(rows):
    rs = lambda r: min(max(r - 4, 0), rows - 8)
    out = []
    for j in range(rows):
        al = [r for r in range(rows) if rs(r) <= j <= rs(r) + 7]
        assert al == list(range(al[0], al[-1] + 1))
        out.append((al[0], al[-1]))
    return out


def phase_na(k, c, fm16, vD, sg, btab, ob, T):
    ph = Phase(k, "na")
    rows = T // GW
    nt = T // 128
    rng_ = na_ranges(rows)
    cmk = ph.sb([128, GW], F32)
    k.dma(cmk[0:64, :], c.colmask[:, :], cmk, True)
    k.dma(cmk[64:128, :], c.colmask[:, :], cmk, True)
    onesb = ph.sb([128, 64], F32)
    k.dma(onesb[:, :], c.ones[:, :], onesb, True)
    qr = ph.ring(2, [128, T], BF16)
    kr = ph.ring(2, [128, T], BF16)
    vr = ph.ring(2, [128, nt, 129], BF16)
    for v in vr.tiles:
        copy(k, "dve", v[:, :, 128:129], onesb[:, 0:nt].rearrange("p (a b) -> p a b", b=1), [onesb], [v])
    sgr = ph.ring(2, [128, nt, 128], BF16)
    obr = ph.ring(2, [128, nt, 128], BF16)
    crf = ph.ring(2, [128, 15, GW], F32)
    crb = ph.ring(2, [128, 15, GW], BF16)
    exr = ph.ring(2, [128, 768], BF16)
    ptr = ph.ring(8, [128, 768], BF16)
    pss = ph.ring(2, [128, 1024], F32, psum=True)
    pso = ph.ring(2, [128, 512], F32, psum=True)
    rdr = ph.ring(4, [128, 2], F32)
    info = []
    for kt in range(nt):
        (a0, b0), (a1, b1) = rng_[2 * kt], rng_[2 * kt + 1]
        qlo = (min(a0, a1) // 2) * 2
        qhi = (max(b0, b1) // 2) * 2 + 1
        info.append((qlo, qhi, (a0, b0), (a1, b1)))
    kts_of = {}
    for kt in range(nt):
        qlo, qhi = info[kt][0], info[kt][1]
        for R in range(qlo // 2, qhi // 2 + 1):
            kts_of.setdefault(R, []).append(kt)
    for R, l in kts_of.items():
        assert l[-1] - l[0] <= 6, (R, l)
    sc = float(128 ** -0.5)
    tl = {}

    def load(h):
        q = qr.next(); kk = kr.next(); v = vr.next(); g = sgr.next(); cf = crf.next()
        tl[h] = (q, kk, v, g, cf)
        k.dma(q[:, :], fm16[h, :, :], q, True)
        k.dma(kk[:, :], fm16[NH + h, :, :], kk, True)
        k.dma(v[:, :, 0:128], vD[:, h * 128:(h + 1) * 128].rearrange("(b p) d -> p b d", p=128), v, True)
        k.dma(g[:, :, :], sg[:, h * 128:(h + 1) * 128].rearrange("(b p) d -> p b d", p=128), g, True)
        k.dma(cf[0:64, :, :], btab[h], cf, True)
        k.dma(cf[64:128, :, :], btab[h], cf, True)

    def finalize(kt_done, pts, v, g, obh):
        for R in sorted(kts_of):
            l = kts_of[R]
            if l[-1] != kt_done:
                continue
            po = pso.next()
            for n_, k2 in enumerate(l):
                off = (R - info[k2][0] // 2) * 128
                mm(k, po[:, 0:129], pts[k2][:, off:off + 128], v[:, k2, :], n_ == 0, n_ == len(l) - 1,
                   [pts[k2], v], [po])
            rd = rdr.next()
            k.op("dve", lambda e, rd=rd, po=po: e.reciprocal(out=rd[:, 0:1], in_=po[:, 128:129]), [po], [rd])
            stt(k, obh[:, R, :], po[:, 0:128], rd[:, 0:1], g[:, R, :], ALU.mult, ALU.mult, [po, rd, g], [obh])

    load(0)
    for h in range(NH):
        if h + 1 < NH:
            load(h + 1)
        q, kk, v, g, cf = tl.pop(h)
        act(k, cf[:, :, :], cf[:, :, :], AF.Exp, [cf], [cf])
        cb = crb.next()
        tt(k, "dve", cb[:, :, :], cf[:, :, :], bc_mid(cmk[:, :], 15), ALU.mult, [cf, cmk], [cb])
        obh = obr.next()
        pts = {}
        for kt in range(nt):
            qlo, qhi, r0, r1 = info[kt]
            nq = (qhi - qlo + 1) * GW
            ps = pss.next()
            n1 = min(512, nq)
            mm(k, ps[:, 0:n1], kk[:, kt * 128:(kt + 1) * 128], q[:, qlo * GW:qlo * GW + n1], True, True, [kk, q], [ps])
            if nq > 512:
                mm(k, ps[:, 512:nq], kk[:, kt * 128:(kt + 1) * 128], q[:, qlo * GW + 512:qlo * GW + nq], True, True,
                   [kk, q], [ps])
            ex = exr.next()
            act(k, ex[:, 0:nq], ps[:, 0:nq], AF.Exp, [ps], [ex], scale=sc)
            pt = ptr.next()
            pts[kt] = pt
            k.op("pool", lambda e, pt=pt, nq=nq: e.memset(pt[:, 0:nq], 0.0), [], [pt])
            for hf, (rlo, rhi) in enumerate((r0, r1)):
                j = 2 * kt + hf
                c0 = (rlo - qlo) * GW
                nr = rhi - rlo + 1
                d0 = 7 - j + rlo
                assert 0 <= d0 and d0 + nr <= 15
                tt(k, "dve", pt[64 * hf:64 * hf + 64, c0:c0 + nr * GW], ex[64 * hf:64 * hf + 64, c0:c0 + nr * GW],
                   cb[64 * hf:64 * hf + 64, d0:d0 + nr, :].rearrange("p a b -> p (a b)"), ALU.mult, [ex, cb], [pt])
            if kt >= 1:
                finalize(kt - 1, pts, v, g, obh)
        finalize(nt - 1, pts, v, g, obh)
        k.dma(ob[:, h * 128:(h + 1) * 128].rearrange("(b p) d -> p b d", p=128), obh[:, :, :], obh, False)
    ph.end()


def phase_lb(k, c, lblT, lbD, cbD):
    ph = Phase(k, "lb")
    l0 = ph.sb([128, 48], F32)
    l1 = ph.sb([128, 48], F32)
    one = ph.sb([128, 64], F32)
    k.dma(l0[:, :], lblT[0], l0, True)
    k.dma(l1[:, :], lblT[1], l1, True)
    k.dma(one[:, :], c.ones[:, :], one, True)
    d = ph.sb([128, 48], F32)
    e = ph.sb([128, 48], F32)
    cb = ph.sb([128, 48], F32)
    lb = ph.sb([128, 48], F32)
    tt(k, "dve", d[:, :], l1[:, :], l0[:, :], ALU.subtract, [l0, l1], [d])
    act(k, e[:, :], d[:, :], AF.Exp, [d], [e])
    act(k, cb[:, :], e[:, :], AF.Ln, [e, one], [cb], bias=one[:, 0:1])
    ts(k, "dve", cb[:, :], cb[:, :], -1.0, None, ALU.mult, None, [cb], [cb])
    tt(k, "dve", d[:, :], d[:, :], cb[:, :], ALU.add, [d, cb], [d])
    act(k, lb[:, :], d[:, :], AF.Exp, [d], [lb])
    k.dma(lbD[:, :], lb[:, :], lb, False)
    k.dma(cbD[:, :], cb[:, :], cb, False)
    ph.end()


def build(T, NS, depth, debug=()):
    nc = bass.Bass("TRN2", target_bir_lowering=False)
    c = Cfg()
    c.T = T

    def din(name, shape, dt=F32):
        return nc.dram_tensor(name, list(shape), dt, kind="ExternalInput").ap()

    def dscr(name, shape, dt):
        kind = "ExternalOutput" if name in debug else "Internal"
        return nc.dram_tensor(name, list(shape), dt, kind=kind).ap()

    x_in = din("x_in", [NS, T, D])
    mem_in = din("mem_in", [NS, NMEM, D])
    w_in_a = din("w_in_a", [2, D, P_A])
    w_in_b = din("w_in_b", [2, D, P_B])
    w_mem_kv = din("w_mem_kv", [4, D, 2 * MEM_W])
    w_out = din("w_out", [4, MIX_W, D])
    norm_g = din("norm_g", [4, D])
    mem_norm_g = din("mem_norm_g", [4, D])
    final_g = din("final_g", [1, D])
    onorm_g = din("hg_onorm_g", [2, TOK_W])
    lblT = din("lblT", [2, 128, 48])
    btab = din("btab", [2, NH, GW, 15, GW])
    c.ident = din("ident", [128, 128], BF16)
    c.jrev = din("jrev", [128, 128], BF16)
    c.jrevf = din("jrevf", [128, 128])
    c.ones = din("ones", [128, 64])
    c.zeros = din("zeros", [128, 128])
    c.cmask = din("cmask", [128, 512])
    c.trimask = din("trimask", [128, 128])
    c.colmask = din("colmask", [GW, GW])
    y = nc.dram_tensor("y", [NS, T, D], F32, kind="ExternalOutput").ap()

    hT = dscr("hT", [128, KC, T], BF16)
    hTr = dscr("hTr", [128, KC, T], BF16)
    wbf = [dscr("wbf%d" % l, [128, KC, P_A if l % 2 == 0 else P_B], BF16) for l in range(4)]
    wkv = [dscr("wkv%d" % l, [128, KC, 2 * MEM_W], BF16) for l in range(4)]
    wo = [dscr("wo%d" % l, [128, 32, D], BF16) for l in range(4)]
    xres = dscr("xres", [NS, T, D], F32)
    zA = dscr("zA", [48, 128, T], F32)
    fm16 = dscr("fm16", [48, 128, T], BF16)
    qmT = dscr("qmT", [8, 128, T], BF16)
    vD = dscr("vD", [T, TOK_W], BF16)
    sg = dscr("sg", [T, MIX_W], BF16)
    om = dscr("om", [T, MEM_W], BF16)
    ob = dscr("ob", [T, TOK_W], BF16)
    kmT = dscr("kmT", [128, 8, NMEM], BF16)
    vmx = dscr("vmx", [128, 2, 4, 257], BF16)
    qtD = dscr("qtD", [NH, 128, T], BF16)
    ktD = dscr("ktD", [NH, 128, T], BF16)
    khD = dscr("khD", [T, TOK_W], BF16)
    decD = dscr("decD", [128, NH, T // 32], F32)
    zB = dscr("zB", [48, 128, T], F32)
    vD2 = dscr("vD2", [T, TOK_W], BF16)
    qtD2 = dscr("qtD2", [NH, 128, T], BF16)
    ktD2 = dscr("ktD2", [NH, 128, T], BF16)
    khD2 = dscr("khD2", [T, TOK_W], BF16)
    decD2 = dscr("decD2", [128, NH, T // 32], F32)
    oF = dscr("oF", [T, TOK_W], F32)
    oB = dscr("oB", [T, TOK_W], F32)
    lbD = [dscr("lbD%d" % a, [128, 48], F32) for a in range(2)]
    cbD = [dscr("cbD%d" % a, [128, 48], F32) for a in range(2)]

    with ExitStack() as es:
        k = KB(nc, es)
        def conv_specs(l):
            return [((w_in_a if l % 2 == 0 else w_in_b)[l // 2], wbf[l], D, P_A if l % 2 == 0 else P_B),
                    (w_mem_kv[l], wkv[l], D, 2 * MEM_W), (w_out[l], wo[l], MIX_W, D)]

        for sp in conv_specs(0):
            phase_conv(k, *sp)

        def proj_maybe_conv(l, s, hsrc, jobs):
            gens = [lambda ph: gen_proj(k, ph, c, hsrc, wbf[l], jobs, T)]
            if s == 0 and l + 1 < depth:
                gens.append(lambda ph: gen_conv_multi(k, ph, conv_specs(l + 1)))
            co_run(k, "pj", gens)
        ph = Phase(k, "z")
        zt = ph.sb([128, 48], F32)
        k.dma(zt[:, :], c.zeros[:, 0:48], zt, True)
        k.dma(lbD[0][:, :], zt[:, :], zt, False)
        k.dma(cbD[0][:, :], zt[:, :], zt, False)
        ph.end()
        phase_lb(k, c, lblT, lbD[1], cbD[1])
        for l in range(depth):
            for s in range(NS):
                xsrc = x_in[s] if l == 0 else xres[s]
                if l % 2 == 0:
                    a = l // 2
                    phase_prep(k, c, xsrc, norm_g[l:l + 1, :], T, hT=hT, hTr=hTr)
                    phase_memkv(k, c, mem_in[s], mem_norm_g[l:l + 1, :], wkv[l], kmT, vmx)
                    jobs_f1 = [("FM", 0, 3072, zA[0:24], "f32"), ("FM", 3072, 3072, zA[24:48], "f32")]
                    jobs_b = [("FM", 0, 3072, zB[0:24], "f32"), ("FM", 6144, 3072, zB[24:48], "f32"),
                              ("TM", 9216, 3072, vD2, "bf16")]
                    jobs_f2 = [("TM", 9216, 3072, vD, "bf16"), ("FM", 12288, 1024, qmT, "bf16"),
                               ("TM", 13312, 4096, sg, "silu")]
                    proj_maybe_conv(l, s, hT, jobs_f1)
                    co_run(k, "pc", [
                        lambda ph: gen_proj(k, ph, c, hTr, wbf[l], jobs_b, T),
                        lambda ph: gen_chain(k, ph, c, zA, 0, 24, lbD[a], cbD[a], 0, qtD, ktD, khD, decD, T)])
                    co_run(k, "pd", [
                        lambda ph: gen_proj(k, ph, c, hT, wbf[l], jobs_f2, T),
                        lambda ph: gen_chain(k, ph, c, zB, 0, 24, lbD[a], cbD[a], 24, qtD2, ktD2, khD2, decD2, T)])
                    phase_mematt(k, c, qmT, kmT, vmx, sg, om, T)
                    phase_mix(k, c, qtD, ktD, khD, vD, decD, oF, T)
                    phase_mix(k, c, qtD2, ktD2, khD2, vD2, decD2, oB, T)
                    phase_outn(k, c, oF, oB, sg, onorm_g[a:a + 1, :], ob, T)
                else:
                    phase_prep(k, c, xsrc, norm_g[l:l + 1, :], T, hT=hT)
                    phase_memkv(k, c, mem_in[s], mem_norm_g[l:l + 1, :], wkv[l], kmT, vmx)
                    jobs = [("FM", 0, 3072, fm16[0:24], "bf16"), ("FM", 3072, 3072, fm16[24:48], "bf16"),
                            ("TM", 6144, 3072, vD, "bf16"), ("FM", 9216, 1024, qmT, "bf16"),
                            ("TM", 10240, 4096, sg, "silu")]
                    proj_maybe_conv(l, s, hT, jobs)
                    phase_mematt(k, c, qmT, kmT, vmx, sg, om, T)
                    phase_na(k, c, fm16, vD, sg, btab[l // 2], ob, T)
                phase_outw(k, c, ob, om, wo[l], xsrc, xres[s], T)
        for s in range(NS):
            phase_prep(k, c, xres[s] if depth > 0 else x_in[s], final_g[0:1, :], T, yout=y[s])
    return nc, k


def _consts():
    cm = np.ones((128, 16, 32), np.float32)
    cm[:, :, 0] = 0
    s_ = np.arange(128)[:, None]
    t_ = np.arange(128)[None, :]
    tri = ((s_ // 32 == t_ // 32) & (s_ <= t_)).astype(np.float32)
    jr = np.eye(128, dtype=np.float32)[::-1].copy()
    kc = np.arange(GW)[:, None]
    qc = np.arange(GW)[None, :]
    cs = np.clip(qc - 8, 0, GW - 16)
    colmask = ((kc >= cs) & (kc < cs + 16)).astype(np.float32)
    return {"ident": np.eye(128, dtype=np.float32).astype(ml_dtypes.bfloat16), "ones": np.ones((128, 64), np.float32),
            "zeros": np.zeros((128, 128), np.float32), "cmask": cm.reshape(128, 512), "trimask": tri,
            "jrev": jr.astype(ml_dtypes.bfloat16), "jrevf": jr, "colmask": colmask}


def _layout_inputs(hg_lb_logits, na_rpb):
    lblT = np.ascontiguousarray(
        np.asarray(hg_lb_logits, np.float32).reshape(2, 2, NH, 128).transpose(0, 3, 1, 2).reshape(2, 128, 48))
    kc = np.arange(GW)[:, None]
    qc = np.arange(GW)[None, :]
    dc = np.clip(kc - qc + 15, 0, 30)
    rp = np.asarray(na_rpb, np.float32)[:, :, ::-1, :]
    bt = rp[:, :, :, dc]
    btab = np.ascontiguousarray(bt.transpose(0, 1, 3, 2, 4))
    return lblT, btab


def make_in_maps(NS, xs, mems, w_in_a, w_in_b, w_mem_kv, w_out, norm_g, mem_norm_g, hg_lb_logits, hg_onorm_g,
                 na_rpb, final_g):
    cst = _consts()
    lblT, btab = _layout_inputs(hg_lb_logits, na_rpb)
    maps = []
    for x, m_ in zip(xs, mems):
        m = {"x_in": x, "mem_in": m_, "w_in_a": w_in_a, "w_in_b": w_in_b, "w_mem_kv": w_mem_kv, "w_out": w_out,
             "norm_g": norm_g, "mem_norm_g": mem_norm_g, "final_g": np.asarray(final_g).reshape(1, D),
             "hg_onorm_g": hg_onorm_g, "lblT": lblT, "btab": btab}
        m.update(cst)
        maps.append(m)
    return maps


def kernel(x_prompt, x_sample, mem_prompt, mem_sample, w_in_a, w_in_b, w_mem_kv, w_out,
           norm_g, mem_norm_g, hg_lb_logits, hg_onorm_g, na_rpb, final_g):
    T = 4096
    NS = 2
    nc, k = build(T, NS, 4)
    xs = [np.ascontiguousarray(np.stack([x_sample[cid], x_prompt[cid % 2]])) for cid in range(8)]
    mems = [np.ascontiguousarray(np.stack([mem_sample[cid], mem_prompt[cid % 2]])) for cid in range(8)]
    in_maps = make_in_maps(NS, xs, mems, w_in_a, w_in_b, w_mem_kv, w_out, norm_g, mem_norm_g, hg_lb_logits,
                           hg_onorm_g, na_rpb, final_g)
    res = run_bass_kernel_spmd(nc, in_maps, core_ids=list(range(8)))
    ys = [r["y"] for r in res.results]
    y_sample = np.stack([ys[cid][0] for cid in range(8)])
    y_prompt = np.stack([ys[cid][1] for cid in range(2)])
    return (y_prompt.astype(np.float32), y_sample.astype(np.float32))
```
